# Optimizing a Trainium2 kernel written in Bass

```python
import jax, jax.numpy as jnp
from jax import lax
import numpy as np

D_MODEL = 1024
BATCH = 8
SEQ = 4096
DEPTH = 2

MEM_LEN = 256
GRID_W = 64
EPS = 1e-6

GLA_HEADS = 4
GLA_DK = 128
GLA_DV = 256
GLA_QK = GLA_HEADS * GLA_DK
GLA_V = GLA_HEADS * GLA_DV
GLA_RANK = 16
GLA_TAU = 16.0
GLA_CHUNK = 64

NA_HEADS = 8
NA_HD = 64
NA_W = NA_HEADS * NA_HD
NA_ROWS = 8
NA_COLS = 16

MEM_HEADS = 4
MEM_HD = 128
MEM_W = MEM_HEADS * MEM_HD

N_BRANCH = 3

IN_SPLIT_SIZES = (GLA_QK, GLA_QK, GLA_V, GLA_V, GLA_RANK, GLA_RANK,
                  NA_W, NA_W, NA_W, NA_W,
                  MEM_W, MEM_W,
                  N_BRANCH * D_MODEL)
IN_COLS = sum(IN_SPLIT_SIZES)

kernel_name = "hybrid_gla_natten_memory_gated_encoder"


def rmsnorm(x, g):
    xf = x.astype(jnp.float32)
    y = xf * lax.rsqrt(jnp.mean(xf * xf, axis=-1, keepdims=True) + EPS)
    return (y * g.astype(jnp.float32)).astype(x.dtype)


def gla_scan(q, k, v, g):
    B, H, T, dk = q.shape
    dv = v.shape[-1]
    C = GLA_CHUNK
    NC = T // C

    def chunks(a):
        return jnp.moveaxis(a.reshape(B, H, NC, C, a.shape[-1]), 2, 0)

    qc, kc, vc = chunks(q), chunks(k), chunks(v)
    bc = jnp.cumsum(chunks(g), axis=3)
    mask = jnp.tril(jnp.ones((C, C), dtype=bool))[:, :, None]

    def step(S, inp):
        qi, ki, vi, bi = inp
        o_inter = jnp.einsum('bhik,bhkv->bhiv', qi * jnp.exp(bi), S)
        diff = bi[:, :, :, None, :] - bi[:, :, None, :, :]
        decay = jnp.exp(jnp.where(mask, diff, -jnp.inf))
        A = jnp.einsum('bhijk,bhjk->bhij', qi[:, :, :, None, :] * decay, ki)
        o = o_inter + jnp.einsum('bhij,bhjv->bhiv', A, vi)
        b_last = bi[:, :, -1:, :]
        S = jnp.exp(b_last[:, :, 0, :])[..., None] * S + jnp.einsum(
            'bhjk,bhjv->bhkv', ki * jnp.exp(b_last - bi), vi)
        return S, o

    S0 = jnp.zeros((B, H, dk, dv), jnp.float32)
    _, o = lax.scan(step, S0, (qc, kc, vc, bc))
    return jnp.moveaxis(o, 0, 2).reshape(B, H, T, dv)


def neighbourhood_attention(q, k, v, rpb):
    B, S, H, hd = q.shape
    rows = S // GRID_W
    kr = min(NA_ROWS, rows)
    qg = q.reshape(B, rows, GRID_W, H, hd)
    kg = k.reshape(B, rows, GRID_W, H, hd)
    vg = v.reshape(B, rows, GRID_W, H, hd)
    col = np.arange(GRID_W)
    cs = np.clip(col - NA_COLS // 2, 0, GRID_W - NA_COLS)
    col_idx = cs[:, None] + np.arange(NA_COLS)[None, :]
    dc_idx = (col_idx - col[:, None]) + (NA_COLS - 1)

    def row_block(r):
        rs = jnp.clip(r - kr // 2, 0, rows - kr)
        q_row = lax.dynamic_index_in_dim(qg, r, axis=1, keepdims=False)
        k_band = lax.dynamic_slice_in_dim(kg, rs, kr, axis=1)
        v_band = lax.dynamic_slice_in_dim(vg, rs, kr, axis=1)
        k_nb = k_band[:, :, col_idx]
        v_nb = v_band[:, :, col_idx]
        dr_idx = rs + jnp.arange(kr) - r + (NA_ROWS - 1)
        bias = rpb[:, dr_idx[None, :, None], dc_idx[:, None, :]]
        s = jnp.einsum('bqhd,bkqjhd->bhqkj', q_row, k_nb).astype(jnp.float32) \
            + bias.astype(jnp.float32)
        p = jax.nn.softmax(s.reshape(B, H, GRID_W, kr * NA_COLS), axis=-1)
        p = p.reshape(s.shape).astype(v.dtype)
        return jnp.einsum('bhqkj,bkqjhd->bqhd', p, v_nb)

    out = lax.map(row_block, jnp.arange(rows))
    return jnp.moveaxis(out, 0, 1).reshape(B, S, H * hd)


def memory_attention(q, k, v):
    s = jnp.einsum('bshd,bmhd->bhsm', q, k).astype(jnp.float32)
    p = jax.nn.softmax(s, axis=-1).astype(v.dtype)
    return jnp.einsum('bhsm,bmhd->bshd', p, v)


def setup_inputs(seed: int = 0) -> dict:
    key = jax.random.key(seed)
    ks = jax.random.split(key, 24)
    L, D = DEPTH, D_MODEL
    n = lambda k, s, sc: jax.random.normal(k, s, jnp.float32) * sc
    gain = lambda k, s: 1.0 + 0.05 * jax.random.normal(k, s, jnp.float32)
    return {
        "x": jax.random.normal(ks[0], (BATCH, SEQ, D), jnp.float32),
        "mem": jax.random.normal(ks[1], (BATCH, MEM_LEN, D), jnp.float32),
        "norm_g": gain(ks[2], (L, D)),
        "w_in": n(ks[3], (L, D, IN_COLS), D ** -0.5),
        "gla_w2_f": n(ks[4], (L, GLA_RANK, GLA_QK), GLA_RANK ** -0.5),
        "gla_b_f": n(ks[5], (L, GLA_QK), 0.1),
        "gla_w2_b": n(ks[6], (L, GLA_RANK, GLA_QK), GLA_RANK ** -0.5),
        "gla_b_b": n(ks[7], (L, GLA_QK), 0.1),
        "gla_out_g": gain(ks[8], (L, GLA_DV)),
        "p_a": n(ks[9], (L, GLA_V, D), GLA_V ** -0.5),
        "na_q_g": gain(ks[10], (L, NA_HD)),
        "na_k_g": gain(ks[11], (L, NA_HD)),
        "na_rpb": n(ks[12], (L, NA_HEADS, 2 * NA_ROWS - 1, 2 * NA_COLS - 1), 0.1),
        "p_b": n(ks[13], (L, NA_W, D), NA_W ** -0.5),
        "mem_norm_g": gain(ks[14], (L, D)),
        "w_mem_kv": n(ks[15], (L, D, 2 * MEM_W), D ** -0.5),
        "mem_q_g": gain(ks[16], (L, MEM_HD)),
        "mem_k_g": gain(ks[17], (L, MEM_HD)),
        "p_c": n(ks[18], (L, MEM_W, D), MEM_W ** -0.5),
        "w_out": n(ks[19], (L, D, D), D ** -0.5),
    }


def reference(x, mem, norm_g, w_in, gla_w2_f, gla_b_f, gla_w2_b, gla_b_b, gla_out_g, p_a,
              na_q_g, na_k_g, na_rpb, p_b, mem_norm_g, w_mem_kv, mem_q_g, mem_k_g, p_c,
              w_out):
    B, S, D = x.shape
    M = mem.shape[1]
    f32 = jnp.float32
    split_idx = np.cumsum(IN_SPLIT_SIZES)[:-1]

    def heads_first(a, h, d):
        return a.reshape(B, S, h, d).transpose(0, 2, 1, 3).astype(f32)

    for l in range(DEPTH):
        h = rmsnorm(x, norm_g[l])
        proj = h @ w_in[l]
        (gq, gk, gv, ggate, glr_f, glr_b, nq, nk, nv, ngate, mq, mgate, merge) = \
            jnp.split(proj, split_idx, axis=-1)

        qa = heads_first(gq, GLA_HEADS, GLA_DK) * (GLA_DK ** -0.5)
        ka = heads_first(gk, GLA_HEADS, GLA_DK)
        va = heads_first(gv, GLA_HEADS, GLA_DV)
        g_f = jax.nn.log_sigmoid((glr_f @ gla_w2_f[l] + gla_b_f[l]).astype(f32)) / GLA_TAU
        g_b = jax.nn.log_sigmoid((glr_b @ gla_w2_b[l] + gla_b_b[l]).astype(f32)) / GLA_TAU
        g_f = heads_first(g_f, GLA_HEADS, GLA_DK)
        g_b = heads_first(g_b, GLA_HEADS, GLA_DK)
        o_fwd = gla_scan(qa, ka, va, g_f)
        o_bwd = jnp.flip(gla_scan(jnp.flip(qa, 2), jnp.flip(ka, 2), jnp.flip(va, 2),
                                  jnp.flip(g_b, 2)), 2)
        oa = (o_fwd + o_bwd).transpose(0, 2, 1, 3)
        oa = rmsnorm(oa, gla_out_g[l]).reshape(B, S, GLA_V).astype(x.dtype)
        ya = (oa * jax.nn.silu(ggate)) @ p_a[l]

        qb = rmsnorm(nq.reshape(B, S, NA_HEADS, NA_HD), na_q_g[l]) * (NA_HD ** -0.5)
        kb = rmsnorm(nk.reshape(B, S, NA_HEADS, NA_HD), na_k_g[l])
        vb = nv.reshape(B, S, NA_HEADS, NA_HD)
        ob = neighbourhood_attention(qb, kb, vb, na_rpb[l])
        yb = (ob * jax.nn.silu(ngate)) @ p_b[l]

        mem_kv = rmsnorm(mem, mem_norm_g[l]) @ w_mem_kv[l]
        mk, mv = jnp.split(mem_kv, 2, axis=-1)
        kc = rmsnorm(mk.reshape(B, M, MEM_HEADS, MEM_HD), mem_k_g[l])
        vc = mv.reshape(B, M, MEM_HEADS, MEM_HD)
        qc = rmsnorm(mq.reshape(B, S, MEM_HEADS, MEM_HD), mem_q_g[l]) * (MEM_HD ** -0.5)
        oc = memory_attention(qc, kc, vc).reshape(B, S, MEM_W)
        yc = (oc * jax.nn.silu(mgate)) @ p_c[l]

        gate_a, gate_b, gate_c = jnp.split(jax.nn.sigmoid(merge), N_BRANCH, axis=-1)
        y = gate_a * ya + gate_b * yb + gate_c * yc
        x = x + y @ w_out[l]
    return x
```

```python
import numpy as np
import ml_dtypes
from contextlib import ExitStack
import concourse.bass as bass
import concourse.mybir as mybir
from concourse.bass_utils import run_bass_kernel_spmd

F32 = mybir.dt.float32
BF16 = mybir.dt.bfloat16
AF = mybir.ActivationFunctionType
ALU = mybir.AluOpType

S = 4096
D = 1024
NT = 32
NB = 8
L = 2
MEM = 256
INC = 9248
EPS = 1e-6
NEG = -30000.0

C_GQ, C_GK, C_GV, C_GG = 0, 512, 1024, 2048
C_LRF, C_LRB = 3072, 3088
C_NQ, C_NK, C_NV, C_NG = 3104, 3616, 4128, 4640
C_MQ, C_MG, C_MRG = 5152, 5664, 6176

V_NORMG, V_MEMG, V_GOUT, V_NAQ, V_NAK, V_MQ, V_MK, V_BF, V_BB = 0, 8, 16, 18, 19, 20, 21, 22, 26
NVEC = 30


class Region:
    __slots__ = ("name", "w", "r")

    def __init__(self, name):
        self.name = name
        self.w = {}
        self.r = {}


class K:
    ENG = ("pe", "act", "dve", "pool", "sp")

    def __init__(self, nc, es):
        self.nc = nc
        self.es = es
        self.sem = {}
        self.cnt = {}
        for e in self.ENG:
            self.sem[e] = es.enter_context(nc.semaphore("s_" + e))
            self.cnt[e] = 0
        self.dma_pool = [es.enter_context(nc.semaphore("d%d" % i)) for i in range(72)]
        self.dma_val = {id(s): 0 for s in self.dma_pool}
        self.dma_free = list(self.dma_pool)
        self.dma_used = []
        self.reg_sem = {}
        self.q = {e: [] for e in self.ENG}
        self.seen = {e: {} for e in self.ENG}
        self.nreg = 0
        self.dbgset = ()

    def region(self, name="r"):
        self.nreg += 1
        return Region("%s%d" % (name, self.nreg))

    def _need(self, eng, ev, waits):
        sem, val, src = ev
        key = id(sem)
        if self.seen[eng].get(key, 0) >= val:
            return
        cur = waits.get(key)
        if cur is None or cur[1] < val:
            waits[key] = (sem, val)

    def _deps(self, eng, reads, writes):
        waits = {}
        for r in reads:
            for ev in r.w.values():
                self._need(eng, ev, waits)
        for r in writes:
            for ev in r.w.values():
                if ev[2] != eng or eng in ("sp",):
                    self._need(eng, ev, waits)
            for ev in r.r.values():
                if ev[2] != eng or eng in ("sp",):
                    self._need(eng, ev, waits)
        for key, (sem, val) in waits.items():
            self.q[eng].append(("w", sem, val))
            self.seen[eng][key] = val

    def _record(self, ev, reads, writes, partial):
        key = id(ev[0])
        for r in reads:
            r.r[key] = ev
        for r in writes:
            if partial:
                r.w[key] = ev
            else:
                r.w = {key: ev}
                r.r = {}

    def op(self, eng, fn, reads=(), writes=(), partial=False):
        self._deps(eng, reads, writes)
        self.cnt[eng] += 1
        ev = (self.sem[eng], self.cnt[eng], eng)
        self.q[eng].append(("i", fn, self.sem[eng], 1))
        self._record(ev, reads, writes, partial)

    def op_noinc(self, eng, fn, reads=(), writes=()):
        self._deps(eng, reads, writes)
        ev = (self.sem[eng], self.cnt[eng] + 1, eng)
        self.q[eng].append(("n", fn))
        self._record(ev, reads, writes, True)

    def dma(self, q, out, in_, reads, writes, semreg, partial=False):
        self._deps(q, reads, writes)
        sem = self.reg_sem.get(id(semreg))
        if sem is None:
            sem = self.dma_free.pop()
            self.reg_sem[id(semreg)] = sem
            self.dma_used.append(sem)
        self.dma_val[id(sem)] += 16
        ev = (sem, self.dma_val[id(sem)], "dma")
        self.q[q].append(("i", LZ("dma_start", out=out, in_=in_), sem, 16))
        self._record(ev, reads, writes, partial)

    def dbg(self, name, ap, reg, shape, dtype):
        if name not in self.dbgset:
            return
        d = self.nc.dram_tensor(name, list(shape), dtype, kind="ExternalOutput").ap()
        r = self.region("dbg")
        self.dma("sp", d, ap, [reg], [], r)

    def flush(self):
        nc = self.nc
        for sem in self.dma_used:
            v = self.dma_val[id(sem)]
            if self.seen["sp"].get(id(sem), 0) < v:
                self.q["sp"].append(("w", sem, v))
                self.seen["sp"][id(sem)] = v
        for e in self.ENG:
            for f in self.ENG:
                if f == e or self.cnt[f] == 0:
                    continue
                if self.seen[e].get(id(self.sem[f]), 0) < self.cnt[f]:
                    self.q[e].append(("w", self.sem[f], self.cnt[f]))
                    self.seen[e][id(self.sem[f])] = self.cnt[f]
        qs = self.q
        with nc.Block() as block:
            def mk(items):
                def body(e):
                    for it in items:
                        if it[0] == "w":
                            e.wait_ge(it[1], it[2])
                        elif it[0] == "i":
                            it[1](e).then_inc(it[2], it[3])
                        else:
                            it[1](e)
                return body
            block.tensor(mk(qs["pe"]))
            block.scalar(mk(qs["act"]))
            block.vector(mk(qs["dve"]))
            block.gpsimd(mk(qs["pool"]))
            block.sync(mk(qs["sp"]))
        self.q = {e: [] for e in self.ENG}
        for e in self.ENG:
            for f in self.ENG:
                self.seen[e][id(self.sem[f])] = self.cnt[f]
            for sem in self.dma_pool:
                self.seen[e][id(sem)] = self.dma_val[id(sem)]
        self.dma_free = list(self.dma_pool)
        self.dma_used = []
        self.reg_sem = {}


class Ring:
    def __init__(self, k, es, name, shape, dtype, n, psum=False):
        self.t = []
        self.r = []
        for i in range(n):
            k.nreg += 1
            if psum:
                t = es.enter_context(k.nc.psum_tensor("%s%d_%d" % (name, i, k.nreg), shape, dtype))
            else:
                t = es.enter_context(k.nc.sbuf_tensor("%s%d_%d" % (name, i, k.nreg), shape, dtype))
            self.t.append(t)
            self.r.append(k.region(name))
        self.i = 0
        self.n = n

    def next(self):
        j = self.i % self.n
        self.i += 1
        return self.t[j], self.r[j]


def LZ(name, *args, **kw):
    return lambda e: getattr(e, name)(*args, **kw)


def sb(k, es, name, shape, dtype):
    k.nreg += 1
    return es.enter_context(k.nc.sbuf_tensor("%s_%d" % (name, k.nreg), shape, dtype)), k.region(name)


def ps(k, es, name, shape, dtype=F32):
    k.nreg += 1
    return es.enter_context(k.nc.psum_tensor("%s_%d" % (name, k.nreg), shape, dtype)), k.region(name)


def mm_group(k, out_ap, pairs, reads, out_reg):
    n = len(pairs)
    for i, (a, b) in enumerate(pairs):
        fn = LZ("matmul", out_ap, a, b, start=(i == 0), stop=(i == n - 1))
        if i == n - 1:
            k.op("pe", fn, reads=reads, writes=[out_reg])
        else:
            k.op_noinc("pe", fn, reads=reads if i == 0 else (), writes=[out_reg] if i == 0 else ())


def build(dbg=()):
    nc = bass.Bass("TRN2", target_bir_lowering=False)
    dt = lambda name, shape, dtype, kind: nc.dram_tensor(name, list(shape), dtype, kind=kind).ap()
    x_in = dt("x", (S, D), F32, "ExternalInput")
    mem_in = dt("mem", (MEM, D), F32, "ExternalInput")
    w_in = dt("w_in", (L, D, INC), F32, "ExternalInput")
    w2f = dt("w2f", (L, 16, 512), F32, "ExternalInput")
    w2b = dt("w2b", (L, 16, 512), F32, "ExternalInput")
    p_a = dt("p_a", (L, 1024, D), F32, "ExternalInput")
    p_b = dt("p_b", (L, 512, D), F32, "ExternalInput")
    p_c = dt("p_c", (L, 512, D), F32, "ExternalInput")
    w_kv = dt("w_kv", (L, D, 1024), F32, "ExternalInput")
    w_out = dt("w_out", (L, D, D), F32, "ExternalInput")
    vecs = dt("vecs", (L, 128, NVEC), F32, "ExternalInput")
    nab = dt("nab", (L, 5, 128, 8 * 5 * 128), F32, "ExternalInput")
    cst = dt("cst", (128, 8 * 128), F32, "ExternalInput")
    scanm = dt("scanm", (128, 512), F32, "ExternalInput")
    y_out = dt("y", (S, D), F32, "ExternalOutput")

    def scratch(name, shape, dtype=BF16):
        kind = "ExternalOutput" if name in dbg else "Internal"
        return dt(name, shape, dtype, kind)

    x_mid = scratch("x_mid", (S, D), F32)
    qT_s = scratch("qT_s", (4, 128, S))
    kT_s = scratch("kT_s", (4, 128, S))
    v_s = scratch("v_s", (S, 1024))
    sg_s = scratch("sg_s", (8, 128, S))
    lrf_s = scratch("lrf_s", (16, S))
    lrb_s = scratch("lrb_s", (16, S))
    nq_s = scratch("nq_s", (4, 128, S))
    nk_s = scratch("nk_s", (4, 128, S))
    nv_s = scratch("nv_s", (S, 8 * 65))
    sng_s = scratch("sng_s", (S, 512))
    mq_s = scratch("mq_s", (4, 128, S))
    smg_s = scratch("smg_s", (4, 128, S))
    mrg_s = scratch("mrg_s", (24, 128, S))
    oa_s = scratch("oa_s", (8, 128, S))
    ob_s = scratch("ob_s", (4, 128, S))
    oc_s = scratch("oc_s", (4, 128, S))

    stop = [d for d in dbg if d.startswith("stop:")]
    stop = stop[0][5:] if stop else None
    sc = dict(qT=qT_s, kT=kT_s, v=v_s, sg=sg_s, lrf=lrf_s, lrb=lrb_s, nq=nq_s, nk=nk_s,
              nv=nv_s, sng=sng_s, mq=mq_s, smg=smg_s, mrg=mrg_s, oa=oa_s, ob=ob_s, oc=oc_s)

    with ExitStack() as es0:
        k = K(nc, es0)
        k.dbgset = dbg
        cst_f, cst_f_r = sb(k, es0, "cst_f", [128, 1024], F32)
        cst_b, cst_b_r = sb(k, es0, "cst_b", [128, 1024], BF16)
        scan_m, scan_m_r = sb(k, es0, "scan_m", [128, 512], F32)
        eps_t, eps_r = sb(k, es0, "eps_t", [128, 2], F32)
        k.dma("sp", cst_f[:], cst, [], [cst_f_r], cst_f_r)
        k.dma("sp", scan_m[:], scanm, [], [scan_m_r], scan_m_r)
        k.op("dve", LZ("tensor_copy", cst_b[:], cst_f[:]), [cst_f_r], [cst_b_r])
        k.op("pool", LZ("memset", eps_t[:, 0:1], EPS), [], [eps_r], partial=True)
        k.op("pool", LZ("memset", eps_t[:, 1:2], 1.0), [], [eps_r], partial=True)
        g = dict(ident=cst_b[:, 0:128], ones_b=cst_b[:, 128:256], blk64=cst_b[:, 256:384],
                 maskFB=cst_f[:, 384:640], ones128=cst_b[:, 640:768], ones256=cst_b[:, 768:896],
                 cst_b_r=cst_b_r, cst_f_r=cst_f_r, scan_m=scan_m, scan_m_r=scan_m_r,
                 eps=eps_t[:, 0:1], one=eps_t[:, 1:2], eps_r=eps_r)
        k.flush()

        for l in range(L):
            x_src = x_in if l == 0 else x_mid
            x_dst = x_mid if l == 0 else y_out
            with ExitStack() as esl:
                vec, vec_r = sb(k, esl, "vec", [128, NVEC], F32)
                vx, vx_r = sb(k, esl, "vx", [128, 16], F32)
                kcT, kcT_r = sb(k, esl, "kcT", [128, 4, MEM], BF16)
                vc, vc_r = sb(k, esl, "vc", [128, 2, 512], BF16)
                k.dma("sp", vec[:], vecs[l], [], [vec_r], vec_r)
                k.op("dve", LZ("tensor_scalar", vx[:, 0:1], vec[:, V_NAQ:V_NAQ + 1], 0.125, None, ALU.mult),
                     [vec_r], [vx_r], partial=True)
                k.op("dve", LZ("tensor_scalar", vx[:, 1:2], vec[:, V_MQ:V_MQ + 1], float(128 ** -0.5), None, ALU.mult),
                     [vec_r], [vx_r], partial=True)
                k.op("dve", LZ("tensor_scalar", vx[:, 2:10], vec[:, V_BF:V_BF + 8], -1.0, None, ALU.mult),
                     [vec_r], [vx_r], partial=True)
                g.update(vec=vec, vec_r=vec_r, vx=vx, vx_r=vx_r)

                with ExitStack() as es:
                    phase_P(k, es, nc, l, x_src, w_in, vec, vec_r, vx, vx_r, g["ident"], cst_b, cst_b_r, sc)
                    k.flush()
                if stop == "P":
                    break
                with ExitStack() as es:
                    phase_M(k, es, l, g, mem_in, w_kv, kcT, kcT_r, vc, vc_r)
                    k.flush()
                for h in range(4):
                    with ExitStack() as es:
                        phase_G(k, es, l, h, g, w2f, w2b, sc)
                        k.flush()
                if stop == "G":
                    break
                with ExitStack() as es:
                    phase_N(k, es, l, g, nab, sc)
                    k.flush()
                if stop == "N":
                    break
                with ExitStack() as esw:
                    w = alloc_F_weights(k, esw, l)
                    with ExitStack() as es:
                        phase_C(k, es, l, g, kcT, kcT_r, vc, vc_r, sc,
                                preload=lambda: load_F_weights(k, l, g, w, p_a, p_b, p_c, w_out))
                        k.flush()
                    if stop == "C":
                        break
                    with ExitStack() as es:
                        phase_F(k, es, l, g, w, x_src, x_dst, sc)
                        k.flush()
        k.flush()
    return nc


def norm_tile_load(k, rings, src_ap):
    xt, xt_r = rings[0].next()
    k.dma("sp", xt[:], src_ap, [], [xt_r], xt_r)
    return xt, xt_r


def norm_tile_a(k, g, rings, src_ap, loaded=None):
    xr, jr, hb, ssr, s2r, rsr, tp = rings
    xt, xt_r = loaded if loaded is not None else norm_tile_load(k, rings, src_ap)
    jk, jk_r = jr.next()
    ss, ss_r = ssr.next()
    k.op("act", LZ("activation", jk[:], xt[:], AF.Square, scale=1.0 / 32.0, accum_out=ss[:]), [xt_r], [jk_r, ss_r])
    s2, s2_r = s2r.next()
    k.op("act", LZ("activation", s2[:], ss[:], AF.Sqrt, bias=g["eps"]), [ss_r, g["eps_r"]], [s2_r])
    rs, rs_r = rsr.next()
    k.op("dve", LZ("reciprocal", rs[:], s2[:]), [s2_r], [rs_r])
    h, h_r = hb.next()
    k.op("dve", LZ("tensor_scalar", h[:], xt[:], rs[:, 0:1], None, ALU.mult), [xt_r, rs_r], [h_r])
    return h, h_r


def norm_tile_b(k, g, rings, h, h_r, dstT, dstT_r, t):
    tp = rings[6]
    p, p_r = tp.next()
    for kc in range(8):
        fn = LZ("transpose", p[:, kc, :], h[:, kc * 128:(kc + 1) * 128], g["ident"])
        if kc == 7:
            k.op("pe", fn, [h_r, g["cst_b_r"]], [p_r])
        else:
            k.op_noinc("pe", fn, [h_r, g["cst_b_r"]] if kc == 0 else (), [p_r] if kc == 0 else ())
    if t % 2 == 0:
        k.op("act", LZ("copy", dstT[:, :, t * 128:(t + 1) * 128], p[:]), [p_r], [dstT_r], partial=True)
    else:
        k.op("dve", LZ("tensor_copy", dstT[:, :, t * 128:(t + 1) * 128], p[:]), [p_r], [dstT_r], partial=True)


def norm_tile(k, g, rings, src_ap, dstT, dstT_r, t):
    h, h_r = norm_tile_a(k, g, rings, src_ap)
    norm_tile_b(k, g, rings, h, h_r, dstT, dstT_r, t)


def norm_rings(k, es, nhb=2, nxt=3, ntp=2):
    return (Ring(k, es, "xt", [128, D], F32, nxt), Ring(k, es, "junk", [128, D], BF16, 2),
            Ring(k, es, "hb", [128, D], BF16, nhb), Ring(k, es, "ss", [128, 1], F32, 4),
            Ring(k, es, "s2", [128, 1], F32, 4), Ring(k, es, "rs", [128, 1], F32, 4),
            Ring(k, es, "tp", [128, 8, 128], BF16, ntp, psum=True))


def fm_norm_a(k, g, pa, pa_r, rings, n=512):
    sqr, pss, rsq, rstd = rings
    sq, sq_r = sqr.next()
    k.op("act", LZ("activation", sq[:, 0:n], pa, AF.Square), [pa_r], [sq_r])
    return sq, sq_r


def fm_norm_b(k, g, sq, sq_r, pa, pa_r, out_ap, out_r, ones_ap, gcol, rings, n=512, partial=False):
    sqr, pss, rsq, rstd = rings
    p2, p2_r = pss.next()
    k.op("pe", LZ("matmul", p2[:, 0:n], ones_ap, sq[:, 0:n], start=True, stop=True), [sq_r, g["cst_b_r"]], [p2_r])
    rq, rq_r = rsq.next()
    k.op("act", LZ("activation", rq[:, 0:n], p2[:, 0:n], AF.Ln, bias=g["eps"]), [p2_r, g["eps_r"]], [rq_r])
    rd, rd_r = rstd.next()
    k.op("act", LZ("activation", rd[:, 0:n], rq[:, 0:n], AF.Exp, scale=-0.5), [rq_r], [rd_r])
    k.op("dve", LZ("scalar_tensor_tensor", out_ap, pa, gcol, rd[:, 0:n], ALU.mult, ALU.mult),
         [pa_r, rd_r, g["vec_r"], g["vx_r"]], [out_r], partial=partial)


def fm_norm(k, g, pa, pa_r, out_ap, out_r, ones_ap, gcol, rings, n=512, partial=False):
    sq, sq_r = fm_norm_a(k, g, pa, pa_r, rings, n)
    fm_norm_b(k, g, sq, sq_r, pa, pa_r, out_ap, out_r, ones_ap, gcol, rings, n, partial)


def phase_M(k, es, l, g, mem_in, w_kv, kcT, kcT_r, vc, vc_r):
    vec, vec_r = g["vec"], g["vec_r"]
    memT, memT_r = sb(k, es, "memT", [128, 8, MEM], BF16)
    rings = norm_rings(k, es)
    for t in range(2):
        norm_tile(k, g, rings, mem_in[t * 128:(t + 1) * 128, :], memT, memT_r, t)
    wf = Ring(k, es, "wf", [128, 8, 512], F32, 2)
    wb = Ring(k, es, "wb", [128, 8, 512], BF16, 2)
    pacc = Ring(k, es, "pacc", [128, 512], F32, 2, psum=True)
    nrings = (Ring(k, es, "sq", [128, 512], BF16, 2), Ring(k, es, "pss", [128, 512], F32, 2, psum=True),
              Ring(k, es, "rsq", [128, 512], F32, 2), Ring(k, es, "rstd", [128, 512], F32, 2))
    w_l = w_kv[l].rearrange("(kc p) c -> p kc c", p=128)
    bs = []
    for j in range(2):
        f, f_r = wf.next()
        k.dma("sp", f[:], w_l[:, :, j * 512:(j + 1) * 512], [], [f_r], f_r)
        b, b_r = wb.next()
        g_b = vec[:, V_MEMG:V_MEMG + 8].unsqueeze(2).to_broadcast([128, 8, 512])
        k.op("pool", LZ("tensor_tensor", b[:], f[:], g_b, ALU.mult), [f_r, vec_r], [b_r])
        bs.append((b, b_r))
    b, b_r = bs[0]
    for h in range(4):
        pa, pa_r = pacc.next()
        mm_group(k, pa[:, 0:MEM], [(b[:, kc, h * 128:(h + 1) * 128], memT[:, kc, :]) for kc in range(8)],
                 [b_r, memT_r], pa_r)
        fm_norm(k, g, pa[:, 0:MEM], pa_r, kcT[:, h, :], kcT_r, g["ones128"], vec[:, V_MK:V_MK + 1], nrings,
                n=MEM, partial=True)
    b, b_r = bs[1]
    for t in range(2):
        pa, pa_r = pacc.next()
        mm_group(k, pa[:], [(memT[:, kc, t * 128:(t + 1) * 128], b[:, kc, :]) for kc in range(8)],
                 [b_r, memT_r], pa_r)
        k.op("act", LZ("copy", vc[:, t, :], pa[:]), [pa_r], [vc_r], partial=True)


def phase_G(k, es, l, h, g, w2f, w2b, sc):
    vx, vx_r = g["vx"], g["vx_r"]
    ident, cbr = g["ident"], g["cst_b_r"]
    QK = float(128 ** -0.5)
    vh, vh_r = sb(k, es, "vh", [128, NT, 256], BF16)
    sgh, sgh_r = sb(k, es, "sgh", [128, 2, S], BF16)
    qe = [sb(k, es, "qe%d" % d, [128, S], BF16) for d in range(2)]
    ke = [sb(k, es, "ke%d" % d, [128, S], BF16) for d in range(2)]
    kd = [sb(k, es, "kd%d" % d, [128, NT, 128], BF16) for d in range(2)]
    eT = [sb(k, es, "eT%d" % d, [128, NT], F32) for d in range(2)]
    Sb_all, Sb_r = sb(k, es, "Sb_all", [128, NT, 256], BF16)
    psS = Ring(k, es, "psS", [128, 256], F32, 2, psum=True)
    Sst = Ring(k, es, "Sst", [128, 256], F32, 3)

    stA = {}

    def sweepA_init():
        Scur, Scur_r = Sst.next()
        k.op("pool", LZ("memset", Scur[:], 0.0), [], [Scur_r])
        k.op("pool", LZ("memset", Sb_all[:, NT - 1, :], 0.0), [], [Sb_r], partial=True)
        stA["S"] = (Scur, Scur_r)
        stA["c"] = NT - 1

    def sweepA_step():
        c = stA["c"]
        if c < 1:
            return
        kd_t, kd_r = kd[1]
        eT_t, eT_r = eT[1]
        Scur, Scur_r = stA["S"]
        pS, pS_r = psS.next()
        k.op("pe", LZ("matmul", pS[:], kd_t[:, c, :], vh[:, c, :], start=True, stop=True), [kd_r, vh_r], [pS_r])
        Sn, Sn_r = Sst.next()
        k.op("dve", LZ("scalar_tensor_tensor", Sn[:], Scur[:], eT_t[:, c - 1:c], pS[:], ALU.mult, ALU.add),
             [Scur_r, pS_r, eT_r], [Sn_r])
        k.op("act", LZ("copy", Sb_all[:, c - 1, :], Sn[:]), [Sn_r], [Sb_r], partial=True)
        stA["S"] = (Sn, Sn_r)
        stA["c"] = c - 1

    with ExitStack() as ep:
        qT, qT_r = sb(k, ep, "qT", [128, S], BF16)
        kT, kT_r = sb(k, ep, "kT", [128, S], BF16)
        lr, lr_r = sb(k, ep, "lr", [16, 2, S], BF16)
        w2s, w2s_r = sb(k, ep, "w2s", [16, 2, 128], F32)
        w2, w2_r = sb(k, ep, "w2", [16, 2, 128], BF16)
        k.dma("sp", lr[:, 0, :], sc["lrf"], [], [lr_r], lr_r, partial=True)
        k.dma("sp", lr[:, 1, :], sc["lrb"], [], [lr_r], lr_r, partial=True)
        k.dma("sp", w2s[:, 0, :], w2f[l][:, h * 128:(h + 1) * 128], [], [w2s_r], w2s_r, partial=True)
        k.dma("sp", w2s[:, 1, :], w2b[l][:, h * 128:(h + 1) * 128], [], [w2s_r], w2s_r, partial=True)
        k.dma("sp", qT[:], sc["qT"][h], [], [qT_r], qT_r)
        k.dma("sp", kT[:], sc["kT"][h], [], [kT_r], kT_r)
        k.dma("sp", vh[:], sc["v"].rearrange("(t p) c -> p t c", p=128)[:, :, h * 256:(h + 1) * 256], [], [vh_r], vh_r)
        k.dma("sp", sgh[:], sc["sg"][2 * h:2 * h + 2].rearrange("c p s -> p c s"), [], [sgh_r], sgh_r)
        k.op("dve", LZ("tensor_copy", w2[:], w2s[:]), [w2s_r], [w2_r])
        pz = Ring(k, ep, "pz", [128, 512], F32, 2, psum=True)
        ptp = Ring(k, ep, "ptp", [128, 4, 128], BF16, 2, psum=True)
        tmp = Ring(k, ep, "gtmp", [128, 512], F32, 12)
        kdT = Ring(k, ep, "kdT", [128, 4, 128], BF16, 4)

        def stage1(d, tb):
            blk = slice(tb * 512, (tb + 1) * 512)
            nb = vx[:, 2 + 4 * d + h:3 + 4 * d + h]
            zp, zp_r = pz.next()
            k.op("pe", LZ("matmul", zp[:], w2[:, d, :], lr[:, d, blk], start=True, stop=True), [w2_r, lr_r], [zp_r])
            e1, e1_r = tmp.next()
            k.op("act", LZ("activation", e1[:], zp[:], AF.Exp, bias=nb, scale=-1.0), [zp_r, vx_r], [e1_r])
            sp_, sp_r = tmp.next()
            k.op("act", LZ("activation", sp_[:], e1[:], AF.Ln, bias=g["one"]), [e1_r, g["eps_r"]], [sp_r])
            Q, Q_r = tmp.next()
            k.op("dve", LZ("tensor_tensor_scan", Q[:], g["scan_m"][:], sp_[:], 0.0, ALU.mult, ALU.add),
                 [sp_r, g["scan_m_r"]], [Q_r])
            if d == 0:
                X, X_r = Q, Q_r
            else:
                X, X_r = tmp.next()
                k.op("dve", LZ("tensor_tensor", X[:], Q[:], sp_[:], ALU.subtract), [Q_r, sp_r], [X_r])
            return (Q, Q_r, X, X_r)

        def stage2(d, tb, Q, Q_r, X, X_r):
            blk = slice(tb * 512, (tb + 1) * 512)
            qe_t, qe_r = qe[d]
            ke_t, ke_r = ke[d]
            kd_t, kd_r = kd[d]
            eT_t, eT_r = eT[d]
            sq_, sk_ = (-1.0 / 16, 1.0 / 16) if d == 0 else (1.0 / 16, -1.0 / 16)
            E1, E1_r = tmp.next()
            k.op("act", LZ("activation", E1[:], X[:], AF.Exp, scale=sq_), [X_r], [E1_r])
            E2, E2_r = tmp.next()
            k.op("act", LZ("activation", E2[:], X[:], AF.Exp, scale=sk_), [X_r], [E2_r])
            if d == 0:
                k.op("dve", LZ("tensor_copy", eT_t[:, tb * 4:(tb + 1) * 4],
                               E1[:].rearrange("p (c j) -> p c j", j=128)[:, :, 127]), [E1_r], [eT_r], partial=True)
            else:
                k.op("act", LZ("activation", eT_t[:, tb * 4:(tb + 1) * 4],
                               Q[:].rearrange("p (c j) -> p c j", j=128)[:, :, 127], AF.Exp, scale=-1.0 / 16),
                     [Q_r], [eT_r], partial=True)
            k.op("dve", LZ("scalar_tensor_tensor", qe_t[:, blk], qT[:, blk], QK, E1[:], ALU.mult, ALU.mult),
                 [qT_r, E1_r], [qe_r], partial=True)
            k.op("dve", LZ("tensor_tensor", ke_t[:, blk], kT[:, blk], E2[:], ALU.mult), [kT_r, E2_r], [ke_r], partial=True)
            kt_, kt_r = kdT.next()
            kev = ke_t[:, blk].rearrange("p (c j) -> p c j", j=128)
            if d == 0:
                k.op("dve", LZ("tensor_tensor", kt_[:], kev,
                               eT_t[:, tb * 4:(tb + 1) * 4].unsqueeze(2).to_broadcast([128, 4, 128]), ALU.mult),
                     [ke_r, eT_r], [kt_r])
            elif tb == 0:
                k.op("dve", LZ("tensor_copy", kt_[:, 0:1, :], kev[:, 0:1, :]), [ke_r], [kt_r], partial=True)
                k.op("dve", LZ("tensor_tensor", kt_[:, 1:4, :], kev[:, 1:4, :],
                               eT_t[:, 0:3].unsqueeze(2).to_broadcast([128, 3, 128]), ALU.mult),
                     [ke_r, eT_r], [kt_r], partial=True)
            else:
                k.op("dve", LZ("tensor_tensor", kt_[:], kev,
                               eT_t[:, tb * 4 - 1:tb * 4 + 3].unsqueeze(2).to_broadcast([128, 4, 128]), ALU.mult),
                     [ke_r, eT_r], [kt_r])
            return kt_, kt_r

        def stage3(d, tb, kt_, kt_r):
            kd_t, kd_r = kd[d]
            pt, pt_r = ptp.next()
            for j in range(4):
                fn = LZ("transpose", pt[:, j, :], kt_[:, j, :], ident)
                if j == 3:
                    k.op("pe", fn, [kt_r, cbr], [pt_r])
                else:
                    k.op_noinc("pe", fn, [kt_r, cbr] if j == 0 else (), [pt_r] if j == 0 else ())
            k.op("act", LZ("copy", kd_t[:, tb * 4:(tb + 1) * 4, :], pt[:]), [pt_r], [kd_r], partial=True)

        its = [(d, tb) for d in (1, 0) for tb in range(NB)]
        s1 = stage1(*its[0])
        s3 = None
        for i, (d, tb) in enumerate(its):
            s1n = stage1(*its[i + 1]) if i + 1 < len(its) else None
            if d == 0 and tb == 0:
                sweepA_init()
            kt = stage2(d, tb, *s1)
            if s3 is not None:
                stage3(*s3)
            s3 = (d, tb) + kt
            if d == 0 and tb >= 2:
                for _ in range(6):
                    sweepA_step()
            s1 = s1n
        stage3(*s3)
        while stA["c"] >= 1:
            sweepA_step()
        if h == 0:
            k.dbg("dbg_qef", qe[0][0][:], qe[0][1], [128, S], BF16)
            k.dbg("dbg_keb", ke[1][0][:], ke[1][1], [128, S], BF16)
        k.flush()

    psA = Ring(k, es, "psA", [128, 2, 128], F32, 2, psum=True)
    psO = Ring(k, es, "psO", [128, 2, 128], F32, 3, psum=True)
    psN = Ring(k, es, "psN", [128, 512], F32, 1, psum=True)
    ATm = Ring(k, es, "ATm", [128, 2, 128], BF16, 3)
    Sfb = Ring(k, es, "Sfb", [128, 256], BF16, 3)
    obuf = Ring(k, es, "obuf", [128, 2, 512], F32, 2)
    sqb = Ring(k, es, "sqb", [128, 2, 512], BF16, 2)
    rq4 = Ring(k, es, "rq4", [128, 512], F32, 2)
    rd4 = Ring(k, es, "rd4", [128, 512], F32, 2)
    t14 = Ring(k, es, "t14", [128, 2, 512], F32, 2)
    ostg = Ring(k, es, "ostg", [128, 2, 512], BF16, 2)
    Scur, Scur_r = Sst.next()
    k.op("pool", LZ("memset", Scur[:], 0.0), [], [Scur_r])
    Sb_c, Sb_cr = Sfb.next()
    k.op("pool", LZ("memset", Sb_c[:], 0.0), [], [Sb_cr])
    kd_t, kd_r = kd[0]
    eT_t, eT_r = eT[0]
    maskv = g["maskFB"].rearrange("p (a b) -> p a b", a=2)

    def emit_AT(c):
        tok = slice(c * 128, (c + 1) * 128)
        pA, pA_r = psA.next()
        k.op_noinc("pe", LZ("matmul", pA[:, 0, :], ke[0][0][:, tok], qe[0][0][:, tok], start=True, stop=True),
                   [ke[0][1], qe[0][1]], [pA_r])
        k.op("pe", LZ("matmul", pA[:, 1, :], ke[1][0][:, tok], qe[1][0][:, tok], start=True, stop=True),
             [ke[1][1], qe[1][1]], [pA_r])
        am, am_r = ATm.next()
        k.op("dve", LZ("tensor_tensor", am[:], pA[:], maskv, ALU.mult), [pA_r, g["cst_f_r"]], [am_r])
        return am, am_r

    nxt_am = emit_AT(0)
    postq = []
    for c in range(NT):
        tok = slice(c * 128, (c + 1) * 128)
        j = c % 4
        am, am_r = nxt_am
        if c + 1 < NT:
            nxt_am = emit_AT(c + 1)
        if j == 0:
            ob_, ob_r = obuf.next()
            sq4, sq4_r = sqb.next()
        pS, pS_r = psS.next()
        k.op("pe", LZ("matmul", pS[:], kd_t[:, c, :], vh[:, c, :], start=True, stop=True), [kd_r, vh_r], [pS_r])
        pO, pO_r = psO.next()
        for dvc in range(2):
            dv = slice(dvc * 128, (dvc + 1) * 128)
            pairs = [(Sb_c[:, dv], qe[0][0][:, tok]), (vh[:, c, dv], am[:, 0, :]),
                     (Sb_all[:, c, dv], qe[1][0][:, tok]), (vh[:, c, dv], am[:, 1, :])]
            for i, (a, b) in enumerate(pairs):
                fn = LZ("matmul", pO[:, dvc, :], a, b, start=(i == 0), stop=(i == 3))
                rds = [Sb_cr, qe[0][1], vh_r, am_r, Sb_r, qe[1][1]]
                if dvc == 1 and i == 3:
                    k.op("pe", fn, rds, [pO_r])
                else:
                    k.op_noinc("pe", fn, rds if (dvc == 0 and i == 0) else (), [pO_r] if (dvc == 0 and i == 0) else ())
        Sn, Sn_r = Sst.next()
        k.op("dve", LZ("scalar_tensor_tensor", Sn[:], Scur[:], eT_t[:, c:c + 1], pS[:], ALU.mult, ALU.add),
             [Scur_r, pS_r, eT_r], [Sn_r])
        Sb_c, Sb_cr = Sfb.next()
        k.op("act", LZ("copy", Sb_c[:], Sn[:]), [Sn_r], [Sb_cr])
        Scur, Scur_r = Sn, Sn_r
        k.op("act", LZ("copy", ob_[:, :, j * 128:(j + 1) * 128], pO[:]), [pO_r], [ob_r], partial=True)
        k.op("act", LZ("activation", sq4[:, :, j * 128:(j + 1) * 128], pO[:], AF.Square), [pO_r], [sq4_r], partial=True)
        if j == 1 and postq:
            postq.pop()()
        if j == 3:
            def post(tb=c // 4, ob_=ob_, ob_r=ob_r, sq4=sq4, sq4_r=sq4_r):
                pN, pN_r = psN.next()
                mm_group(k, pN[:], [(g["ones256"], sq4[:, 0, :]), (g["ones256"], sq4[:, 1, :])], [sq4_r, cbr], pN_r)
                rq, rq_r = rq4.next()
                k.op("act", LZ("activation", rq[:], pN[:], AF.Ln, bias=g["eps"]), [pN_r, g["eps_r"]], [rq_r])
                rd, rd_r = rd4.next()
                k.op("act", LZ("activation", rd[:], rq[:], AF.Exp, scale=-0.5), [rq_r], [rd_r])
                t1, t1_r = t14.next()
                k.op("dve", LZ("tensor_tensor", t1[:], ob_[:], rd[:].unsqueeze(1).to_broadcast([128, 2, 512]), ALU.mult),
                     [ob_r, rd_r], [t1_r])
                st, st_r = ostg.next()
                k.op("pool", LZ("tensor_tensor", st[:], t1[:], sgh[:, :, tb * 512:(tb + 1) * 512], ALU.mult),
                     [t1_r, sgh_r], [st_r])
                k.dma("sp", sc["oa"][2 * h:2 * h + 2, :, tb * 512:(tb + 1) * 512].rearrange("c p s -> p c s"), st[:],
                      [st_r], [], st_r)
            postq.append(post)
    while postq:
        postq.pop()()


def phase_N(k, es, l, g, nab, sc):
    nc = k.nc
    ident, cbr = g["ident"], g["cst_b_r"]
    NQ = 4
    qn = es.enter_context(nc.sbuf_tensor("qn_%d" % l, [128, 4, S], BF16))
    kn = es.enter_context(nc.sbuf_tensor("kn_%d" % l, [128, 4, S], BF16))
    Vx = es.enter_context(nc.sbuf_tensor("Vx_%d" % l, [128, NT, 520], BF16))
    bT = es.enter_context(nc.sbuf_tensor("bT_%d" % l, [128, 5, 5120], BF16))
    qn_r = [k.region("qn") for _ in range(NQ)]
    kn_r = [k.region("kn") for _ in range(NQ)]
    Vx_r = [k.region("Vx") for _ in range(NQ)]
    bT_r = [k.region("bT") for _ in range(5)]
    bst = Ring(k, es, "bst", [128, 2560], F32, 2)
    nqv = sc["nq"].rearrange("c p s -> p c s")
    nkv = sc["nk"].rearrange("c p s -> p c s")
    nvv = sc["nv"].rearrange("(t p) c -> p t c", p=128)

    def load_q(i):
        tk = slice(i * 1024, (i + 1) * 1024)
        k.dma("sp", qn[:, :, tk], nqv[:, :, tk], [], [qn_r[i]], qn_r[i])
        k.dma("sp", kn[:, :, tk], nkv[:, :, tk], [], [kn_r[i]], kn_r[i])
        k.dma("sp", Vx[:, i * 8:(i + 1) * 8, :], nvv[:, i * 8:(i + 1) * 8, :], [], [Vx_r[i]], Vx_r[i])

    def load_b(ty):
        for hf in range(2):
            b_, b_r = bst.next()
            k.dma("sp", b_[:], nab[l, ty, :, hf * 2560:(hf + 1) * 2560], [], [b_r], b_r)
            k.op("act", LZ("activation", bT[:, ty, hf * 2560:(hf + 1) * 2560], b_[:], AF.Exp), [b_r], [bT_r[ty]], partial=True)

    load_q(0)
    load_b(0)
    load_b(1)
    load_b(2)
    load_q(1)
    load_q(2)
    load_q(3)
    load_b(3)
    load_b(4)
    bTv = bT[:].rearrange("p t (h c q) -> p t h c q", h=8, c=5)
    psST = Ring(k, es, "psST", [128, 8, 128], F32, 3, psum=True)
    po = Ring(k, es, "po", [128, 4, 65], F32, 2, psum=True)
    PT = Ring(k, es, "PT", [128, 5, 128], BF16, 6)
    sng = Ring(k, es, "sng", [128, 512], BF16, 3)
    rec = Ring(k, es, "rec", [128, 8], F32, 3)
    obf = Ring(k, es, "obf", [128, 8, 64], F32, 3)
    obg = Ring(k, es, "obg", [128, 512], BF16, 2)
    stg = Ring(k, es, "nstg", [128, 4, 512], BF16, 2)
    types = {0: 0, 1: 1, 30: 3, 31: 4}
    tile = {}

    def scores(m, h):
        ty = types.get(m, 2)
        kb = min(max(m - 2, 0), 27)
        qtok = slice(m * 128, (m + 1) * 128)
        p_, hf = h // 2, h % 2
        prt = slice(64 * hf, 64 * hf + 64)
        pst, pst_r = psST.next()
        rds = [qn_r[m // 8]] + [kn_r[q] for q in sorted({kb // 8, (kb + 4) // 8})]
        for ch in range(5):
            kt = slice((kb + ch) * 128, (kb + ch + 1) * 128)
            fn = LZ("matmul", pst[:, ch, :], kn[prt, p_, kt], qn[prt, p_, qtok], start=True, stop=True)
            if ch == 4:
                k.op("pe", fn, rds, [pst_r])
            else:
                k.op_noinc("pe", fn, rds if ch == 0 else (), [pst_r] if ch == 0 else ())
        pt, pt_r = PT.next()
        k.op("act", LZ("activation", pt[:], pst[:, 0:5, :], AF.Exp), [pst_r], [pt_r])
        k.op("dve", LZ("tensor_tensor", pt[:], pt[:], bTv[:, ty, h, :, :], ALU.mult), [pt_r, bT_r[ty]], [pt_r])
        return pt, pt_r

    def pv(m, h, pt, pt_r):
        kb = min(max(m - 2, 0), 27)
        if h % 4 == 0:
            tile[m]["pos"].append(po.next())
        pO, pO_r = tile[m]["pos"][h // 4]
        rds = [pt_r] + [Vx_r[q] for q in sorted({kb // 8, (kb + 4) // 8})]
        for ch in range(5):
            fn = LZ("matmul", pO[:, h % 4, :], pt[:, ch, :], Vx[:, kb + ch, h * 65:(h + 1) * 65],
                    start=(ch == 0), stop=(ch == 4))
            if ch == 4:
                k.op("pe", fn, rds, [pO_r], partial=True)
            else:
                k.op_noinc("pe", fn, rds if ch == 0 else (), [pO_r] if ch == 0 else ())
        if h % 4 == 3:
            post_half(m, h // 4)
        if h == 7:
            post(m)

    def post_half(m, i):
        if i == 0:
            tile[m]["rc"] = rec.next()
            tile[m]["of"] = obf.next()
        rc, rc_r = tile[m]["rc"]
        of, of_r = tile[m]["of"]
        pO, pO_r = tile[m]["pos"][i]
        k.op("dve", LZ("reciprocal", rc[:, i * 4:(i + 1) * 4], pO[:, :, 64]), [pO_r], [rc_r], partial=True)
        k.op("dve", LZ("tensor_tensor", of[:, i * 4:(i + 1) * 4, :], pO[:, :, 0:64],
                       rc[:, i * 4:(i + 1) * 4].unsqueeze(2).to_broadcast([128, 4, 64]), ALU.mult),
             [pO_r, rc_r], [of_r], partial=True)

    def post(m):
        sg_, sg_r = tile[m]["sng"]
        of, of_r = tile[m]["of"]
        og, og_r = obg.next()
        k.op("dve", LZ("tensor_tensor", og[:], of[:].rearrange("p h d -> p (h d)"), sg_[:], ALU.mult),
             [of_r, sg_r], [og_r])
        pst, pt2_r = psST.next()
        pt2 = pst[:, 0:2, :].bitcast(BF16).rearrange("p a (b c) -> p (a b) c", c=128)
        for j in range(4):
            fn = LZ("transpose", pt2[:, j, :], og[:, j * 128:(j + 1) * 128], ident)
            if j == 3:
                k.op("pe", fn, [og_r, cbr], [pt2_r])
            else:
                k.op_noinc("pe", fn, [og_r, cbr] if j == 0 else (), [pt2_r] if j == 0 else ())
        if m % 4 == 0:
            tile["stg"] = stg.next()
        st, st_r = tile["stg"]
        k.op("act", LZ("copy", st[:, :, (m % 4) * 128:(m % 4 + 1) * 128], pt2), [pt2_r], [st_r], partial=True)
        if m % 4 == 3:
            tb = m // 4
            k.dma("sp", sc["ob"][:, :, tb * 512:(tb + 1) * 512].rearrange("c p s -> p c s"), st[:], [st_r], [], st_r)
        del tile[m]

    pend = []
    for m in range(NT):
        sg_, sg_r = sng.next()
        k.dma("sp", sg_[:], sc["sng"][m * 128:(m + 1) * 128, :], [], [sg_r], sg_r)
        tile[m] = dict(sng=(sg_, sg_r), pos=[])
        for h in range(8):
            pt, pt_r = scores(m, h)
            pend.append((m, h, pt, pt_r))
            if len(pend) > 2:
                pv(*pend.pop(0))
    while pend:
        pv(*pend.pop(0))


def alloc_F_weights(k, es, l):
    nc = k.nc
    w = dict(pa=sb(k, es, "pa", [128, 8, D], BF16), pb=sb(k, es, "pb", [128, 4, D], BF16),
             pc=sb(k, es, "pc", [128, 4, D], BF16), wo=sb(k, es, "wo", [128, 8, D], BF16))
    w["wst"] = Ring(k, es, "wst", [128, 2, D], F32, 2)
    return w


def load_F_weights(k, l, g, w, p_a, p_b, p_c, w_out):
    vec, vec_r = g["vec"], g["vec_r"]
    cnt = [0]

    def load(dst, dst_r, src, nkc, gout=False):
        v = src.rearrange("(kc p) c -> p kc c", p=128)
        for j in range(nkc // 2):
            s_, s_r = w["wst"].next()
            k.dma("sp", s_[:], v[:, 2 * j:2 * j + 2, :], [], [s_r], s_r)
            for i in range(2):
                kc = 2 * j + i
                cnt[0] += 1
                if gout:
                    col = V_GOUT + kc % 2
                    if cnt[0] % 2:
                        k.op("act", LZ("activation", dst[:, kc, :], s_[:, i, :], AF.Copy, scale=vec[:, col:col + 1]),
                             [s_r, vec_r], [dst_r], partial=True)
                    else:
                        k.op("dve", LZ("tensor_scalar", dst[:, kc, :], s_[:, i, :], vec[:, col:col + 1], None, ALU.mult),
                             [s_r, vec_r], [dst_r], partial=True)
                elif cnt[0] % 2:
                    k.op("act", LZ("copy", dst[:, kc, :], s_[:, i, :]), [s_r], [dst_r], partial=True)
                else:
                    k.op("dve", LZ("tensor_copy", dst[:, kc, :], s_[:, i, :]), [s_r], [dst_r], partial=True)

    load(w["pa"][0], w["pa"][1], p_a[l], 8, gout=True)
    load(w["pb"][0], w["pb"][1], p_b[l], 4)
    load(w["pc"][0], w["pc"][1], p_c[l], 4)
    load(w["wo"][0], w["wo"][1], w_out[l], 8)


def phase_C(k, es, l, g, kcT, kcT_r, vc, vc_r, sc, preload=None):
    cbr = g["cst_b_r"]
    mqr = Ring(k, es, "mqr", [128, 4, 512], BF16, 3)
    smr = Ring(k, es, "smr", [128, 4, 512], BF16, 3)
    pS = Ring(k, es, "pSc", [128, 512], F32, 4, psum=True)
    pN = Ring(k, es, "pNc", [128, 512], F32, 2, psum=True)
    pD = Ring(k, es, "pDc", [128, 512], F32, 2, psum=True)
    PT = Ring(k, es, "PTc", [128, 512], BF16, 6)
    rdr = Ring(k, es, "rdc", [128, 512], F32, 2)
    lqr = Ring(k, es, "lqc", [128, 512], F32, 2)
    t1r = Ring(k, es, "t1c", [128, 512], F32, 2)
    stg = Ring(k, es, "cstg", [128, 4, 512], BF16, 2)

    def loads(tb):
        blk = slice(tb * 512, (tb + 1) * 512)
        mq, mq_r = mqr.next()
        sm, sm_r = smr.next()
        k.dma("sp", mq[:], sc["mq"][:, :, blk].rearrange("c p s -> p c s"), [], [mq_r], mq_r)
        k.dma("sp", sm[:], sc["smg"][:, :, blk].rearrange("c p s -> p c s"), [], [sm_r], sm_r)
        return mq, mq_r, sm, sm_r

    def scores(mq, mq_r, h):
        pts = []
        for mc in range(2):
            ps_, ps_r = pS.next()
            k.op("pe", LZ("matmul", ps_[:], kcT[:, h, mc * 128:(mc + 1) * 128], mq[:, h, :], start=True, stop=True),
                 [kcT_r, mq_r], [ps_r])
            pt, pt_r = PT.next()
            k.op("act", LZ("activation", pt[:], ps_[:], AF.Exp), [ps_r], [pt_r])
            pts.append((pt, pt_r))
        return pts

    def pvn(pts, sm, sm_r, st, st_r, h):
        pn, pn_r = pN.next()
        mm_group(k, pn[:], [(vc[:, mc, h * 128:(h + 1) * 128], pts[mc][0][:]) for mc in range(2)],
                 [vc_r, pts[0][1], pts[1][1]], pn_r)
        pd, pd_r = pD.next()
        mm_group(k, pd[:], [(g["ones_b"], pts[mc][0][:]) for mc in range(2)], [cbr, pts[0][1], pts[1][1]], pd_r)
        lq, lq_r = lqr.next()
        k.op("act", LZ("activation", lq[:], pd[:], AF.Ln), [pd_r], [lq_r])
        rd, rd_r = rdr.next()
        k.op("act", LZ("activation", rd[:], lq[:], AF.Exp, scale=-1.0), [lq_r], [rd_r])
        t1, t1_r = t1r.next()
        k.op("dve", LZ("tensor_tensor", t1[:], pn[:], rd[:], ALU.mult), [pn_r, rd_r], [t1_r])
        k.op("pool", LZ("tensor_tensor", st[:, h, :], t1[:], sm[:, h, :], ALU.mult), [t1_r, sm_r], [st_r], partial=True)

    cur = loads(0)
    if preload is not None:
        preload()
    prev = None
    for tb in range(NB):
        nxt = loads(tb + 1) if tb + 1 < NB else None
        mq, mq_r, sm, sm_r = cur
        st, st_r = stg.next()
        for h in range(4):
            pts = scores(mq, mq_r, h)
            if prev is not None:
                pvn(*prev[:6])
                if prev[5] == 3:
                    ptb = prev[6]
                    k.dma("sp", sc["oc"][:, :, ptb * 512:(ptb + 1) * 512].rearrange("c p s -> p c s"), prev[3][:],
                          [prev[4]], [], prev[4])
            prev = (pts, sm, sm_r, st, st_r, h, tb)
        cur = nxt
    pvn(*prev[:6])
    k.dma("sp", sc["oc"][:, :, (NB - 1) * 512:NB * 512].rearrange("c p s -> p c s"), prev[3][:], [prev[4]], [], prev[4])


def pvn_unpack(*a):
    return a


def phase_F(k, es, l, g, w, x_src, x_dst, sc):
    ident, cbr = g["ident"], g["cst_b_r"]
    pa, pa_r = w["pa"]
    pb, pb_r = w["pb"]
    pc, pc_r = w["pc"]
    wo, wo_r = w["wo"]
    oar = Ring(k, es, "oar", [128, 8, 512], BF16, 2)
    obr = Ring(k, es, "obr", [128, 4, 512], BF16, 2)
    ocr = Ring(k, es, "ocr", [128, 4, 512], BF16, 2)
    mgr = Ring(k, es, "mgr", [128, 24, 512], BF16, 2)
    yTr = Ring(k, es, "yT", [128, 8, 512], BF16, 2)
    pY = Ring(k, es, "pY", [128, 512], F32, 4, psum=True)
    pZ = Ring(k, es, "pZ", [128, 512], F32, 2, psum=True)
    pX = Ring(k, es, "pX", [128, 512], F32, 2, psum=True)
    tr = Ring(k, es, "ft", [128, 512], BF16, 8)
    xr = Ring(k, es, "fx", [128, D], F32, 3)
    xnr = Ring(k, es, "fxn", [128, D], F32, 2)

    def loads(tb):
        blk = slice(tb * 512, (tb + 1) * 512)
        oa, oa_r = oar.next()
        ob, ob_r = obr.next()
        oc, oc_r = ocr.next()
        mg, mg_r = mgr.next()
        k.dma("sp", oa[:], sc["oa"][:, :, blk].rearrange("c p s -> p c s"), [], [oa_r], oa_r)
        k.dma("sp", ob[:], sc["ob"][:, :, blk].rearrange("c p s -> p c s"), [], [ob_r], ob_r)
        k.dma("sp", oc[:], sc["oc"][:, :, blk].rearrange("c p s -> p c s"), [], [oc_r], oc_r)
        k.dma("sp", mg[:], sc["mrg"][:, :, blk].rearrange("c p s -> p c s"), [], [mg_r], mg_r)
        return oa, oa_r, ob, ob_r, oc, oc_r, mg, mg_r

    cur = loads(0)
    pend_x = []
    for tb in range(NB):
        nxt = loads(tb + 1) if tb + 1 < NB else None
        oa, oa_r, ob, ob_r, oc, oc_r, mg, mg_r = cur
        xts = []
        for tt in range(4):
            t = tb * 4 + tt
            xts.append((None, None))
        yT, yT_r = yTr.next()
        pend_z = []
        for dc in range(8):
            dcs = slice(dc * 128, (dc + 1) * 128)
            pA, pA_r = pY.next()
            mm_group(k, pA[:], [(pa[:, kc, dcs], oa[:, kc, :]) for kc in range(8)], [pa_r, oa_r], pA_r)
            pB, pB_r = pY.next()
            mm_group(k, pB[:], [(pb[:, kc, dcs], ob[:, kc, :]) for kc in range(4)], [pb_r, ob_r], pB_r)
            pC, pC_r = pY.next()
            mm_group(k, pC[:], [(pc[:, kc, dcs], oc[:, kc, :]) for kc in range(4)], [pc_r, oc_r], pC_r)
            ts = []
            for (pp, pp_r, gi) in ((pA, pA_r, dc), (pB, pB_r, 8 + dc), (pC, pC_r, 16 + dc)):
                t_, t_r = tr.next()
                k.op("dve", LZ("tensor_tensor", t_[:], pp[:], mg[:, gi, :], ALU.mult), [pp_r, mg_r], [t_r])
                ts.append((t_, t_r))
            def zpart(ts=ts, dc=dc, yT=yT, yT_r=yT_r):
                pz, pz_r = pZ.next()
                mm_group(k, pz[:], [(ident, t_[:]) for (t_, t_r) in ts], [cbr] + [t_r for (t_, t_r) in ts], pz_r)
                k.op("act", LZ("copy", yT[:, dc, :], pz[:]), [pz_r], [yT_r], partial=True)
            if pend_z:
                pend_z.pop()()
            pend_z.append(zpart)
        pend_z.pop()()
        def xpart(tb=tb, xts=xts, yT=yT, yT_r=yT_r):
            def xload(tt):
                t = tb * 4 + tt
                xt, xt_r = xr.next()
                k.dma("sp", xt[:], x_src[t * 128:(t + 1) * 128, :], [], [xt_r], xt_r)
                return xt, xt_r
            xl = {tt: xload(tt) for tt in range(3)}
            for tt in range(4):
                t = tb * 4 + tt
                if tt == 1:
                    xl[3] = xload(3)
                xt, xt_r = xl[tt]
                xn, xn_r = xnr.next()
                for hf in range(2):
                    px, px_r = pX.next()
                    mm_group(k, px[:], [(yT[:, kc, tt * 128:(tt + 1) * 128], wo[:, kc, hf * 512:(hf + 1) * 512]) for kc in range(8)],
                             [yT_r, wo_r], px_r)
                    k.op("dve", LZ("tensor_tensor", xn[:, hf * 512:(hf + 1) * 512], px[:], xt[:, hf * 512:(hf + 1) * 512], ALU.add),
                         [px_r, xt_r], [xn_r], partial=True)
                k.dma("sp", x_dst[t * 128:(t + 1) * 128, :], xn[:], [xn_r], [], xn_r)
        if pend_x:
            pend_x.pop()()
        pend_x.append(xpart)
        cur = nxt
    pend_x.pop()()


def phase_P(k, es, nc, l, x_src, w_in, vec, vec_r, vx, vx_r, ident, cst_b, cst_b_r, sc):
    g = dict(ident=ident, cst_b_r=cst_b_r, vec_r=vec_r, vx_r=vx_r)
    hT = es.enter_context(nc.sbuf_tensor("hT_%d" % l, [128, 8, S], BF16))
    hT_rs = [k.region("hT") for _ in range(NB)]
    eps_t, eps_r = sb(k, es, "eps_p", [128, 1], F32)
    k.op("pool", LZ("memset", eps_t[:], EPS), [], [eps_r])
    g.update(eps=eps_t[:, 0:1], eps_r=eps_r)
    rings = norm_rings(k, es, nhb=10, nxt=6, ntp=1)
    hbs = {}
    xls = {}

    def p1l(t):
        if t < NT:
            xls[t] = norm_tile_load(k, rings, x_src[t * 128:(t + 1) * 128, :])

    def p1a(t):
        if t < NT:
            hbs[t] = norm_tile_a(k, g, rings, None, loaded=xls.pop(t))

    def p1b(t):
        if t < NT:
            h, h_r = hbs.pop(t)
            norm_tile_b(k, g, rings, h, h_r, hT, hT_rs[t // 4], t)

    wf = Ring(k, es, "wf", [128, 8, 512], F32, 2)
    wb = Ring(k, es, "wb", [128, 8, 512], BF16, 2)
    pacc = Ring(k, es, "pacc", [128, 512], F32, 5, psum=True)
    pss = Ring(k, es, "pss", [128, 512], F32, 2, psum=True)
    stg = Ring(k, es, "stg", [128, 512], BF16, 6)
    nrings = (Ring(k, es, "sq", [128, 512], BF16, 3), pss,
              Ring(k, es, "rsq", [128, 512], F32, 2), Ring(k, es, "rstd", [128, 512], F32, 2))
    nvst = Ring(k, es, "nvst", [128, 8, 65], BF16, 3)
    for i in range(3):
        k.op("pool", LZ("memset", nvst.t[i][:], 1.0), [], [nvst.r[i]])
    w_l = w_in[l].rearrange("(kc p) c -> p kc c", p=128)
    ev = [0]

    def load_w(c0, ncol):
        f, f_r = wf.next()
        k.dma("sp", f[:, :, 0:ncol], w_l[:, :, c0:c0 + ncol], [], [f_r], f_r)
        b, b_r = wb.next()
        g_b = vec[:, V_NORMG:V_NORMG + 8].unsqueeze(2).to_broadcast([128, 8, ncol])
        k.op("pool", LZ("tensor_tensor", b[:, :, 0:ncol], f[:, :, 0:ncol], g_b, ALU.mult), [f_r, vec_r], [b_r])
        return b, b_r

    def evac(pa, pa_r, st, st_r, func):
        ev[0] += 1
        if func is None and ev[0] % 2 == 0:
            k.op("dve", LZ("tensor_copy", st, pa), [pa_r], [st_r])
        else:
            k.op("act", LZ("activation", st, pa, AF.Copy if func is None else func), [pa_r], [st_r])

    def fm_block(b, b_r, dst, func=None, norm=None, first=False):
        pend = []
        for tb in range(NB):
            if first and tb == 0:
                for t in range(0, 5):
                    p1l(t)
                for t in range(0, 8):
                    p1l(t + 5)
                    p1a(t)
                for t in range(0, 4):
                    p1b(t)
            for sub in range(4):
                if first:
                    p1l(tb * 4 + 13 + sub)
                    p1b(tb * 4 + 4 + sub)
                    p1a(tb * 4 + 8 + sub)
                pa, pa_r = pacc.next()
                mm_group(k, pa[:], [(b[:, kc, sub * 128:(sub + 1) * 128], hT[:, kc, tb * 512:(tb + 1) * 512])
                                    for kc in range(8)], [b_r, hT_rs[tb]], pa_r)
                if norm is None:
                    st, st_r = stg.next()
                    evac(pa[:], pa_r, st[:], st_r, func)
                    k.dma("sp", dst[sub, :, tb * 512:(tb + 1) * 512], st[:], [st_r], [], st_r)
                else:
                    sq, sq_r = fm_norm_a(k, g, pa[:], pa_r, nrings)
                    if pend:
                        pend.pop()()

                    def tail(sq=sq, sq_r=sq_r, pa=pa, pa_r=pa_r, sub=sub, tb=tb):
                        st, st_r = stg.next()
                        fm_norm_b(k, g, sq, sq_r, pa[:], pa_r, st[:], st_r, norm[0], norm[1], nrings)
                        k.dma("sp", dst[sub, :, tb * 512:(tb + 1) * 512], st[:], [st_r], [], st_r)
                    pend.append(tail)
        if pend:
            pend.pop()()

    def lr_block(b, b_r):
        for tb in range(NB):
            for j, dst in enumerate((sc["lrf"], sc["lrb"])):
                pa, pa_r = pacc.next()
                mm_group(k, pa[0:16, :], [(b[:, kc, 16 * j:16 * j + 16], hT[:, kc, tb * 512:(tb + 1) * 512])
                                          for kc in range(8)], [b_r, hT_rs[tb]], pa_r)
                st, st_r = stg.next()
                evac(pa[0:16, :], pa_r, st[0:16, :], st_r, None)
                k.dma("sp", dst[:, tb * 512:(tb + 1) * 512], st[0:16, :], [st_r], [], st_r)

    def tm_block(b, b_r, dst_fn, func=None, nv=False):
        for t in range(NT):
            pa, pa_r = pacc.next()
            mm_group(k, pa[:], [(hT[:, kc, t * 128:(t + 1) * 128], b[:, kc, :]) for kc in range(8)],
                     [b_r, hT_rs[t // 4]], pa_r)
            if nv:
                st, st_r = nvst.next()
                pv_ = pa[:].rearrange("p (h d) -> p h d", h=8)
                if t % 2:
                    k.op("act", LZ("activation", st[:, :, 0:64], pv_, AF.Copy), [pa_r], [st_r], partial=True)
                else:
                    k.op("dve", LZ("tensor_copy", st[:, :, 0:64], pv_), [pa_r], [st_r], partial=True)
                k.dma("sp", dst_fn(t), st[:].rearrange("p h d -> p (h d)"), [st_r], [], st_r)
            else:
                st, st_r = stg.next()
                evac(pa[:], pa_r, st[:], st_r, func)
                k.dma("sp", dst_fn(t), st[:], [st_r], [], st_r)

    blk64 = cst_b[:, 256:384]
    ones128 = cst_b[:, 640:768]
    vrow = lambda name, c0: (lambda t: sc[name][t * 128:(t + 1) * 128, c0:c0 + 512])
    blocks = [
        (C_GQ, 512, lambda b, r: fm_block(b, r, sc["qT"], first=True)),
        (C_GK, 512, lambda b, r: fm_block(b, r, sc["kT"])),
        (C_GV, 512, lambda b, r: tm_block(b, r, vrow("v", 0))),
        (C_GV + 512, 512, lambda b, r: tm_block(b, r, vrow("v", 512))),
        (C_GG, 512, lambda b, r: fm_block(b, r, sc["sg"][0:4], AF.Silu)),
        (C_GG + 512, 512, lambda b, r: fm_block(b, r, sc["sg"][4:8], AF.Silu)),
        (C_NG, 512, lambda b, r: tm_block(b, r, vrow("sng", 0), AF.Silu)),
        (C_MG, 512, lambda b, r: fm_block(b, r, sc["smg"], AF.Silu)),
        (C_LRF, 32, lambda b, r: lr_block(b, r)),
        (C_NQ, 512, lambda b, r: fm_block(b, r, sc["nq"], norm=(blk64, vx[:, 0:1]))),
        (C_NK, 512, lambda b, r: fm_block(b, r, sc["nk"], norm=(blk64, vec[:, V_NAK:V_NAK + 1]))),
        (C_MQ, 512, lambda b, r: fm_block(b, r, sc["mq"], norm=(ones128, vx[:, 1:2]))),
        (C_NV, 512, lambda b, r: tm_block(b, r, lambda t: sc["nv"][t * 128:(t + 1) * 128, :], nv=True)),
    ]
    for j in range(6):
        blocks.append((C_MRG + 512 * j, 512,
                       (lambda j: (lambda b, r: fm_block(b, r, sc["mrg"][4 * j:4 * j + 4], AF.Sigmoid)))(j)))
    cur = load_w(blocks[0][0], blocks[0][1])
    for i, (c0, ncol, fn) in enumerate(blocks):
        nxt = load_w(blocks[i + 1][0], blocks[i + 1][1]) if i + 1 < len(blocks) else None
        fn(cur[0], cur[1])
        cur = nxt


def make_consts():
    c = np.zeros((128, 1024), np.float32)
    c[:, 0:128] = np.eye(128, dtype=np.float32)
    c[:, 128:256] = 1.0
    blk = np.zeros((128, 128), np.float32)
    blk[:64, :64] = 1.0 / 64
    blk[64:, 64:] = 1.0 / 64
    c[:, 256:384] = blk
    j = np.arange(128)[:, None]
    i = np.arange(128)[None, :]
    c[:, 384:512] = (i >= j).astype(np.float32)
    c[:, 512:640] = (i <= j).astype(np.float32)
    c[:, 640:768] = 1.0 / 128
    c[:, 768:896] = 1.0 / 256
    m = np.ones((128, 512), np.float32)
    m[:, 0::128] = 0.0
    return c, m


def na_gather_index():
    idx = np.full((5, 128, 5, 128), 15 * 31, np.int64)
    tiles = [0, 1, 2, 30, 31]
    for ti, m in enumerate(tiles):
        kb = min(max(m - 2, 0), 27)
        for q in range(128):
            r = 2 * m + q // 64
            c = q % 64
            rs = min(max(r - 4, 0), 56)
            cs = min(max(c - 8, 0), 48)
            for ch in range(5):
                for kk in range(128):
                    tok = (kb + ch) * 128 + kk
                    kr, kc = tok // 64, tok % 64
                    if rs <= kr < rs + 8 and cs <= kc < cs + 16:
                        idx[ti, kk, ch, q] = (kr - r + 7) * 31 + (kc - c + 15)
    return idx


_NA_IDX = None


def host_layout(inp):
    global _NA_IDX
    f = lambda a: np.ascontiguousarray(np.asarray(a, dtype=np.float32))
    vecs = np.zeros((L, 128, NVEC), np.float32)
    for l in range(L):
        vecs[l, :, V_NORMG:V_NORMG + 8] = f(inp["norm_g"])[l].reshape(8, 128).T
        vecs[l, :, V_MEMG:V_MEMG + 8] = f(inp["mem_norm_g"])[l].reshape(8, 128).T
        vecs[l, :, V_GOUT:V_GOUT + 2] = f(inp["gla_out_g"])[l].reshape(2, 128).T
        vecs[l, :, V_NAQ] = np.tile(f(inp["na_q_g"])[l], 2)
        vecs[l, :, V_NAK] = np.tile(f(inp["na_k_g"])[l], 2)
        vecs[l, :, V_MQ] = f(inp["mem_q_g"])[l]
        vecs[l, :, V_MK] = f(inp["mem_k_g"])[l]
        vecs[l, :, V_BF:V_BF + 4] = f(inp["gla_b_f"])[l].reshape(4, 128).T
        vecs[l, :, V_BB:V_BB + 4] = f(inp["gla_b_b"])[l].reshape(4, 128).T
    if _NA_IDX is None:
        _NA_IDX = na_gather_index()
    rpb = f(inp["na_rpb"]).reshape(L, 8, 15 * 31)
    rpb_pad = np.concatenate([rpb, np.full((L, 8, 1), NEG, np.float32)], axis=2)
    g = rpb_pad[:, :, _NA_IDX]
    nab = np.ascontiguousarray(g.transpose(0, 2, 3, 1, 4, 5)).reshape(L, 5, 128, 8 * 5 * 128)
    c, m = make_consts()
    shared = dict(w_in=f(inp["w_in"]), w2f=f(inp["gla_w2_f"]), w2b=f(inp["gla_w2_b"]), p_a=f(inp["p_a"]),
                  p_b=f(inp["p_b"]), p_c=f(inp["p_c"]), w_kv=f(inp["w_mem_kv"]), w_out=f(inp["w_out"]),
                  vecs=vecs, nab=nab, cst=c, scanm=m)
    x = f(inp["x"])
    mem = f(inp["mem"])
    return [dict(shared, x=x[b], mem=mem[b]) for b in range(8)]


def kernel(**inputs):
    in_maps = host_layout(inputs)
    nc = build()
    res = run_bass_kernel_spmd(nc, in_maps, core_ids=list(range(8)))
    return np.stack([np.asarray(r["y"], dtype=np.float32) for r in res.results], axis=0)
```

```python
import numpy as np
import ml_dtypes
from contextlib import ExitStack
import concourse.bass as bass
import concourse.mybir as mybir
from concourse.bass_utils import run_bass_kernel_spmd

F32 = mybir.dt.float32
BF16 = mybir.dt.bfloat16
AF = mybir.ActivationFunctionType
ALU = mybir.AluOpType

S = 4096
D = 1024
NT = 32
NB = 8
L = 2
MEM = 256
INC = 9248
EPS = 1e-6
NEG = -30000.0

C_GQ, C_GK, C_GV, C_GG = 0, 512, 1024, 2048
C_LRF, C_LRB = 3072, 3088
C_NQ, C_NK, C_NV, C_NG = 3104, 3616, 4128, 4640
C_MQ, C_MG, C_MRG = 5152, 5664, 6176

V_NORMG, V_MEMG, V_GOUT, V_NAQ, V_NAK, V_MQ, V_MK, V_BF, V_BB = 0, 8, 16, 18, 19, 20, 21, 22, 26
NVEC = 30


class Region:
    __slots__ = ("name", "w", "r")

    def __init__(self, name):
        self.name = name
        self.w = {}
        self.r = {}


class K:
    ENG = ("pe", "act", "dve", "pool", "sp")

    def __init__(self, nc, es):
        self.nc = nc
        self.es = es
        self.sem = {}
        self.cnt = {}
        for e in self.ENG:
            self.sem[e] = es.enter_context(nc.semaphore("s_" + e))
            self.cnt[e] = 0
        self.dma_pool = [es.enter_context(nc.semaphore("d%d" % i)) for i in range(72)]
        self.dma_val = {id(s): 0 for s in self.dma_pool}
        self.dma_free = list(self.dma_pool)
        self.dma_used = []
        self.reg_sem = {}
        self.q = {e: [] for e in self.ENG}
        self.seen = {e: {} for e in self.ENG}
        self.nreg = 0
        self.dbgset = ()

    def region(self, name="r"):
        self.nreg += 1
        return Region("%s%d" % (name, self.nreg))

    def _need(self, eng, ev, waits):
        sem, val, src = ev
        key = id(sem)
        if self.seen[eng].get(key, 0) >= val:
            return
        cur = waits.get(key)
        if cur is None or cur[1] < val:
            waits[key] = (sem, val)

    def _deps(self, eng, reads, writes):
        waits = {}
        for r in reads:
            for ev in r.w.values():
                self._need(eng, ev, waits)
        for r in writes:
            for ev in r.w.values():
                if ev[2] != eng or eng in ("sp",):
                    self._need(eng, ev, waits)
            for ev in r.r.values():
                if ev[2] != eng or eng in ("sp",):
                    self._need(eng, ev, waits)
        for key, (sem, val) in waits.items():
            self.q[eng].append(("w", sem, val))
            self.seen[eng][key] = val

    def _record(self, ev, reads, writes, partial):
        key = id(ev[0])
        for r in reads:
            r.r[key] = ev
        for r in writes:
            if partial:
                r.w[key] = ev
            else:
                r.w = {key: ev}
                r.r = {}

    def op(self, eng, fn, reads=(), writes=(), partial=False):
        self._deps(eng, reads, writes)
        self.cnt[eng] += 1
        ev = (self.sem[eng], self.cnt[eng], eng)
        self.q[eng].append(("i", fn, self.sem[eng], 1))
        self._record(ev, reads, writes, partial)

    def op_noinc(self, eng, fn, reads=(), writes=()):
        self._deps(eng, reads, writes)
        ev = (self.sem[eng], self.cnt[eng] + 1, eng)
        self.q[eng].append(("n", fn))
        self._record(ev, reads, writes, True)

    def dma(self, q, out, in_, reads, writes, semreg, partial=False):
        self._deps(q, reads, writes)
        sem = self.reg_sem.get(id(semreg))
        if sem is None:
            sem = self.dma_free.pop()
            self.reg_sem[id(semreg)] = sem
            self.dma_used.append(sem)
        self.dma_val[id(sem)] += 16
        ev = (sem, self.dma_val[id(sem)], "dma")
        self.q[q].append(("i", LZ("dma_start", out=out, in_=in_), sem, 16))
        self._record(ev, reads, writes, partial)

    def dbg(self, name, ap, reg, shape, dtype):
        if name not in self.dbgset:
            return
        d = self.nc.dram_tensor(name, list(shape), dtype, kind="ExternalOutput").ap()
        r = self.region("dbg")
        self.dma("sp", d, ap, [reg], [], r)

    def flush(self):
        nc = self.nc
        for sem in self.dma_used:
            v = self.dma_val[id(sem)]
            if self.seen["sp"].get(id(sem), 0) < v:
                self.q["sp"].append(("w", sem, v))
                self.seen["sp"][id(sem)] = v
        for e in self.ENG:
            for f in self.ENG:
                if f == e or self.cnt[f] == 0:
                    continue
                if self.seen[e].get(id(self.sem[f]), 0) < self.cnt[f]:
                    self.q[e].append(("w", self.sem[f], self.cnt[f]))
                    self.seen[e][id(self.sem[f])] = self.cnt[f]
        qs = self.q
        with nc.Block() as block:
            def mk(items):
                def body(e):
                    for it in items:
                        if it[0] == "w":
                            e.wait_ge(it[1], it[2])
                        elif it[0] == "i":
                            it[1](e).then_inc(it[2], it[3])
                        else:
                            it[1](e)
                return body
            block.tensor(mk(qs["pe"]))
            block.scalar(mk(qs["act"]))
            block.vector(mk(qs["dve"]))
            block.gpsimd(mk(qs["pool"]))
            block.sync(mk(qs["sp"]))
        self.q = {e: [] for e in self.ENG}
        for e in self.ENG:
            for f in self.ENG:
                self.seen[e][id(self.sem[f])] = self.cnt[f]
            for sem in self.dma_pool:
                self.seen[e][id(sem)] = self.dma_val[id(sem)]
        self.dma_free = list(self.dma_pool)
        self.dma_used = []
        self.reg_sem = {}


class Ring:
    def __init__(self, k, es, name, shape, dtype, n, psum=False):
        self.t = []
        self.r = []
        for i in range(n):
            k.nreg += 1
            if psum:
                t = es.enter_context(k.nc.psum_tensor("%s%d_%d" % (name, i, k.nreg), shape, dtype))
            else:
                t = es.enter_context(k.nc.sbuf_tensor("%s%d_%d" % (name, i, k.nreg), shape, dtype))
            self.t.append(t)
            self.r.append(k.region(name))
        self.i = 0
        self.n = n

    def next(self):
        j = self.i % self.n
        self.i += 1
        return self.t[j], self.r[j]


def LZ(name, *args, **kw):
    return lambda e: getattr(e, name)(*args, **kw)


def sb(k, es, name, shape, dtype):
    k.nreg += 1
    return es.enter_context(k.nc.sbuf_tensor("%s_%d" % (name, k.nreg), shape, dtype)), k.region(name)


def ps(k, es, name, shape, dtype=F32):
    k.nreg += 1
    return es.enter_context(k.nc.psum_tensor("%s_%d" % (name, k.nreg), shape, dtype)), k.region(name)


def mm_group(k, out_ap, pairs, reads, out_reg):
    n = len(pairs)
    for i, (a, b) in enumerate(pairs):
        fn = LZ("matmul", out_ap, a, b, start=(i == 0), stop=(i == n - 1))
        if i == n - 1:
            k.op("pe", fn, reads=reads, writes=[out_reg])
        else:
            k.op_noinc("pe", fn, reads=reads if i == 0 else (), writes=[out_reg] if i == 0 else ())


def build(dbg=()):
    nc = bass.Bass("TRN2", target_bir_lowering=False)
    dt = lambda name, shape, dtype, kind: nc.dram_tensor(name, list(shape), dtype, kind=kind).ap()
    x_in = dt("x", (S, D), F32, "ExternalInput")
    mem_in = dt("mem", (MEM, D), F32, "ExternalInput")
    w_in = dt("w_in", (L, D, INC), F32, "ExternalInput")
    w2f = dt("w2f", (L, 16, 512), F32, "ExternalInput")
    w2b = dt("w2b", (L, 16, 512), F32, "ExternalInput")
    p_a = dt("p_a", (L, 1024, D), F32, "ExternalInput")
    p_b = dt("p_b", (L, 512, D), F32, "ExternalInput")
    p_c = dt("p_c", (L, 512, D), F32, "ExternalInput")
    w_kv = dt("w_kv", (L, D, 1024), F32, "ExternalInput")
    w_out = dt("w_out", (L, D, D), F32, "ExternalInput")
    vecs = dt("vecs", (L, 128, NVEC), F32, "ExternalInput")
    nab = dt("nab", (L, 5, 128, 8 * 5 * 128), F32, "ExternalInput")
    cst = dt("cst", (128, 8 * 128), F32, "ExternalInput")
    scanm = dt("scanm", (128, 512), F32, "ExternalInput")
    y_out = dt("y", (S, D), F32, "ExternalOutput")

    def scratch(name, shape, dtype=BF16):
        kind = "ExternalOutput" if name in dbg else "Internal"
        return dt(name, shape, dtype, kind)

    x_mid = scratch("x_mid", (S, D), F32)
    qT_s = scratch("qT_s", (4, 128, S))
    kT_s = scratch("kT_s", (4, 128, S))
    v_s = scratch("v_s", (S, 1024))
    sg_s = scratch("sg_s", (8, 128, S))
    lrf_s = scratch("lrf_s", (16, S))
    lrb_s = scratch("lrb_s", (16, S))
    nq_s = scratch("nq_s", (4, 128, S))
    nk_s = scratch("nk_s", (4, 128, S))
    nv_s = scratch("nv_s", (S, 8 * 65))
    sng_s = scratch("sng_s", (S, 512))
    mq_s = scratch("mq_s", (4, 128, S))
    smg_s = scratch("smg_s", (4, 128, S))
    mrg_s = scratch("mrg_s", (24, 128, S))
    oa_s = scratch("oa_s", (8, 128, S))
    ob_s = scratch("ob_s", (4, 128, S))
    oc_s = scratch("oc_s", (4, 128, S))

    stop = [d for d in dbg if d.startswith("stop:")]
    stop = stop[0][5:] if stop else None
    sc = dict(qT=qT_s, kT=kT_s, v=v_s, sg=sg_s, lrf=lrf_s, lrb=lrb_s, nq=nq_s, nk=nk_s,
              nv=nv_s, sng=sng_s, mq=mq_s, smg=smg_s, mrg=mrg_s, oa=oa_s, ob=ob_s, oc=oc_s)

    with ExitStack() as es0:
        k = K(nc, es0)
        k.dbgset = dbg
        cst_f, cst_f_r = sb(k, es0, "cst_f", [128, 1024], F32)
        cst_b, cst_b_r = sb(k, es0, "cst_b", [128, 1024], BF16)
        scan_m, scan_m_r = sb(k, es0, "scan_m", [128, 512], F32)
        eps_t, eps_r = sb(k, es0, "eps_t", [128, 2], F32)
        k.dma("sp", cst_f[:], cst, [], [cst_f_r], cst_f_r)
        k.dma("sp", scan_m[:], scanm, [], [scan_m_r], scan_m_r)
        k.op("dve", LZ("tensor_copy", cst_b[:], cst_f[:]), [cst_f_r], [cst_b_r])
        k.op("pool", LZ("memset", eps_t[:, 0:1], EPS), [], [eps_r], partial=True)
        k.op("pool", LZ("memset", eps_t[:, 1:2], 1.0), [], [eps_r], partial=True)
        g = dict(ident=cst_b[:, 0:128], ones_b=cst_b[:, 128:256], blk64=cst_b[:, 256:384],
                 maskFB=cst_f[:, 384:640], ones128=cst_b[:, 640:768], ones256=cst_b[:, 768:896],
                 cst_b_r=cst_b_r, cst_f_r=cst_f_r, scan_m=scan_m, scan_m_r=scan_m_r,
                 eps=eps_t[:, 0:1], one=eps_t[:, 1:2], eps_r=eps_r)
        k.flush()

        for l in range(L):
            x_src = x_in if l == 0 else x_mid
            x_dst = x_mid if l == 0 else y_out
            with ExitStack() as esl:
                vec, vec_r = sb(k, esl, "vec", [128, NVEC], F32)
                vx, vx_r = sb(k, esl, "vx", [128, 16], F32)
                kcT, kcT_r = sb(k, esl, "kcT", [128, 4, MEM], BF16)
                vc, vc_r = sb(k, esl, "vc", [128, 2, 512], BF16)
                k.dma("sp", vec[:], vecs[l], [], [vec_r], vec_r)
                k.op("dve", LZ("tensor_scalar", vx[:, 0:1], vec[:, V_NAQ:V_NAQ + 1], 0.125, None, ALU.mult),
                     [vec_r], [vx_r], partial=True)
                k.op("dve", LZ("tensor_scalar", vx[:, 1:2], vec[:, V_MQ:V_MQ + 1], float(128 ** -0.5), None, ALU.mult),
                     [vec_r], [vx_r], partial=True)
                k.op("dve", LZ("tensor_scalar", vx[:, 2:10], vec[:, V_BF:V_BF + 8], -1.0, None, ALU.mult),
                     [vec_r], [vx_r], partial=True)
                g.update(vec=vec, vec_r=vec_r, vx=vx, vx_r=vx_r)
                k.flush()

                with ExitStack() as es:
                    phase_P(k, es, nc, l, x_src, w_in, vec, vec_r, vx, vx_r, g["ident"], cst_b, cst_b_r, sc)
                    k.flush()
                if stop == "P":
                    break
                with ExitStack() as es:
                    phase_M(k, es, l, g, mem_in, w_kv, kcT, kcT_r, vc, vc_r)
                    k.flush()
                for h in range(4):
                    with ExitStack() as es:
                        phase_G(k, es, l, h, g, w2f, w2b, sc)
                        k.flush()
                if stop == "G":
                    break
                with ExitStack() as es:
                    phase_N(k, es, l, g, nab, sc)
                    k.flush()
                if stop == "N":
                    break
                with ExitStack() as esw:
                    w = alloc_F_weights(k, esw, l)
                    with ExitStack() as es:
                        phase_C(k, es, l, g, kcT, kcT_r, vc, vc_r, sc,
                                preload=lambda: load_F_weights(k, l, g, w, p_a, p_b, p_c, w_out))
                        k.flush()
                    if stop == "C":
                        break
                    with ExitStack() as es:
                        phase_F(k, es, l, g, w, x_src, x_dst, sc)
                        k.flush()
        k.flush()
    return nc


def norm_tile_load(k, rings, src_ap):
    xt, xt_r = rings[0].next()
    k.dma("sp", xt[:], src_ap, [], [xt_r], xt_r)
    return xt, xt_r


def norm_tile_a(k, g, rings, src_ap, loaded=None):
    xr, jr, hb, ssr, s2r, rsr, tp = rings
    xt, xt_r = loaded if loaded is not None else norm_tile_load(k, rings, src_ap)
    jk, jk_r = jr.next()
    ss, ss_r = ssr.next()
    k.op("act", LZ("activation", jk[:], xt[:], AF.Square, scale=1.0 / 32.0, accum_out=ss[:]), [xt_r], [jk_r, ss_r])
    s2, s2_r = s2r.next()
    k.op("act", LZ("activation", s2[:], ss[:], AF.Sqrt, bias=g["eps"]), [ss_r, g["eps_r"]], [s2_r])
    rs, rs_r = rsr.next()
    k.op("dve", LZ("reciprocal", rs[:], s2[:]), [s2_r], [rs_r])
    h, h_r = hb.next()
    k.op("dve", LZ("tensor_scalar", h[:], xt[:], rs[:, 0:1], None, ALU.mult), [xt_r, rs_r], [h_r])
    return h, h_r


def norm_tile_b(k, g, rings, h, h_r, dstT, dstT_r, t):
    tp = rings[6]
    p, p_r = tp.next()
    for kc in range(8):
        fn = LZ("transpose", p[:, kc, :], h[:, kc * 128:(kc + 1) * 128], g["ident"])
        if kc == 7:
            k.op("pe", fn, [h_r, g["cst_b_r"]], [p_r])
        else:
            k.op_noinc("pe", fn, [h_r, g["cst_b_r"]] if kc == 0 else (), [p_r] if kc == 0 else ())
    if t % 2 == 0:
        k.op("act", LZ("copy", dstT[:, :, t * 128:(t + 1) * 128], p[:]), [p_r], [dstT_r], partial=True)
    else:
        k.op("dve", LZ("tensor_copy", dstT[:, :, t * 128:(t + 1) * 128], p[:]), [p_r], [dstT_r], partial=True)


def norm_tile(k, g, rings, src_ap, dstT, dstT_r, t):
    h, h_r = norm_tile_a(k, g, rings, src_ap)
    norm_tile_b(k, g, rings, h, h_r, dstT, dstT_r, t)


def norm_rings(k, es, nhb=2, nxt=3, ntp=2):
    return (Ring(k, es, "xt", [128, D], F32, nxt), Ring(k, es, "junk", [128, D], BF16, 2),
            Ring(k, es, "hb", [128, D], BF16, nhb), Ring(k, es, "ss", [128, 1], F32, 4),
            Ring(k, es, "s2", [128, 1], F32, 4), Ring(k, es, "rs", [128, 1], F32, 4),
            Ring(k, es, "tp", [128, 8, 128], BF16, ntp, psum=True))


def fm_norm_a(k, g, pa, pa_r, rings, n=512):
    sqr, pss, rsq, rstd = rings
    sq, sq_r = sqr.next()
    k.op("act", LZ("activation", sq[:, 0:n], pa, AF.Square), [pa_r], [sq_r])
    return sq, sq_r


def fm_norm_b(k, g, sq, sq_r, pa, pa_r, out_ap, out_r, ones_ap, gcol, rings, n=512, partial=False):
    sqr, pss, rsq, rstd = rings
    p2, p2_r = pss.next()
    k.op("pe", LZ("matmul", p2[:, 0:n], ones_ap, sq[:, 0:n], start=True, stop=True), [sq_r, g["cst_b_r"]], [p2_r])
    rq, rq_r = rsq.next()
    k.op("act", LZ("activation", rq[:, 0:n], p2[:, 0:n], AF.Ln, bias=g["eps"]), [p2_r, g["eps_r"]], [rq_r])
    rd, rd_r = rstd.next()
    k.op("act", LZ("activation", rd[:, 0:n], rq[:, 0:n], AF.Exp, scale=-0.5), [rq_r], [rd_r])
    k.op("dve", LZ("scalar_tensor_tensor", out_ap, pa, gcol, rd[:, 0:n], ALU.mult, ALU.mult),
         [pa_r, rd_r, g["vec_r"], g["vx_r"]], [out_r], partial=partial)


def fm_norm(k, g, pa, pa_r, out_ap, out_r, ones_ap, gcol, rings, n=512, partial=False):
    sq, sq_r = fm_norm_a(k, g, pa, pa_r, rings, n)
    fm_norm_b(k, g, sq, sq_r, pa, pa_r, out_ap, out_r, ones_ap, gcol, rings, n, partial)


def phase_M(k, es, l, g, mem_in, w_kv, kcT, kcT_r, vc, vc_r):
    vec, vec_r = g["vec"], g["vec_r"]
    memT, memT_r = sb(k, es, "memT", [128, 8, MEM], BF16)
    rings = norm_rings(k, es)
    for t in range(2):
        norm_tile(k, g, rings, mem_in[t * 128:(t + 1) * 128, :], memT, memT_r, t)
    wf = Ring(k, es, "wf", [128, 8, 512], F32, 2)
    wb = Ring(k, es, "wb", [128, 8, 512], BF16, 2)
    pacc = Ring(k, es, "pacc", [128, 512], F32, 2, psum=True)
    nrings = (Ring(k, es, "sq", [128, 512], BF16, 2), Ring(k, es, "pss", [128, 512], F32, 2, psum=True),
              Ring(k, es, "rsq", [128, 512], F32, 2), Ring(k, es, "rstd", [128, 512], F32, 2))
    w_l = w_kv[l].rearrange("(kc p) c -> p kc c", p=128)
    bs = []
    for j in range(2):
        f, f_r = wf.next()
        k.dma("sp", f[:], w_l[:, :, j * 512:(j + 1) * 512], [], [f_r], f_r)
        b, b_r = wb.next()
        g_b = vec[:, V_MEMG:V_MEMG + 8].unsqueeze(2).to_broadcast([128, 8, 512])
        k.op("pool", LZ("tensor_tensor", b[:], f[:], g_b, ALU.mult), [f_r, vec_r], [b_r])
        bs.append((b, b_r))
    b, b_r = bs[0]
    for h in range(4):
        pa, pa_r = pacc.next()
        mm_group(k, pa[:, 0:MEM], [(b[:, kc, h * 128:(h + 1) * 128], memT[:, kc, :]) for kc in range(8)],
                 [b_r, memT_r], pa_r)
        fm_norm(k, g, pa[:, 0:MEM], pa_r, kcT[:, h, :], kcT_r, g["ones128"], vec[:, V_MK:V_MK + 1], nrings,
                n=MEM, partial=True)
    b, b_r = bs[1]
    for t in range(2):
        pa, pa_r = pacc.next()
        mm_group(k, pa[:], [(memT[:, kc, t * 128:(t + 1) * 128], b[:, kc, :]) for kc in range(8)],
                 [b_r, memT_r], pa_r)
        k.op("act", LZ("copy", vc[:, t, :], pa[:]), [pa_r], [vc_r], partial=True)


def phase_G(k, es, l, h, g, w2f, w2b, sc):
    vx, vx_r = g["vx"], g["vx_r"]
    ident, cbr = g["ident"], g["cst_b_r"]
    QK = float(128 ** -0.5)
    vh, vh_r = sb(k, es, "vh", [128, NT, 256], BF16)
    sgh, sgh_r = sb(k, es, "sgh", [128, 2, S], BF16)
    qe = [sb(k, es, "qe%d" % d, [128, S], BF16) for d in range(2)]
    ke = [sb(k, es, "ke%d" % d, [128, S], BF16) for d in range(2)]
    kd = [sb(k, es, "kd%d" % d, [128, NT, 128], BF16) for d in range(2)]
    eT = [sb(k, es, "eT%d" % d, [128, NT], F32) for d in range(2)]
    Sb_all, Sb_r = sb(k, es, "Sb_all", [128, NT, 256], BF16)
    psS = Ring(k, es, "psS", [128, 256], F32, 2, psum=True)
    Sst = Ring(k, es, "Sst", [128, 256], F32, 3)

    stA = {}

    def sweepA_init():
        Scur, Scur_r = Sst.next()
        k.op("pool", LZ("memset", Scur[:], 0.0), [], [Scur_r])
        k.op("pool", LZ("memset", Sb_all[:, NT - 1, :], 0.0), [], [Sb_r], partial=True)
        stA["S"] = (Scur, Scur_r)
        stA["c"] = NT - 1

    def sweepA_step():
        c = stA["c"]
        if c < 1:
            return
        kd_t, kd_r = kd[1]
        eT_t, eT_r = eT[1]
        Scur, Scur_r = stA["S"]
        pS, pS_r = psS.next()
        k.op("pe", LZ("matmul", pS[:], kd_t[:, c, :], vh[:, c, :], start=True, stop=True), [kd_r, vh_r], [pS_r])
        Sn, Sn_r = Sst.next()
        k.op("dve", LZ("scalar_tensor_tensor", Sn[:], Scur[:], eT_t[:, c - 1:c], pS[:], ALU.mult, ALU.add),
             [Scur_r, pS_r, eT_r], [Sn_r])
        k.op("act", LZ("copy", Sb_all[:, c - 1, :], Sn[:]), [Sn_r], [Sb_r], partial=True)
        stA["S"] = (Sn, Sn_r)
        stA["c"] = c - 1

    with ExitStack() as ep:
        qT, qT_r = sb(k, ep, "qT", [128, S], BF16)
        kT, kT_r = sb(k, ep, "kT", [128, S], BF16)
        lr, lr_r = sb(k, ep, "lr", [16, 2, S], BF16)
        w2s, w2s_r = sb(k, ep, "w2s", [16, 2, 128], F32)
        w2, w2_r = sb(k, ep, "w2", [16, 2, 128], BF16)
        k.dma("sp", lr[:, 0, :], sc["lrf"], [], [lr_r], lr_r, partial=True)
        k.dma("sp", lr[:, 1, :], sc["lrb"], [], [lr_r], lr_r, partial=True)
        k.dma("sp", w2s[:, 0, :], w2f[l][:, h * 128:(h + 1) * 128], [], [w2s_r], w2s_r, partial=True)
        k.dma("sp", w2s[:, 1, :], w2b[l][:, h * 128:(h + 1) * 128], [], [w2s_r], w2s_r, partial=True)
        k.dma("sp", qT[:], sc["qT"][h], [], [qT_r], qT_r)
        k.dma("sp", kT[:], sc["kT"][h], [], [kT_r], kT_r)
        k.dma("sp", vh[:], sc["v"].rearrange("(t p) c -> p t c", p=128)[:, :, h * 256:(h + 1) * 256], [], [vh_r], vh_r)
        k.dma("sp", sgh[:], sc["sg"][2 * h:2 * h + 2].rearrange("c p s -> p c s"), [], [sgh_r], sgh_r)
        k.op("dve", LZ("tensor_copy", w2[:], w2s[:]), [w2s_r], [w2_r])
        pz = Ring(k, ep, "pz", [128, 512], F32, 3, psum=True)
        ptp = Ring(k, ep, "ptp", [128, 4, 128], BF16, 3, psum=True)
        tmp = Ring(k, ep, "gtmp", [128, 512], F32, 12)
        kdT = Ring(k, ep, "kdT", [128, 4, 128], BF16, 4)

        def stage1(d, tb):
            blk = slice(tb * 512, (tb + 1) * 512)
            nb = vx[:, 2 + 4 * d + h:3 + 4 * d + h]
            zp, zp_r = pz.next()
            k.op("pe", LZ("matmul", zp[:], w2[:, d, :], lr[:, d, blk], start=True, stop=True), [w2_r, lr_r], [zp_r])
            e1, e1_r = tmp.next()
            k.op("act", LZ("activation", e1[:], zp[:], AF.Exp, bias=nb, scale=-1.0), [zp_r, vx_r], [e1_r])
            sp_, sp_r = tmp.next()
            k.op("act", LZ("activation", sp_[:], e1[:], AF.Ln, bias=g["one"]), [e1_r, g["eps_r"]], [sp_r])
            Q, Q_r = tmp.next()
            k.op("dve", LZ("tensor_tensor_scan", Q[:], g["scan_m"][:], sp_[:], 0.0, ALU.mult, ALU.add),
                 [sp_r, g["scan_m_r"]], [Q_r])
            if d == 0:
                X, X_r = Q, Q_r
            else:
                X, X_r = tmp.next()
                k.op("dve", LZ("tensor_tensor", X[:], Q[:], sp_[:], ALU.subtract), [Q_r, sp_r], [X_r])
            return (Q, Q_r, X, X_r)

        def stage2(d, tb, Q, Q_r, X, X_r):
            blk = slice(tb * 512, (tb + 1) * 512)
            qe_t, qe_r = qe[d]
            ke_t, ke_r = ke[d]
            kd_t, kd_r = kd[d]
            eT_t, eT_r = eT[d]
            sq_, sk_ = (-1.0 / 16, 1.0 / 16) if d == 0 else (1.0 / 16, -1.0 / 16)
            E1, E1_r = tmp.next()
            k.op("act", LZ("activation", E1[:], X[:], AF.Exp, scale=sq_), [X_r], [E1_r])
            E2, E2_r = tmp.next()
            k.op("act", LZ("activation", E2[:], X[:], AF.Exp, scale=sk_), [X_r], [E2_r])
            if d == 0:
                k.op("dve", LZ("tensor_copy", eT_t[:, tb * 4:(tb + 1) * 4],
                               E1[:].rearrange("p (c j) -> p c j", j=128)[:, :, 127]), [E1_r], [eT_r], partial=True)
            else:
                k.op("act", LZ("activation", eT_t[:, tb * 4:(tb + 1) * 4],
                               Q[:].rearrange("p (c j) -> p c j", j=128)[:, :, 127], AF.Exp, scale=-1.0 / 16),
                     [Q_r], [eT_r], partial=True)
            k.op("dve", LZ("scalar_tensor_tensor", qe_t[:, blk], qT[:, blk], QK, E1[:], ALU.mult, ALU.mult),
                 [qT_r, E1_r], [qe_r], partial=True)
            k.op("dve", LZ("tensor_tensor", ke_t[:, blk], kT[:, blk], E2[:], ALU.mult), [kT_r, E2_r], [ke_r], partial=True)
            kt_, kt_r = kdT.next()
            kev = ke_t[:, blk].rearrange("p (c j) -> p c j", j=128)
            if d == 0:
                k.op("dve", LZ("tensor_tensor", kt_[:], kev,
                               eT_t[:, tb * 4:(tb + 1) * 4].unsqueeze(2).to_broadcast([128, 4, 128]), ALU.mult),
                     [ke_r, eT_r], [kt_r])
            elif tb == 0:
                k.op("dve", LZ("tensor_copy", kt_[:, 0:1, :], kev[:, 0:1, :]), [ke_r], [kt_r], partial=True)
                k.op("dve", LZ("tensor_tensor", kt_[:, 1:4, :], kev[:, 1:4, :],
                               eT_t[:, 0:3].unsqueeze(2).to_broadcast([128, 3, 128]), ALU.mult),
                     [ke_r, eT_r], [kt_r], partial=True)
            else:
                k.op("dve", LZ("tensor_tensor", kt_[:], kev,
                               eT_t[:, tb * 4 - 1:tb * 4 + 3].unsqueeze(2).to_broadcast([128, 4, 128]), ALU.mult),
                     [ke_r, eT_r], [kt_r])
            return kt_, kt_r

        def stage3(d, tb, kt_, kt_r):
            kd_t, kd_r = kd[d]
            pt, pt_r = ptp.next()
            for j in range(4):
                fn = LZ("transpose", pt[:, j, :], kt_[:, j, :], ident)
                if j == 3:
                    k.op("pe", fn, [kt_r, cbr], [pt_r])
                else:
                    k.op_noinc("pe", fn, [kt_r, cbr] if j == 0 else (), [pt_r] if j == 0 else ())
            k.op("act", LZ("copy", kd_t[:, tb * 4:(tb + 1) * 4, :], pt[:]), [pt_r], [kd_r], partial=True)

        its = [(d, tb) for d in (1, 0) for tb in range(NB)]
        s1 = stage1(*its[0])
        s3 = None
        for i, (d, tb) in enumerate(its):
            s1n = stage1(*its[i + 1]) if i + 1 < len(its) else None
            if d == 0 and tb == 0:
                sweepA_init()
            kt = stage2(d, tb, *s1)
            if s3 is not None:
                stage3(*s3)
            s3 = (d, tb) + kt
            if d == 0 and tb >= 2:
                for _ in range(6):
                    sweepA_step()
            s1 = s1n
        stage3(*s3)
        while stA["c"] >= 1:
            sweepA_step()
        if h == 0:
            k.dbg("dbg_qef", qe[0][0][:], qe[0][1], [128, S], BF16)
            k.dbg("dbg_keb", ke[1][0][:], ke[1][1], [128, S], BF16)
        k.flush()

    psA = Ring(k, es, "psA", [128, 2, 128], F32, 2, psum=True)
    psO = Ring(k, es, "psO", [128, 2, 128], F32, 3, psum=True)
    psN = Ring(k, es, "psN", [128, 512], F32, 1, psum=True)
    ATm = Ring(k, es, "ATm", [128, 2, 128], BF16, 3)
    Sfb = Ring(k, es, "Sfb", [128, 256], BF16, 3)
    obuf = Ring(k, es, "obuf", [128, 2, 512], F32, 2)
    sqb = Ring(k, es, "sqb", [128, 2, 512], BF16, 2)
    rq4 = Ring(k, es, "rq4", [128, 512], F32, 2)
    rd4 = Ring(k, es, "rd4", [128, 512], F32, 2)
    t14 = Ring(k, es, "t14", [128, 2, 512], F32, 2)
    ostg = Ring(k, es, "ostg", [128, 2, 512], BF16, 2)
    Scur, Scur_r = Sst.next()
    k.op("pool", LZ("memset", Scur[:], 0.0), [], [Scur_r])
    Sb_c, Sb_cr = Sfb.next()
    k.op("pool", LZ("memset", Sb_c[:], 0.0), [], [Sb_cr])
    kd_t, kd_r = kd[0]
    eT_t, eT_r = eT[0]
    maskv = g["maskFB"].rearrange("p (a b) -> p a b", a=2)

    def emit_AT(c):
        tok = slice(c * 128, (c + 1) * 128)
        pA, pA_r = psA.next()
        k.op_noinc("pe", LZ("matmul", pA[:, 0, :], ke[0][0][:, tok], qe[0][0][:, tok], start=True, stop=True),
                   [ke[0][1], qe[0][1]], [pA_r])
        k.op("pe", LZ("matmul", pA[:, 1, :], ke[1][0][:, tok], qe[1][0][:, tok], start=True, stop=True),
             [ke[1][1], qe[1][1]], [pA_r])
        am, am_r = ATm.next()
        k.op("dve", LZ("tensor_tensor", am[:], pA[:], maskv, ALU.mult), [pA_r, g["cst_f_r"]], [am_r])
        return am, am_r

    nxt_am = emit_AT(0)
    postq = []
    for c in range(NT):
        tok = slice(c * 128, (c + 1) * 128)
        j = c % 4
        am, am_r = nxt_am
        if c + 1 < NT:
            nxt_am = emit_AT(c + 1)
        if j == 0:
            ob_, ob_r = obuf.next()
            sq4, sq4_r = sqb.next()
        pS, pS_r = psS.next()
        k.op("pe", LZ("matmul", pS[:], kd_t[:, c, :], vh[:, c, :], start=True, stop=True), [kd_r, vh_r], [pS_r])
        pO, pO_r = psO.next()
        for dvc in range(2):
            dv = slice(dvc * 128, (dvc + 1) * 128)
            pairs = [(Sb_c[:, dv], qe[0][0][:, tok]), (vh[:, c, dv], am[:, 0, :]),
                     (Sb_all[:, c, dv], qe[1][0][:, tok]), (vh[:, c, dv], am[:, 1, :])]
            for i, (a, b) in enumerate(pairs):
                fn = LZ("matmul", pO[:, dvc, :], a, b, start=(i == 0), stop=(i == 3))
                rds = [Sb_cr, qe[0][1], vh_r, am_r, Sb_r, qe[1][1]]
                if dvc == 1 and i == 3:
                    k.op("pe", fn, rds, [pO_r])
                else:
                    k.op_noinc("pe", fn, rds if (dvc == 0 and i == 0) else (), [pO_r] if (dvc == 0 and i == 0) else ())
        Sn, Sn_r = Sst.next()
        k.op("dve", LZ("scalar_tensor_tensor", Sn[:], Scur[:], eT_t[:, c:c + 1], pS[:], ALU.mult, ALU.add),
             [Scur_r, pS_r, eT_r], [Sn_r])
        Sb_c, Sb_cr = Sfb.next()
        k.op("act", LZ("copy", Sb_c[:], Sn[:]), [Sn_r], [Sb_cr])
        Scur, Scur_r = Sn, Sn_r
        k.op("act", LZ("copy", ob_[:, :, j * 128:(j + 1) * 128], pO[:]), [pO_r], [ob_r], partial=True)
        k.op("act", LZ("activation", sq4[:, :, j * 128:(j + 1) * 128], pO[:], AF.Square), [pO_r], [sq4_r], partial=True)
        if j == 1 and postq:
            postq.pop()()
        if j == 3:
            def post(tb=c // 4, ob_=ob_, ob_r=ob_r, sq4=sq4, sq4_r=sq4_r):
                pN, pN_r = psN.next()
                mm_group(k, pN[:], [(g["ones256"], sq4[:, 0, :]), (g["ones256"], sq4[:, 1, :])], [sq4_r, cbr], pN_r)
                rq, rq_r = rq4.next()
                k.op("act", LZ("activation", rq[:], pN[:], AF.Ln, bias=g["eps"]), [pN_r, g["eps_r"]], [rq_r])
                rd, rd_r = rd4.next()
                k.op("act", LZ("activation", rd[:], rq[:], AF.Exp, scale=-0.5), [rq_r], [rd_r])
                t1, t1_r = t14.next()
                k.op("dve", LZ("tensor_tensor", t1[:], ob_[:], rd[:].unsqueeze(1).to_broadcast([128, 2, 512]), ALU.mult),
                     [ob_r, rd_r], [t1_r])
                st, st_r = ostg.next()
                k.op("pool", LZ("tensor_tensor", st[:], t1[:], sgh[:, :, tb * 512:(tb + 1) * 512], ALU.mult),
                     [t1_r, sgh_r], [st_r])
                k.dma("sp", sc["oa"][2 * h:2 * h + 2, :, tb * 512:(tb + 1) * 512].rearrange("c p s -> p c s"), st[:],
                      [st_r], [], st_r)
            postq.append(post)
    while postq:
        postq.pop()()


def phase_N(k, es, l, g, nab, sc):
    nc = k.nc
    ident, cbr = g["ident"], g["cst_b_r"]
    NQ = 4
    qn = es.enter_context(nc.sbuf_tensor("qn_%d" % l, [128, 4, S], BF16))
    kn = es.enter_context(nc.sbuf_tensor("kn_%d" % l, [128, 4, S], BF16))
    Vx = es.enter_context(nc.sbuf_tensor("Vx_%d" % l, [128, NT, 520], BF16))
    bT = es.enter_context(nc.sbuf_tensor("bT_%d" % l, [128, 5, 5120], BF16))
    qn_r = [k.region("qn") for _ in range(NQ)]
    kn_r = [k.region("kn") for _ in range(NQ)]
    Vx_r = [k.region("Vx") for _ in range(NQ)]
    bT_r = [k.region("bT") for _ in range(5)]
    bst = Ring(k, es, "bst", [128, 2560], F32, 2)
    nqv = sc["nq"].rearrange("c p s -> p c s")
    nkv = sc["nk"].rearrange("c p s -> p c s")
    nvv = sc["nv"].rearrange("(t p) c -> p t c", p=128)

    def load_q(i):
        tk = slice(i * 1024, (i + 1) * 1024)
        k.dma("sp", qn[:, :, tk], nqv[:, :, tk], [], [qn_r[i]], qn_r[i])
        k.dma("sp", kn[:, :, tk], nkv[:, :, tk], [], [kn_r[i]], kn_r[i])
        k.dma("sp", Vx[:, i * 8:(i + 1) * 8, :], nvv[:, i * 8:(i + 1) * 8, :], [], [Vx_r[i]], Vx_r[i])

    def load_b(ty):
        for hf in range(2):
            b_, b_r = bst.next()
            k.dma("sp", b_[:], nab[l, ty, :, hf * 2560:(hf + 1) * 2560], [], [b_r], b_r)
            k.op("act", LZ("activation", bT[:, ty, hf * 2560:(hf + 1) * 2560], b_[:], AF.Exp), [b_r], [bT_r[ty]], partial=True)

    load_q(0)
    load_b(0)
    load_b(1)
    load_b(2)
    load_q(1)
    load_q(2)
    load_q(3)
    load_b(3)
    load_b(4)
    bTv = bT[:].rearrange("p t (h c q) -> p t h c q", h=8, c=5)
    psST = Ring(k, es, "psST", [128, 8, 128], F32, 3, psum=True)
    po = Ring(k, es, "po", [128, 4, 65], F32, 2, psum=True)
    PT = Ring(k, es, "PT", [128, 5, 128], BF16, 6)
    sng = Ring(k, es, "sng", [128, 512], BF16, 3)
    rec = Ring(k, es, "rec", [128, 8], F32, 3)
    obf = Ring(k, es, "obf", [128, 8, 64], F32, 3)
    obg = Ring(k, es, "obg", [128, 512], BF16, 2)
    stg = Ring(k, es, "nstg", [128, 4, 512], BF16, 2)
    types = {0: 0, 1: 1, 30: 3, 31: 4}
    tile = {}

    def scores(m, h):
        ty = types.get(m, 2)
        kb = min(max(m - 2, 0), 27)
        qtok = slice(m * 128, (m + 1) * 128)
        p_, hf = h // 2, h % 2
        prt = slice(64 * hf, 64 * hf + 64)
        pst, pst_r = psST.next()
        rds = [qn_r[m // 8]] + [kn_r[q] for q in sorted({kb // 8, (kb + 4) // 8})]
        for ch in range(5):
            kt = slice((kb + ch) * 128, (kb + ch + 1) * 128)
            fn = LZ("matmul", pst[:, ch, :], kn[prt, p_, kt], qn[prt, p_, qtok], start=True, stop=True)
            if ch == 4:
                k.op("pe", fn, rds, [pst_r])
            else:
                k.op_noinc("pe", fn, rds if ch == 0 else (), [pst_r] if ch == 0 else ())
        pt, pt_r = PT.next()
        k.op("act", LZ("activation", pt[:], pst[:, 0:5, :], AF.Exp), [pst_r], [pt_r])
        k.op("dve", LZ("tensor_tensor", pt[:], pt[:], bTv[:, ty, h, :, :], ALU.mult), [pt_r, bT_r[ty]], [pt_r])
        return pt, pt_r

    def pv(m, h, pt, pt_r):
        kb = min(max(m - 2, 0), 27)
        if h % 4 == 0:
            tile[m]["pos"].append(po.next())
        pO, pO_r = tile[m]["pos"][h // 4]
        rds = [pt_r] + [Vx_r[q] for q in sorted({kb // 8, (kb + 4) // 8})]
        for ch in range(5):
            fn = LZ("matmul", pO[:, h % 4, :], pt[:, ch, :], Vx[:, kb + ch, h * 65:(h + 1) * 65],
                    start=(ch == 0), stop=(ch == 4))
            if ch == 4:
                k.op("pe", fn, rds, [pO_r], partial=True)
            else:
                k.op_noinc("pe", fn, rds if ch == 0 else (), [pO_r] if ch == 0 else ())
        if h % 4 == 3:
            post_half(m, h // 4)
        if h == 7:
            post(m)

    def post_half(m, i):
        if i == 0:
            tile[m]["rc"] = rec.next()
            tile[m]["of"] = obf.next()
        rc, rc_r = tile[m]["rc"]
        of, of_r = tile[m]["of"]
        pO, pO_r = tile[m]["pos"][i]
        k.op("dve", LZ("reciprocal", rc[:, i * 4:(i + 1) * 4], pO[:, :, 64]), [pO_r], [rc_r], partial=True)
        k.op("dve", LZ("tensor_tensor", of[:, i * 4:(i + 1) * 4, :], pO[:, :, 0:64],
                       rc[:, i * 4:(i + 1) * 4].unsqueeze(2).to_broadcast([128, 4, 64]), ALU.mult),
             [pO_r, rc_r], [of_r], partial=True)

    def post(m):
        sg_, sg_r = tile[m]["sng"]
        of, of_r = tile[m]["of"]
        og, og_r = obg.next()
        k.op("dve", LZ("tensor_tensor", og[:], of[:].rearrange("p h d -> p (h d)"), sg_[:], ALU.mult),
             [of_r, sg_r], [og_r])
        pst, pt2_r = psST.next()
        pt2 = pst[:, 0:2, :].bitcast(BF16).rearrange("p a (b c) -> p (a b) c", c=128)
        for j in range(4):
            fn = LZ("transpose", pt2[:, j, :], og[:, j * 128:(j + 1) * 128], ident)
            if j == 3:
                k.op("pe", fn, [og_r, cbr], [pt2_r])
            else:
                k.op_noinc("pe", fn, [og_r, cbr] if j == 0 else (), [pt2_r] if j == 0 else ())
        if m % 4 == 0:
            tile["stg"] = stg.next()
        st, st_r = tile["stg"]
        k.op("act", LZ("copy", st[:, :, (m % 4) * 128:(m % 4 + 1) * 128], pt2), [pt2_r], [st_r], partial=True)
        if m % 4 == 3:
            tb = m // 4
            k.dma("sp", sc["ob"][:, :, tb * 512:(tb + 1) * 512].rearrange("c p s -> p c s"), st[:], [st_r], [], st_r)
        del tile[m]

    pend = []
    for m in range(NT):
        sg_, sg_r = sng.next()
        k.dma("sp", sg_[:], sc["sng"][m * 128:(m + 1) * 128, :], [], [sg_r], sg_r)
        tile[m] = dict(sng=(sg_, sg_r), pos=[])
        for h in range(8):
            pt, pt_r = scores(m, h)
            pend.append((m, h, pt, pt_r))
            if len(pend) > 2:
                pv(*pend.pop(0))
    while pend:
        pv(*pend.pop(0))


def alloc_F_weights(k, es, l):
    nc = k.nc
    w = dict(pa=sb(k, es, "pa", [128, 8, D], BF16), pb=sb(k, es, "pb", [128, 4, D], BF16),
             pc=sb(k, es, "pc", [128, 4, D], BF16), wo=sb(k, es, "wo", [128, 8, D], BF16))
    w["wst"] = Ring(k, es, "wst", [128, 2, D], F32, 2)
    return w


def load_F_weights(k, l, g, w, p_a, p_b, p_c, w_out):
    vec, vec_r = g["vec"], g["vec_r"]
    cnt = [0]

    def load(dst, dst_r, src, nkc, gout=False):
        v = src.rearrange("(kc p) c -> p kc c", p=128)
        for j in range(nkc // 2):
            s_, s_r = w["wst"].next()
            k.dma("sp", s_[:], v[:, 2 * j:2 * j + 2, :], [], [s_r], s_r)
            for i in range(2):
                kc = 2 * j + i
                cnt[0] += 1
                if gout:
                    col = V_GOUT + kc % 2
                    if cnt[0] % 2:
                        k.op("act", LZ("activation", dst[:, kc, :], s_[:, i, :], AF.Copy, scale=vec[:, col:col + 1]),
                             [s_r, vec_r], [dst_r], partial=True)
                    else:
                        k.op("dve", LZ("tensor_scalar", dst[:, kc, :], s_[:, i, :], vec[:, col:col + 1], None, ALU.mult),
                             [s_r, vec_r], [dst_r], partial=True)
                elif cnt[0] % 2:
                    k.op("act", LZ("copy", dst[:, kc, :], s_[:, i, :]), [s_r], [dst_r], partial=True)
                else:
                    k.op("dve", LZ("tensor_copy", dst[:, kc, :], s_[:, i, :]), [s_r], [dst_r], partial=True)

    load(w["pa"][0], w["pa"][1], p_a[l], 8, gout=True)
    load(w["pb"][0], w["pb"][1], p_b[l], 4)
    load(w["pc"][0], w["pc"][1], p_c[l], 4)
    load(w["wo"][0], w["wo"][1], w_out[l], 8)


def phase_C(k, es, l, g, kcT, kcT_r, vc, vc_r, sc, preload=None):
    cbr = g["cst_b_r"]
    mqr = Ring(k, es, "mqr", [128, 4, 512], BF16, 3)
    smr = Ring(k, es, "smr", [128, 4, 512], BF16, 3)
    pS = Ring(k, es, "pSc", [128, 512], F32, 4, psum=True)
    pN = Ring(k, es, "pNc", [128, 512], F32, 2, psum=True)
    pD = Ring(k, es, "pDc", [128, 512], F32, 2, psum=True)
    PT = Ring(k, es, "PTc", [128, 512], BF16, 6)
    rdr = Ring(k, es, "rdc", [128, 512], F32, 2)
    lqr = Ring(k, es, "lqc", [128, 512], F32, 2)
    t1r = Ring(k, es, "t1c", [128, 512], F32, 2)
    stg = Ring(k, es, "cstg", [128, 4, 512], BF16, 2)

    def loads(tb):
        blk = slice(tb * 512, (tb + 1) * 512)
        mq, mq_r = mqr.next()
        sm, sm_r = smr.next()
        k.dma("sp", mq[:], sc["mq"][:, :, blk].rearrange("c p s -> p c s"), [], [mq_r], mq_r)
        k.dma("sp", sm[:], sc["smg"][:, :, blk].rearrange("c p s -> p c s"), [], [sm_r], sm_r)
        return mq, mq_r, sm, sm_r

    def scores(mq, mq_r, h):
        pts = []
        for mc in range(2):
            ps_, ps_r = pS.next()
            k.op("pe", LZ("matmul", ps_[:], kcT[:, h, mc * 128:(mc + 1) * 128], mq[:, h, :], start=True, stop=True),
                 [kcT_r, mq_r], [ps_r])
            pt, pt_r = PT.next()
            k.op("act", LZ("activation", pt[:], ps_[:], AF.Exp), [ps_r], [pt_r])
            pts.append((pt, pt_r))
        return pts

    def pvn(pts, sm, sm_r, st, st_r, h):
        pn, pn_r = pN.next()
        mm_group(k, pn[:], [(vc[:, mc, h * 128:(h + 1) * 128], pts[mc][0][:]) for mc in range(2)],
                 [vc_r, pts[0][1], pts[1][1]], pn_r)
        pd, pd_r = pD.next()
        mm_group(k, pd[:], [(g["ones_b"], pts[mc][0][:]) for mc in range(2)], [cbr, pts[0][1], pts[1][1]], pd_r)
        lq, lq_r = lqr.next()
        k.op("act", LZ("activation", lq[:], pd[:], AF.Ln), [pd_r], [lq_r])
        rd, rd_r = rdr.next()
        k.op("act", LZ("activation", rd[:], lq[:], AF.Exp, scale=-1.0), [lq_r], [rd_r])
        t1, t1_r = t1r.next()
        k.op("dve", LZ("tensor_tensor", t1[:], pn[:], rd[:], ALU.mult), [pn_r, rd_r], [t1_r])
        k.op("pool", LZ("tensor_tensor", st[:, h, :], t1[:], sm[:, h, :], ALU.mult), [t1_r, sm_r], [st_r], partial=True)

    cur = loads(0)
    if preload is not None:
        preload()
    prev = None
    for tb in range(NB):
        nxt = loads(tb + 1) if tb + 1 < NB else None
        mq, mq_r, sm, sm_r = cur
        st, st_r = stg.next()
        for h in range(4):
            pts = scores(mq, mq_r, h)
            if prev is not None:
                pvn(*prev[:6])
                if prev[5] == 3:
                    ptb = prev[6]
                    k.dma("sp", sc["oc"][:, :, ptb * 512:(ptb + 1) * 512].rearrange("c p s -> p c s"), prev[3][:],
                          [prev[4]], [], prev[4])
            prev = (pts, sm, sm_r, st, st_r, h, tb)
        cur = nxt
    pvn(*prev[:6])
    k.dma("sp", sc["oc"][:, :, (NB - 1) * 512:NB * 512].rearrange("c p s -> p c s"), prev[3][:], [prev[4]], [], prev[4])


def pvn_unpack(*a):
    return a


def phase_F(k, es, l, g, w, x_src, x_dst, sc):
    ident, cbr = g["ident"], g["cst_b_r"]
    pa, pa_r = w["pa"]
    pb, pb_r = w["pb"]
    pc, pc_r = w["pc"]
    wo, wo_r = w["wo"]
    oar = Ring(k, es, "oar", [128, 8, 512], BF16, 2)
    obr = Ring(k, es, "obr", [128, 4, 512], BF16, 2)
    ocr = Ring(k, es, "ocr", [128, 4, 512], BF16, 2)
    mgr = Ring(k, es, "mgr", [128, 24, 512], BF16, 2)
    yTr = Ring(k, es, "yT", [128, 8, 512], BF16, 2)
    pY = Ring(k, es, "pY", [128, 512], F32, 4, psum=True)
    pZ = Ring(k, es, "pZ", [128, 512], F32, 2, psum=True)
    pX = Ring(k, es, "pX", [128, 512], F32, 2, psum=True)
    tr = Ring(k, es, "ft", [128, 512], BF16, 8)
    xr = Ring(k, es, "fx", [128, D], F32, 3)
    xnr = Ring(k, es, "fxn", [128, D], F32, 2)

    def loads(tb):
        blk = slice(tb * 512, (tb + 1) * 512)
        oa, oa_r = oar.next()
        ob, ob_r = obr.next()
        oc, oc_r = ocr.next()
        mg, mg_r = mgr.next()
        k.dma("sp", oa[:], sc["oa"][:, :, blk].rearrange("c p s -> p c s"), [], [oa_r], oa_r)
        k.dma("sp", ob[:], sc["ob"][:, :, blk].rearrange("c p s -> p c s"), [], [ob_r], ob_r)
        k.dma("sp", oc[:], sc["oc"][:, :, blk].rearrange("c p s -> p c s"), [], [oc_r], oc_r)
        k.dma("sp", mg[:], sc["mrg"][:, :, blk].rearrange("c p s -> p c s"), [], [mg_r], mg_r)
        return oa, oa_r, ob, ob_r, oc, oc_r, mg, mg_r

    cur = loads(0)
    pend_x = []
    for tb in range(NB):
        nxt = loads(tb + 1) if tb + 1 < NB else None
        oa, oa_r, ob, ob_r, oc, oc_r, mg, mg_r = cur
        xts = []
        for tt in range(4):
            t = tb * 4 + tt
            xts.append((None, None))
        yT, yT_r = yTr.next()
        pend_z = []
        for dc in range(8):
            dcs = slice(dc * 128, (dc + 1) * 128)
            pA, pA_r = pY.next()
            mm_group(k, pA[:], [(pa[:, kc, dcs], oa[:, kc, :]) for kc in range(8)], [pa_r, oa_r], pA_r)
            pB, pB_r = pY.next()
            mm_group(k, pB[:], [(pb[:, kc, dcs], ob[:, kc, :]) for kc in range(4)], [pb_r, ob_r], pB_r)
            pC, pC_r = pY.next()
            mm_group(k, pC[:], [(pc[:, kc, dcs], oc[:, kc, :]) for kc in range(4)], [pc_r, oc_r], pC_r)
            ts = []
            for (pp, pp_r, gi) in ((pA, pA_r, dc), (pB, pB_r, 8 + dc), (pC, pC_r, 16 + dc)):
                t_, t_r = tr.next()
                k.op("dve", LZ("tensor_tensor", t_[:], pp[:], mg[:, gi, :], ALU.mult), [pp_r, mg_r], [t_r])
                ts.append((t_, t_r))
            def zpart(ts=ts, dc=dc, yT=yT, yT_r=yT_r):
                pz, pz_r = pZ.next()
                mm_group(k, pz[:], [(ident, t_[:]) for (t_, t_r) in ts], [cbr] + [t_r for (t_, t_r) in ts], pz_r)
                k.op("act", LZ("copy", yT[:, dc, :], pz[:]), [pz_r], [yT_r], partial=True)
            if pend_z:
                pend_z.pop()()
            pend_z.append(zpart)
        pend_z.pop()()
        def xpart(tb=tb, xts=xts, yT=yT, yT_r=yT_r):
            def xload(tt):
                t = tb * 4 + tt
                xt, xt_r = xr.next()
                k.dma("sp", xt[:], x_src[t * 128:(t + 1) * 128, :], [], [xt_r], xt_r)
                return xt, xt_r
            xl = {tt: xload(tt) for tt in range(3)}
            for tt in range(4):
                t = tb * 4 + tt
                if tt == 1:
                    xl[3] = xload(3)
                xt, xt_r = xl[tt]
                xn, xn_r = xnr.next()
                for hf in range(2):
                    px, px_r = pX.next()
                    mm_group(k, px[:], [(yT[:, kc, tt * 128:(tt + 1) * 128], wo[:, kc, hf * 512:(hf + 1) * 512]) for kc in range(8)],
                             [yT_r, wo_r], px_r)
                    k.op("dve", LZ("tensor_tensor", xn[:, hf * 512:(hf + 1) * 512], px[:], xt[:, hf * 512:(hf + 1) * 512], ALU.add),
                         [px_r, xt_r], [xn_r], partial=True)
                k.dma("sp", x_dst[t * 128:(t + 1) * 128, :], xn[:], [xn_r], [], xn_r)
        if pend_x:
            pend_x.pop()()
        pend_x.append(xpart)
        cur = nxt
    pend_x.pop()()


def phase_P(k, es, nc, l, x_src, w_in, vec, vec_r, vx, vx_r, ident, cst_b, cst_b_r, sc):
    g = dict(ident=ident, cst_b_r=cst_b_r, vec_r=vec_r, vx_r=vx_r)
    hT = es.enter_context(nc.sbuf_tensor("hT_%d" % l, [128, 8, S], BF16))
    hT_rs = [k.region("hT") for _ in range(NB)]
    eps_t, eps_r = sb(k, es, "eps_p", [128, 1], F32)
    k.op("pool", LZ("memset", eps_t[:], EPS), [], [eps_r])
    g.update(eps=eps_t[:, 0:1], eps_r=eps_r)
    rings = norm_rings(k, es, nhb=10, nxt=6, ntp=1)
    hbs = {}
    xls = {}

    def p1l(t):
        if t < NT:
            xls[t] = norm_tile_load(k, rings, x_src[t * 128:(t + 1) * 128, :])

    def p1a(t):
        if t < NT:
            hbs[t] = norm_tile_a(k, g, rings, None, loaded=xls.pop(t))

    def p1b(t):
        if t < NT:
            h, h_r = hbs.pop(t)
            norm_tile_b(k, g, rings, h, h_r, hT, hT_rs[t // 4], t)

    wf = Ring(k, es, "wf", [128, 8, 512], F32, 2)
    wb = Ring(k, es, "wb", [128, 8, 512], BF16, 2)
    pacc = Ring(k, es, "pacc", [128, 512], F32, 5, psum=True)
    pss = Ring(k, es, "pss", [128, 512], F32, 2, psum=True)
    stg = Ring(k, es, "stg", [128, 512], BF16, 6)
    nrings = (Ring(k, es, "sq", [128, 512], BF16, 3), pss,
              Ring(k, es, "rsq", [128, 512], F32, 2), Ring(k, es, "rstd", [128, 512], F32, 2))
    nvst = Ring(k, es, "nvst", [128, 8, 65], BF16, 3)
    for i in range(3):
        k.op("pool", LZ("memset", nvst.t[i][:], 1.0), [], [nvst.r[i]])
    w_l = w_in[l].rearrange("(kc p) c -> p kc c", p=128)
    ev = [0]

    def load_w(c0, ncol):
        f, f_r = wf.next()
        k.dma("sp", f[:, :, 0:ncol], w_l[:, :, c0:c0 + ncol], [], [f_r], f_r)
        b, b_r = wb.next()
        g_b = vec[:, V_NORMG:V_NORMG + 8].unsqueeze(2).to_broadcast([128, 8, ncol])
        k.op("pool", LZ("tensor_tensor", b[:, :, 0:ncol], f[:, :, 0:ncol], g_b, ALU.mult), [f_r, vec_r], [b_r])
        return b, b_r

    def evac(pa, pa_r, st, st_r, func):
        ev[0] += 1
        if func is None and ev[0] % 2 == 0:
            k.op("dve", LZ("tensor_copy", st, pa), [pa_r], [st_r])
        else:
            k.op("act", LZ("activation", st, pa, AF.Copy if func is None else func), [pa_r], [st_r])

    def fm_block(b, b_r, dst, func=None, norm=None, first=False):
        pend = []
        for tb in range(NB):
            if first and tb == 0:
                for t in range(0, 5):
                    p1l(t)
                for t in range(0, 8):
                    p1l(t + 5)
                    p1a(t)
                for t in range(0, 4):
                    p1b(t)
            for sub in range(4):
                if first:
                    p1l(tb * 4 + 13 + sub)
                    p1b(tb * 4 + 4 + sub)
                    p1a(tb * 4 + 8 + sub)
                pa, pa_r = pacc.next()
                mm_group(k, pa[:], [(b[:, kc, sub * 128:(sub + 1) * 128], hT[:, kc, tb * 512:(tb + 1) * 512])
                                    for kc in range(8)], [b_r, hT_rs[tb]], pa_r)
                if norm is None:
                    st, st_r = stg.next()
                    evac(pa[:], pa_r, st[:], st_r, func)
                    k.dma("sp", dst[sub, :, tb * 512:(tb + 1) * 512], st[:], [st_r], [], st_r)
                else:
                    sq, sq_r = fm_norm_a(k, g, pa[:], pa_r, nrings)
                    if pend:
                        pend.pop()()

                    def tail(sq=sq, sq_r=sq_r, pa=pa, pa_r=pa_r, sub=sub, tb=tb):
                        st, st_r = stg.next()
                        fm_norm_b(k, g, sq, sq_r, pa[:], pa_r, st[:], st_r, norm[0], norm[1], nrings)
                        k.dma("sp", dst[sub, :, tb * 512:(tb + 1) * 512], st[:], [st_r], [], st_r)
                    pend.append(tail)
        if pend:
            pend.pop()()

    def lr_block(b, b_r):
        for tb in range(NB):
            for j, dst in enumerate((sc["lrf"], sc["lrb"])):
                pa, pa_r = pacc.next()
                mm_group(k, pa[0:16, :], [(b[:, kc, 16 * j:16 * j + 16], hT[:, kc, tb * 512:(tb + 1) * 512])
                                          for kc in range(8)], [b_r, hT_rs[tb]], pa_r)
                st, st_r = stg.next()
                evac(pa[0:16, :], pa_r, st[0:16, :], st_r, None)
                k.dma("sp", dst[:, tb * 512:(tb + 1) * 512], st[0:16, :], [st_r], [], st_r)

    def tm_block(b, b_r, dst_fn, func=None, nv=False):
        for t in range(NT):
            pa, pa_r = pacc.next()
            mm_group(k, pa[:], [(hT[:, kc, t * 128:(t + 1) * 128], b[:, kc, :]) for kc in range(8)],
                     [b_r, hT_rs[t // 4]], pa_r)
            if nv:
                st, st_r = nvst.next()
                pv_ = pa[:].rearrange("p (h d) -> p h d", h=8)
                if t % 2:
                    k.op("act", LZ("activation", st[:, :, 0:64], pv_, AF.Copy), [pa_r], [st_r], partial=True)
                else:
                    k.op("dve", LZ("tensor_copy", st[:, :, 0:64], pv_), [pa_r], [st_r], partial=True)
                k.dma("sp", dst_fn(t), st[:].rearrange("p h d -> p (h d)"), [st_r], [], st_r)
            else:
                st, st_r = stg.next()
                evac(pa[:], pa_r, st[:], st_r, func)
                k.dma("sp", dst_fn(t), st[:], [st_r], [], st_r)

    blk64 = cst_b[:, 256:384]
    ones128 = cst_b[:, 640:768]
    vrow = lambda name, c0: (lambda t: sc[name][t * 128:(t + 1) * 128, c0:c0 + 512])
    blocks = [
        (C_GQ, 512, lambda b, r: fm_block(b, r, sc["qT"], first=True)),
        (C_GK, 512, lambda b, r: fm_block(b, r, sc["kT"])),
        (C_GV, 512, lambda b, r: tm_block(b, r, vrow("v", 0))),
        (C_GV + 512, 512, lambda b, r: tm_block(b, r, vrow("v", 512))),
        (C_GG, 512, lambda b, r: fm_block(b, r, sc["sg"][0:4], AF.Silu)),
        (C_GG + 512, 512, lambda b, r: fm_block(b, r, sc["sg"][4:8], AF.Silu)),
        (C_NG, 512, lambda b, r: tm_block(b, r, vrow("sng", 0), AF.Silu)),
        (C_MG, 512, lambda b, r: fm_block(b, r, sc["smg"], AF.Silu)),
        (C_LRF, 32, lambda b, r: lr_block(b, r)),
        (C_NQ, 512, lambda b, r: fm_block(b, r, sc["nq"], norm=(blk64, vx[:, 0:1]))),
        (C_NK, 512, lambda b, r: fm_block(b, r, sc["nk"], norm=(blk64, vec[:, V_NAK:V_NAK + 1]))),
        (C_MQ, 512, lambda b, r: fm_block(b, r, sc["mq"], norm=(ones128, vx[:, 1:2]))),
        (C_NV, 512, lambda b, r: tm_block(b, r, lambda t: sc["nv"][t * 128:(t + 1) * 128, :], nv=True)),
    ]
    for j in range(6):
        blocks.append((C_MRG + 512 * j, 512,
                       (lambda j: (lambda b, r: fm_block(b, r, sc["mrg"][4 * j:4 * j + 4], AF.Sigmoid)))(j)))
    cur = load_w(blocks[0][0], blocks[0][1])
    for i, (c0, ncol, fn) in enumerate(blocks):
        nxt = load_w(blocks[i + 1][0], blocks[i + 1][1]) if i + 1 < len(blocks) else None
        fn(cur[0], cur[1])
        cur = nxt


def make_consts():
    c = np.zeros((128, 1024), np.float32)
    c[:, 0:128] = np.eye(128, dtype=np.float32)
    c[:, 128:256] = 1.0
    blk = np.zeros((128, 128), np.float32)
    blk[:64, :64] = 1.0 / 64
    blk[64:, 64:] = 1.0 / 64
    c[:, 256:384] = blk
    j = np.arange(128)[:, None]
    i = np.arange(128)[None, :]
    c[:, 384:512] = (i >= j).astype(np.float32)
    c[:, 512:640] = (i <= j).astype(np.float32)
    c[:, 640:768] = 1.0 / 128
    c[:, 768:896] = 1.0 / 256
    m = np.ones((128, 512), np.float32)
    m[:, 0::128] = 0.0
    return c, m


def na_gather_index():
    idx = np.full((5, 128, 5, 128), 15 * 31, np.int64)
    tiles = [0, 1, 2, 30, 31]
    for ti, m in enumerate(tiles):
        kb = min(max(m - 2, 0), 27)
        for q in range(128):
            r = 2 * m + q // 64
            c = q % 64
            rs = min(max(r - 4, 0), 56)
            cs = min(max(c - 8, 0), 48)
            for ch in range(5):
                for kk in range(128):
                    tok = (kb + ch) * 128 + kk
                    kr, kc = tok // 64, tok % 64
                    if rs <= kr < rs + 8 and cs <= kc < cs + 16:
                        idx[ti, kk, ch, q] = (kr - r + 7) * 31 + (kc - c + 15)
    return idx


_NA_IDX = None


def host_layout(inp):
    global _NA_IDX
    f = lambda a: np.ascontiguousarray(np.asarray(a, dtype=np.float32))
    vecs = np.zeros((L, 128, NVEC), np.float32)
    for l in range(L):
        vecs[l, :, V_NORMG:V_NORMG + 8] = f(inp["norm_g"])[l].reshape(8, 128).T
        vecs[l, :, V_MEMG:V_MEMG + 8] = f(inp["mem_norm_g"])[l].reshape(8, 128).T
        vecs[l, :, V_GOUT:V_GOUT + 2] = f(inp["gla_out_g"])[l].reshape(2, 128).T
        vecs[l, :, V_NAQ] = np.tile(f(inp["na_q_g"])[l], 2)
        vecs[l, :, V_NAK] = np.tile(f(inp["na_k_g"])[l], 2)
        vecs[l, :, V_MQ] = f(inp["mem_q_g"])[l]
        vecs[l, :, V_MK] = f(inp["mem_k_g"])[l]
        vecs[l, :, V_BF:V_BF + 4] = f(inp["gla_b_f"])[l].reshape(4, 128).T
        vecs[l, :, V_BB:V_BB + 4] = f(inp["gla_b_b"])[l].reshape(4, 128).T
    if _NA_IDX is None:
        _NA_IDX = na_gather_index()
    rpb = f(inp["na_rpb"]).reshape(L, 8, 15 * 31)
    rpb_pad = np.concatenate([rpb, np.full((L, 8, 1), NEG, np.float32)], axis=2)
    g = rpb_pad[:, :, _NA_IDX]
    nab = np.ascontiguousarray(g.transpose(0, 2, 3, 1, 4, 5)).reshape(L, 5, 128, 8 * 5 * 128)
    c, m = make_consts()
    shared = dict(w_in=f(inp["w_in"]), w2f=f(inp["gla_w2_f"]), w2b=f(inp["gla_w2_b"]), p_a=f(inp["p_a"]),
                  p_b=f(inp["p_b"]), p_c=f(inp["p_c"]), w_kv=f(inp["w_mem_kv"]), w_out=f(inp["w_out"]),
                  vecs=vecs, nab=nab, cst=c, scanm=m)
    x = f(inp["x"])
    mem = f(inp["mem"])
    return [dict(shared, x=x[b], mem=mem[b]) for b in range(8)]


def kernel(**inputs):
    in_maps = host_layout(inputs)
    nc = build()
    res = run_bass_kernel_spmd(nc, in_maps, core_ids=list(range(8)))
    return np.stack([np.asarray(r["y"], dtype=np.float32) for r in res.results], axis=0)
```

```python
import numpy as np
import ml_dtypes
from contextlib import ExitStack
import concourse.bass as bass
import concourse.mybir as mybir
from concourse.bass_utils import run_bass_kernel_spmd

F32 = mybir.dt.float32
BF16 = mybir.dt.bfloat16
AF = mybir.ActivationFunctionType
ALU = mybir.AluOpType

S = 4096
D = 1024
NT = 32
NB = 8
L = 2
MEM = 256
INC = 9248
EPS = 1e-6
NEG = -30000.0

C_GQ, C_GK, C_GV, C_GG = 0, 512, 1024, 2048
C_LRF, C_LRB = 3072, 3088
C_NQ, C_NK, C_NV, C_NG = 3104, 3616, 4128, 4640
C_MQ, C_MG, C_MRG = 5152, 5664, 6176

V_NORMG, V_MEMG, V_GOUT, V_NAQ, V_NAK, V_MQ, V_MK, V_BF, V_BB = 0, 8, 16, 18, 19, 20, 21, 22, 26
NVEC = 30


class Region:
    __slots__ = ("name", "w", "r")

    def __init__(self, name):
        self.name = name
        self.w = {}
        self.r = {}


class K:
    ENG = ("pe", "act", "dve", "pool", "sp")

    def __init__(self, nc, es):
        self.nc = nc
        self.es = es
        self.sem = {}
        self.cnt = {}
        for e in self.ENG:
            self.sem[e] = es.enter_context(nc.semaphore("s_" + e))
            self.cnt[e] = 0
        self.dma_pool = [es.enter_context(nc.semaphore("d%d" % i)) for i in range(72)]
        self.dma_val = {id(s): 0 for s in self.dma_pool}
        self.dma_free = list(self.dma_pool)
        self.dma_used = []
        self.reg_sem = {}
        self.q = {e: [] for e in self.ENG}
        self.seen = {e: {} for e in self.ENG}
        self.nreg = 0
        self.dbgset = ()

    def region(self, name="r"):
        self.nreg += 1
        return Region("%s%d" % (name, self.nreg))

    def _need(self, eng, ev, waits):
        sem, val, src = ev
        key = id(sem)
        if self.seen[eng].get(key, 0) >= val:
            return
        cur = waits.get(key)
        if cur is None or cur[1] < val:
            waits[key] = (sem, val)

    def _deps(self, eng, reads, writes):
        waits = {}
        for r in reads:
            for ev in r.w.values():
                self._need(eng, ev, waits)
        for r in writes:
            for ev in r.w.values():
                if ev[2] != eng or eng in ("sp",):
                    self._need(eng, ev, waits)
            for ev in r.r.values():
                if ev[2] != eng or eng in ("sp",):
                    self._need(eng, ev, waits)
        for key, (sem, val) in waits.items():
            self.q[eng].append(("w", sem, val))
            self.seen[eng][key] = val

    def _record(self, ev, reads, writes, partial):
        key = id(ev[0])
        for r in reads:
            r.r[key] = ev
        for r in writes:
            if partial:
                r.w[key] = ev
            else:
                r.w = {key: ev}
                r.r = {}

    def op(self, eng, fn, reads=(), writes=(), partial=False):
        self._deps(eng, reads, writes)
        self.cnt[eng] += 1
        ev = (self.sem[eng], self.cnt[eng], eng)
        self.q[eng].append(("i", fn, self.sem[eng], 1))
        self._record(ev, reads, writes, partial)

    def op_noinc(self, eng, fn, reads=(), writes=()):
        self._deps(eng, reads, writes)
        ev = (self.sem[eng], self.cnt[eng] + 1, eng)
        self.q[eng].append(("n", fn))
        self._record(ev, reads, writes, True)

    def dma(self, q, out, in_, reads, writes, semreg, partial=False):
        self._deps(q, reads, writes)
        sem = self.reg_sem.get(id(semreg))
        if sem is None:
            sem = self.dma_free.pop()
            self.reg_sem[id(semreg)] = sem
            self.dma_used.append(sem)
        self.dma_val[id(sem)] += 16
        ev = (sem, self.dma_val[id(sem)], "dma")
        self.q[q].append(("i", LZ("dma_start", out=out, in_=in_), sem, 16))
        self._record(ev, reads, writes, partial)

    def dbg(self, name, ap, reg, shape, dtype):
        if name not in self.dbgset:
            return
        d = self.nc.dram_tensor(name, list(shape), dtype, kind="ExternalOutput").ap()
        r = self.region("dbg")
        self.dma("sp", d, ap, [reg], [], r)

    def flush(self):
        nc = self.nc
        for sem in self.dma_used:
            v = self.dma_val[id(sem)]
            if self.seen["sp"].get(id(sem), 0) < v:
                self.q["sp"].append(("w", sem, v))
                self.seen["sp"][id(sem)] = v
        for e in self.ENG:
            for f in self.ENG:
                if f == e or self.cnt[f] == 0:
                    continue
                if self.seen[e].get(id(self.sem[f]), 0) < self.cnt[f]:
                    self.q[e].append(("w", self.sem[f], self.cnt[f]))
                    self.seen[e][id(self.sem[f])] = self.cnt[f]
        qs = self.q
        with nc.Block() as block:
            def mk(items):
                def body(e):
                    for it in items:
                        if it[0] == "w":
                            e.wait_ge(it[1], it[2])
                        elif it[0] == "i":
                            it[1](e).then_inc(it[2], it[3])
                        else:
                            it[1](e)
                return body
            block.tensor(mk(qs["pe"]))
            block.scalar(mk(qs["act"]))
            block.vector(mk(qs["dve"]))
            block.gpsimd(mk(qs["pool"]))
            block.sync(mk(qs["sp"]))
        self.q = {e: [] for e in self.ENG}
        for e in self.ENG:
            for f in self.ENG:
                self.seen[e][id(self.sem[f])] = self.cnt[f]
            for sem in self.dma_pool:
                self.seen[e][id(sem)] = self.dma_val[id(sem)]
        self.dma_free = list(self.dma_pool)
        self.dma_used = []
        self.reg_sem = {}


class Ring:
    def __init__(self, k, es, name, shape, dtype, n, psum=False):
        self.t = []
        self.r = []
        for i in range(n):
            k.nreg += 1
            if psum:
                t = es.enter_context(k.nc.psum_tensor("%s%d_%d" % (name, i, k.nreg), shape, dtype))
            else:
                t = es.enter_context(k.nc.sbuf_tensor("%s%d_%d" % (name, i, k.nreg), shape, dtype))
            self.t.append(t)
            self.r.append(k.region(name))
        self.i = 0
        self.n = n

    def next(self):
        j = self.i % self.n
        self.i += 1
        return self.t[j], self.r[j]


def LZ(name, *args, **kw):
    return lambda e: getattr(e, name)(*args, **kw)


def sb(k, es, name, shape, dtype):
    k.nreg += 1
    return es.enter_context(k.nc.sbuf_tensor("%s_%d" % (name, k.nreg), shape, dtype)), k.region(name)


def ps(k, es, name, shape, dtype=F32):
    k.nreg += 1
    return es.enter_context(k.nc.psum_tensor("%s_%d" % (name, k.nreg), shape, dtype)), k.region(name)


def mm_group(k, out_ap, pairs, reads, out_reg):
    n = len(pairs)
    for i, (a, b) in enumerate(pairs):
        fn = LZ("matmul", out_ap, a, b, start=(i == 0), stop=(i == n - 1))
        if i == n - 1:
            k.op("pe", fn, reads=reads, writes=[out_reg])
        else:
            k.op_noinc("pe", fn, reads=reads if i == 0 else (), writes=[out_reg] if i == 0 else ())


def build(dbg=()):
    nc = bass.Bass("TRN2", target_bir_lowering=False)
    dt = lambda name, shape, dtype, kind: nc.dram_tensor(name, list(shape), dtype, kind=kind).ap()
    x_in = dt("x", (S, D), F32, "ExternalInput")
    mem_in = dt("mem", (MEM, D), F32, "ExternalInput")
    w_in = dt("w_in", (L, D, INC), F32, "ExternalInput")
    w2f = dt("w2f", (L, 16, 512), F32, "ExternalInput")
    w2b = dt("w2b", (L, 16, 512), F32, "ExternalInput")
    p_a = dt("p_a", (L, 1024, D), F32, "ExternalInput")
    p_b = dt("p_b", (L, 512, D), F32, "ExternalInput")
    p_c = dt("p_c", (L, 512, D), F32, "ExternalInput")
    w_kv = dt("w_kv", (L, D, 1024), F32, "ExternalInput")
    w_out = dt("w_out", (L, D, D), F32, "ExternalInput")
    vecs = dt("vecs", (L, 128, NVEC), F32, "ExternalInput")
    nab = dt("nab", (L, 5, 128, 8 * 5 * 128), F32, "ExternalInput")
    cst = dt("cst", (128, 8 * 128), F32, "ExternalInput")
    scanm = dt("scanm", (128, 512), F32, "ExternalInput")
    y_out = dt("y", (S, D), F32, "ExternalOutput")

    def scratch(name, shape, dtype=BF16):
        kind = "ExternalOutput" if name in dbg else "Internal"
        return dt(name, shape, dtype, kind)

    x_mid = scratch("x_mid", (S, D), F32)
    qT_s = scratch("qT_s", (4, 128, S))
    kT_s = scratch("kT_s", (4, 128, S))
    v_s = scratch("v_s", (S, 1024))
    sg_s = scratch("sg_s", (8, 128, S))
    lrf_s = scratch("lrf_s", (16, S))
    lrb_s = scratch("lrb_s", (16, S))
    nq_s = scratch("nq_s", (4, 128, S))
    nk_s = scratch("nk_s", (4, 128, S))
    nv_s = scratch("nv_s", (S, 8 * 65))
    sng_s = scratch("sng_s", (S, 512))
    mq_s = scratch("mq_s", (4, 128, S))
    smg_s = scratch("smg_s", (4, 128, S))
    mrg_s = scratch("mrg_s", (24, 128, S))
    oa_s = scratch("oa_s", (8, 128, S))
    ob_s = scratch("ob_s", (4, 128, S))
    oc_s = scratch("oc_s", (4, 128, S))

    stop = [d for d in dbg if d.startswith("stop:")]
    stop = stop[0][5:] if stop else None
    sc = dict(qT=qT_s, kT=kT_s, v=v_s, sg=sg_s, lrf=lrf_s, lrb=lrb_s, nq=nq_s, nk=nk_s,
              nv=nv_s, sng=sng_s, mq=mq_s, smg=smg_s, mrg=mrg_s, oa=oa_s, ob=ob_s, oc=oc_s)

    with ExitStack() as es0:
        k = K(nc, es0)
        k.dbgset = dbg
        cst_f, cst_f_r = sb(k, es0, "cst_f", [128, 1024], F32)
        cst_b, cst_b_r = sb(k, es0, "cst_b", [128, 1024], BF16)
        scan_m, scan_m_r = sb(k, es0, "scan_m", [128, 512], F32)
        eps_t, eps_r = sb(k, es0, "eps_t", [128, 2], F32)
        k.dma("sp", cst_f[:], cst, [], [cst_f_r], cst_f_r)
        k.dma("sp", scan_m[:], scanm, [], [scan_m_r], scan_m_r)
        k.op("dve", LZ("tensor_copy", cst_b[:], cst_f[:]), [cst_f_r], [cst_b_r])
        k.op("pool", LZ("memset", eps_t[:, 0:1], EPS), [], [eps_r], partial=True)
        k.op("pool", LZ("memset", eps_t[:, 1:2], 1.0), [], [eps_r], partial=True)
        g = dict(ident=cst_b[:, 0:128], ones_b=cst_b[:, 128:256], blk64=cst_b[:, 256:384],
                 maskFB=cst_f[:, 384:640], ones128=cst_b[:, 640:768], ones256=cst_b[:, 768:896],
                 cst_b_r=cst_b_r, cst_f_r=cst_f_r, scan_m=scan_m, scan_m_r=scan_m_r,
                 eps=eps_t[:, 0:1], one=eps_t[:, 1:2], eps_r=eps_r)
        k.flush()

        for l in range(L):
            x_src = x_in if l == 0 else x_mid
            x_dst = x_mid if l == 0 else y_out
            with ExitStack() as esl:
                vec, vec_r = sb(k, esl, "vec", [128, NVEC], F32)
                vx, vx_r = sb(k, esl, "vx", [128, 16], F32)
                kcT, kcT_r = sb(k, esl, "kcT", [128, 4, MEM], BF16)
                vc, vc_r = sb(k, esl, "vc", [128, 2, 512], BF16)
                k.dma("sp", vec[:], vecs[l], [], [vec_r], vec_r)
                k.op("dve", LZ("tensor_scalar", vx[:, 0:1], vec[:, V_NAQ:V_NAQ + 1], 0.125, None, ALU.mult),
                     [vec_r], [vx_r], partial=True)
                k.op("dve", LZ("tensor_scalar", vx[:, 1:2], vec[:, V_MQ:V_MQ + 1], float(128 ** -0.5), None, ALU.mult),
                     [vec_r], [vx_r], partial=True)
                k.op("dve", LZ("tensor_scalar", vx[:, 2:10], vec[:, V_BF:V_BF + 8], -1.0, None, ALU.mult),
                     [vec_r], [vx_r], partial=True)
                g.update(vec=vec, vec_r=vec_r, vx=vx, vx_r=vx_r)
                k.flush()

                with ExitStack() as es:
                    phase_P(k, es, nc, l, x_src, w_in, vec, vec_r, vx, vx_r, g["ident"], cst_b, cst_b_r, sc)
                    k.flush()
                if stop == "P":
                    break
                with ExitStack() as es:
                    phase_M(k, es, l, g, mem_in, w_kv, kcT, kcT_r, vc, vc_r)
                    k.flush()
                for h in range(4):
                    with ExitStack() as es:
                        phase_G(k, es, l, h, g, w2f, w2b, sc)
                        k.flush()
                if stop == "G":
                    break
                with ExitStack() as es:
                    phase_N(k, es, l, g, nab, sc)
                    k.flush()
                if stop == "N":
                    break
                with ExitStack() as esw:
                    w = alloc_F_weights(k, esw, l)
                    with ExitStack() as es:
                        phase_C(k, es, l, g, kcT, kcT_r, vc, vc_r, sc,
                                preload=lambda: load_F_weights(k, l, g, w, p_a, p_b, p_c, w_out))
                        k.flush()
                    if stop == "C":
                        break
                    with ExitStack() as es:
                        phase_F(k, es, l, g, w, x_src, x_dst, sc)
                        k.flush()
        k.flush()
    return nc


def norm_tile_load(k, rings, src_ap):
    xt, xt_r = rings[0].next()
    k.dma("sp", xt[:], src_ap, [], [xt_r], xt_r)
    return xt, xt_r


def norm_tile_a(k, g, rings, src_ap, loaded=None):
    xr, jr, hb, ssr, s2r, rsr, tp = rings
    xt, xt_r = loaded if loaded is not None else norm_tile_load(k, rings, src_ap)
    jk, jk_r = jr.next()
    ss, ss_r = ssr.next()
    k.op("act", LZ("activation", jk[:], xt[:], AF.Square, scale=1.0 / 32.0, accum_out=ss[:]), [xt_r], [jk_r, ss_r])
    s2, s2_r = s2r.next()
    k.op("act", LZ("activation", s2[:], ss[:], AF.Sqrt, bias=g["eps"]), [ss_r, g["eps_r"]], [s2_r])
    rs, rs_r = rsr.next()
    k.op("dve", LZ("reciprocal", rs[:], s2[:]), [s2_r], [rs_r])
    h, h_r = hb.next()
    k.op("dve", LZ("tensor_scalar", h[:], xt[:], rs[:, 0:1], None, ALU.mult), [xt_r, rs_r], [h_r])
    return h, h_r


def norm_tile_b(k, g, rings, h, h_r, dstT, dstT_r, t):
    tp = rings[6]
    p, p_r = tp.next()
    for kc in range(8):
        fn = LZ("transpose", p[:, kc, :], h[:, kc * 128:(kc + 1) * 128], g["ident"])
        if kc == 7:
            k.op("pe", fn, [h_r, g["cst_b_r"]], [p_r])
        else:
            k.op_noinc("pe", fn, [h_r, g["cst_b_r"]] if kc == 0 else (), [p_r] if kc == 0 else ())
    if t % 2 == 0:
        k.op("act", LZ("copy", dstT[:, :, t * 128:(t + 1) * 128], p[:]), [p_r], [dstT_r], partial=True)
    else:
        k.op("dve", LZ("tensor_copy", dstT[:, :, t * 128:(t + 1) * 128], p[:]), [p_r], [dstT_r], partial=True)


def norm_tile(k, g, rings, src_ap, dstT, dstT_r, t):
    h, h_r = norm_tile_a(k, g, rings, src_ap)
    norm_tile_b(k, g, rings, h, h_r, dstT, dstT_r, t)


def norm_rings(k, es, nhb=2, nxt=3, ntp=2):
    return (Ring(k, es, "xt", [128, D], F32, nxt), Ring(k, es, "junk", [128, D], BF16, 2),
            Ring(k, es, "hb", [128, D], BF16, nhb), Ring(k, es, "ss", [128, 1], F32, 4),
            Ring(k, es, "s2", [128, 1], F32, 4), Ring(k, es, "rs", [128, 1], F32, 4),
            Ring(k, es, "tp", [128, 8, 128], BF16, ntp, psum=True))


def fm_norm_a(k, g, pa, pa_r, rings, n=512):
    sqr, pss, rsq, rstd = rings
    sq, sq_r = sqr.next()
    k.op("act", LZ("activation", sq[:, 0:n], pa, AF.Square), [pa_r], [sq_r])
    return sq, sq_r


def fm_norm_b(k, g, sq, sq_r, pa, pa_r, out_ap, out_r, ones_ap, gcol, rings, n=512, partial=False):
    sqr, pss, rsq, rstd = rings
    p2, p2_r = pss.next()
    k.op("pe", LZ("matmul", p2[:, 0:n], ones_ap, sq[:, 0:n], start=True, stop=True), [sq_r, g["cst_b_r"]], [p2_r])
    rq, rq_r = rsq.next()
    k.op("act", LZ("activation", rq[:, 0:n], p2[:, 0:n], AF.Ln, bias=g["eps"]), [p2_r, g["eps_r"]], [rq_r])
    rd, rd_r = rstd.next()
    k.op("act", LZ("activation", rd[:, 0:n], rq[:, 0:n], AF.Exp, scale=-0.5), [rq_r], [rd_r])
    k.op("dve", LZ("scalar_tensor_tensor", out_ap, pa, gcol, rd[:, 0:n], ALU.mult, ALU.mult),
         [pa_r, rd_r, g["vec_r"], g["vx_r"]], [out_r], partial=partial)


def fm_norm(k, g, pa, pa_r, out_ap, out_r, ones_ap, gcol, rings, n=512, partial=False):
    sq, sq_r = fm_norm_a(k, g, pa, pa_r, rings, n)
    fm_norm_b(k, g, sq, sq_r, pa, pa_r, out_ap, out_r, ones_ap, gcol, rings, n, partial)


def phase_M(k, es, l, g, mem_in, w_kv, kcT, kcT_r, vc, vc_r):
    vec, vec_r = g["vec"], g["vec_r"]
    memT, memT_r = sb(k, es, "memT", [128, 8, MEM], BF16)
    rings = norm_rings(k, es)
    for t in range(2):
        norm_tile(k, g, rings, mem_in[t * 128:(t + 1) * 128, :], memT, memT_r, t)
    wf = Ring(k, es, "wf", [128, 8, 512], F32, 2)
    wb = Ring(k, es, "wb", [128, 8, 512], BF16, 2)
    pacc = Ring(k, es, "pacc", [128, 512], F32, 2, psum=True)
    nrings = (Ring(k, es, "sq", [128, 512], BF16, 2), Ring(k, es, "pss", [128, 512], F32, 2, psum=True),
              Ring(k, es, "rsq", [128, 512], F32, 2), Ring(k, es, "rstd", [128, 512], F32, 2))
    w_l = w_kv[l].rearrange("(kc p) c -> p kc c", p=128)
    bs = []
    for j in range(2):
        f, f_r = wf.next()
        k.dma("sp", f[:], w_l[:, :, j * 512:(j + 1) * 512], [], [f_r], f_r)
        b, b_r = wb.next()
        g_b = vec[:, V_MEMG:V_MEMG + 8].unsqueeze(2).to_broadcast([128, 8, 512])
        k.op("pool", LZ("tensor_tensor", b[:], f[:], g_b, ALU.mult), [f_r, vec_r], [b_r])
        bs.append((b, b_r))
    b, b_r = bs[0]
    for h in range(4):
        pa, pa_r = pacc.next()
        mm_group(k, pa[:, 0:MEM], [(b[:, kc, h * 128:(h + 1) * 128], memT[:, kc, :]) for kc in range(8)],
                 [b_r, memT_r], pa_r)
        fm_norm(k, g, pa[:, 0:MEM], pa_r, kcT[:, h, :], kcT_r, g["ones128"], vec[:, V_MK:V_MK + 1], nrings,
                n=MEM, partial=True)
    b, b_r = bs[1]
    for t in range(2):
        pa, pa_r = pacc.next()
        mm_group(k, pa[:], [(memT[:, kc, t * 128:(t + 1) * 128], b[:, kc, :]) for kc in range(8)],
                 [b_r, memT_r], pa_r)
        k.op("act", LZ("copy", vc[:, t, :], pa[:]), [pa_r], [vc_r], partial=True)


def phase_G(k, es, l, h, g, w2f, w2b, sc):
    vx, vx_r = g["vx"], g["vx_r"]
    ident, cbr = g["ident"], g["cst_b_r"]
    QK = float(128 ** -0.5)
    vh, vh_r = sb(k, es, "vh", [128, NT, 256], BF16)
    sgh, sgh_r = sb(k, es, "sgh", [128, 2, S], BF16)
    qe = [sb(k, es, "qe%d" % d, [128, S], BF16) for d in range(2)]
    ke = [sb(k, es, "ke%d" % d, [128, S], BF16) for d in range(2)]
    kd = [sb(k, es, "kd%d" % d, [128, NT, 128], BF16) for d in range(2)]
    eT = [sb(k, es, "eT%d" % d, [128, NT], F32) for d in range(2)]
    Sb_all, Sb_r = sb(k, es, "Sb_all", [128, NT, 256], BF16)
    psS = Ring(k, es, "psS", [128, 256], F32, 2, psum=True)
    Sst = Ring(k, es, "Sst", [128, 256], F32, 3)

    stA = {}

    def sweepA_init():
        Scur, Scur_r = Sst.next()
        k.op("pool", LZ("memset", Scur[:], 0.0), [], [Scur_r])
        k.op("pool", LZ("memset", Sb_all[:, NT - 1, :], 0.0), [], [Sb_r], partial=True)
        stA["S"] = (Scur, Scur_r)
        stA["c"] = NT - 1

    def sweepA_step():
        c = stA["c"]
        if c < 1:
            return
        kd_t, kd_r = kd[1]
        eT_t, eT_r = eT[1]
        Scur, Scur_r = stA["S"]
        pS, pS_r = psS.next()
        k.op("pe", LZ("matmul", pS[:], kd_t[:, c, :], vh[:, c, :], start=True, stop=True), [kd_r, vh_r], [pS_r])
        Sn, Sn_r = Sst.next()
        k.op("dve", LZ("scalar_tensor_tensor", Sn[:], Scur[:], eT_t[:, c - 1:c], pS[:], ALU.mult, ALU.add),
             [Scur_r, pS_r, eT_r], [Sn_r])
        k.op("act", LZ("copy", Sb_all[:, c - 1, :], Sn[:]), [Sn_r], [Sb_r], partial=True)
        stA["S"] = (Sn, Sn_r)
        stA["c"] = c - 1

    with ExitStack() as ep:
        qT, qT_r = sb(k, ep, "qT", [128, S], BF16)
        kT, kT_r = sb(k, ep, "kT", [128, S], BF16)
        lr, lr_r = sb(k, ep, "lr", [16, 2, S], BF16)
        w2s, w2s_r = sb(k, ep, "w2s", [16, 2, 128], F32)
        w2, w2_r = sb(k, ep, "w2", [16, 2, 128], BF16)
        k.dma("sp", lr[:, 0, :], sc["lrf"], [], [lr_r], lr_r, partial=True)
        k.dma("sp", lr[:, 1, :], sc["lrb"], [], [lr_r], lr_r, partial=True)
        k.dma("sp", w2s[:, 0, :], w2f[l][:, h * 128:(h + 1) * 128], [], [w2s_r], w2s_r, partial=True)
        k.dma("sp", w2s[:, 1, :], w2b[l][:, h * 128:(h + 1) * 128], [], [w2s_r], w2s_r, partial=True)
        k.dma("sp", qT[:], sc["qT"][h], [], [qT_r], qT_r)
        k.dma("sp", kT[:], sc["kT"][h], [], [kT_r], kT_r)
        k.dma("sp", vh[:], sc["v"].rearrange("(t p) c -> p t c", p=128)[:, :, h * 256:(h + 1) * 256], [], [vh_r], vh_r)
        k.dma("sp", sgh[:], sc["sg"][2 * h:2 * h + 2].rearrange("c p s -> p c s"), [], [sgh_r], sgh_r)
        k.op("dve", LZ("tensor_copy", w2[:], w2s[:]), [w2s_r], [w2_r])
        pz = Ring(k, ep, "pz", [128, 512], F32, 3, psum=True)
        ptp = Ring(k, ep, "ptp", [128, 4, 128], BF16, 3, psum=True)
        tmp = Ring(k, ep, "gtmp", [128, 512], F32, 12)
        kdT = Ring(k, ep, "kdT", [128, 4, 128], BF16, 4)

        def stage1(d, tb):
            blk = slice(tb * 512, (tb + 1) * 512)
            nb = vx[:, 2 + 4 * d + h:3 + 4 * d + h]
            zp, zp_r = pz.next()
            k.op("pe", LZ("matmul", zp[:], w2[:, d, :], lr[:, d, blk], start=True, stop=True), [w2_r, lr_r], [zp_r])
            e1, e1_r = tmp.next()
            k.op("act", LZ("activation", e1[:], zp[:], AF.Exp, bias=nb, scale=-1.0), [zp_r, vx_r], [e1_r])
            sp_, sp_r = tmp.next()
            k.op("act", LZ("activation", sp_[:], e1[:], AF.Ln, bias=g["one"]), [e1_r, g["eps_r"]], [sp_r])
            Q, Q_r = tmp.next()
            k.op("dve", LZ("tensor_tensor_scan", Q[:], g["scan_m"][:], sp_[:], 0.0, ALU.mult, ALU.add),
                 [sp_r, g["scan_m_r"]], [Q_r])
            if d == 0:
                X, X_r = Q, Q_r
            else:
                X, X_r = tmp.next()
                k.op("dve", LZ("tensor_tensor", X[:], Q[:], sp_[:], ALU.subtract), [Q_r, sp_r], [X_r])
            return (Q, Q_r, X, X_r)

        def stage2(d, tb, Q, Q_r, X, X_r):
            blk = slice(tb * 512, (tb + 1) * 512)
            qe_t, qe_r = qe[d]
            ke_t, ke_r = ke[d]
            kd_t, kd_r = kd[d]
            eT_t, eT_r = eT[d]
            sq_, sk_ = (-1.0 / 16, 1.0 / 16) if d == 0 else (1.0 / 16, -1.0 / 16)
            E1, E1_r = tmp.next()
            k.op("act", LZ("activation", E1[:], X[:], AF.Exp, scale=sq_), [X_r], [E1_r])
            E2, E2_r = tmp.next()
            k.op("act", LZ("activation", E2[:], X[:], AF.Exp, scale=sk_), [X_r], [E2_r])
            if d == 0:
                k.op("dve", LZ("tensor_copy", eT_t[:, tb * 4:(tb + 1) * 4],
                               E1[:].rearrange("p (c j) -> p c j", j=128)[:, :, 127]), [E1_r], [eT_r], partial=True)
            else:
                k.op("act", LZ("activation", eT_t[:, tb * 4:(tb + 1) * 4],
                               Q[:].rearrange("p (c j) -> p c j", j=128)[:, :, 127], AF.Exp, scale=-1.0 / 16),
                     [Q_r], [eT_r], partial=True)
            k.op("dve", LZ("scalar_tensor_tensor", qe_t[:, blk], qT[:, blk], QK, E1[:], ALU.mult, ALU.mult),
                 [qT_r, E1_r], [qe_r], partial=True)
            k.op("dve", LZ("tensor_tensor", ke_t[:, blk], kT[:, blk], E2[:], ALU.mult), [kT_r, E2_r], [ke_r], partial=True)
            kt_, kt_r = kdT.next()
            kev = ke_t[:, blk].rearrange("p (c j) -> p c j", j=128)
            if d == 0:
                k.op("dve", LZ("tensor_tensor", kt_[:], kev,
                               eT_t[:, tb * 4:(tb + 1) * 4].unsqueeze(2).to_broadcast([128, 4, 128]), ALU.mult),
                     [ke_r, eT_r], [kt_r])
            elif tb == 0:
                k.op("dve", LZ("tensor_copy", kt_[:, 0:1, :], kev[:, 0:1, :]), [ke_r], [kt_r], partial=True)
                k.op("dve", LZ("tensor_tensor", kt_[:, 1:4, :], kev[:, 1:4, :],
                               eT_t[:, 0:3].unsqueeze(2).to_broadcast([128, 3, 128]), ALU.mult),
                     [ke_r, eT_r], [kt_r], partial=True)
            else:
                k.op("dve", LZ("tensor_tensor", kt_[:], kev,
                               eT_t[:, tb * 4 - 1:tb * 4 + 3].unsqueeze(2).to_broadcast([128, 4, 128]), ALU.mult),
                     [ke_r, eT_r], [kt_r])
            return kt_, kt_r

        def stage3(d, tb, kt_, kt_r):
            kd_t, kd_r = kd[d]
            pt, pt_r = ptp.next()
            for j in range(4):
                fn = LZ("transpose", pt[:, j, :], kt_[:, j, :], ident)
                if j == 3:
                    k.op("pe", fn, [kt_r, cbr], [pt_r])
                else:
                    k.op_noinc("pe", fn, [kt_r, cbr] if j == 0 else (), [pt_r] if j == 0 else ())
            k.op("act", LZ("copy", kd_t[:, tb * 4:(tb + 1) * 4, :], pt[:]), [pt_r], [kd_r], partial=True)

        its = [(d, tb) for d in (1, 0) for tb in range(NB)]
        s1 = stage1(*its[0])
        s3 = None
        for i, (d, tb) in enumerate(its):
            s1n = stage1(*its[i + 1]) if i + 1 < len(its) else None
            if d == 0 and tb == 0:
                sweepA_init()
            kt = stage2(d, tb, *s1)
            if s3 is not None:
                stage3(*s3)
            s3 = (d, tb) + kt
            if d == 0 and tb >= 2:
                for _ in range(6):
                    sweepA_step()
            s1 = s1n
        stage3(*s3)
        while stA["c"] >= 1:
            sweepA_step()
        if h == 0:
            k.dbg("dbg_qef", qe[0][0][:], qe[0][1], [128, S], BF16)
            k.dbg("dbg_keb", ke[1][0][:], ke[1][1], [128, S], BF16)
        k.flush()

    psA = Ring(k, es, "psA", [128, 2, 128], F32, 2, psum=True)
    psO = Ring(k, es, "psO", [128, 2, 128], F32, 3, psum=True)
    psN = Ring(k, es, "psN", [128, 512], F32, 1, psum=True)
    ATm = Ring(k, es, "ATm", [128, 2, 128], BF16, 3)
    Sfb = Ring(k, es, "Sfb", [128, 256], BF16, 3)
    obuf = Ring(k, es, "obuf", [128, 2, 512], F32, 2)
    sqb = Ring(k, es, "sqb", [128, 2, 512], BF16, 2)
    rq4 = Ring(k, es, "rq4", [128, 512], F32, 2)
    rd4 = Ring(k, es, "rd4", [128, 512], F32, 2)
    t14 = Ring(k, es, "t14", [128, 2, 512], F32, 2)
    ostg = Ring(k, es, "ostg", [128, 2, 512], BF16, 2)
    Scur, Scur_r = Sst.next()
    k.op("pool", LZ("memset", Scur[:], 0.0), [], [Scur_r])
    Sb_c, Sb_cr = Sfb.next()
    k.op("pool", LZ("memset", Sb_c[:], 0.0), [], [Sb_cr])
    kd_t, kd_r = kd[0]
    eT_t, eT_r = eT[0]
    maskv = g["maskFB"].rearrange("p (a b) -> p a b", a=2)

    def emit_AT(c):
        tok = slice(c * 128, (c + 1) * 128)
        pA, pA_r = psA.next()
        k.op_noinc("pe", LZ("matmul", pA[:, 0, :], ke[0][0][:, tok], qe[0][0][:, tok], start=True, stop=True),
                   [ke[0][1], qe[0][1]], [pA_r])
        k.op("pe", LZ("matmul", pA[:, 1, :], ke[1][0][:, tok], qe[1][0][:, tok], start=True, stop=True),
             [ke[1][1], qe[1][1]], [pA_r])
        am, am_r = ATm.next()
        k.op("dve", LZ("tensor_tensor", am[:], pA[:], maskv, ALU.mult), [pA_r, g["cst_f_r"]], [am_r])
        return am, am_r

    nxt_am = emit_AT(0)
    postq = []
    for c in range(NT):
        tok = slice(c * 128, (c + 1) * 128)
        j = c % 4
        am, am_r = nxt_am
        if c + 1 < NT:
            nxt_am = emit_AT(c + 1)
        if j == 0:
            ob_, ob_r = obuf.next()
            sq4, sq4_r = sqb.next()
        pS, pS_r = psS.next()
        k.op("pe", LZ("matmul", pS[:], kd_t[:, c, :], vh[:, c, :], start=True, stop=True), [kd_r, vh_r], [pS_r])
        pO, pO_r = psO.next()
        for dvc in range(2):
            dv = slice(dvc * 128, (dvc + 1) * 128)
            pairs = [(Sb_c[:, dv], qe[0][0][:, tok]), (vh[:, c, dv], am[:, 0, :]),
                     (Sb_all[:, c, dv], qe[1][0][:, tok]), (vh[:, c, dv], am[:, 1, :])]
            for i, (a, b) in enumerate(pairs):
                fn = LZ("matmul", pO[:, dvc, :], a, b, start=(i == 0), stop=(i == 3))
                rds = [Sb_cr, qe[0][1], vh_r, am_r, Sb_r, qe[1][1]]
                if dvc == 1 and i == 3:
                    k.op("pe", fn, rds, [pO_r])
                else:
                    k.op_noinc("pe", fn, rds if (dvc == 0 and i == 0) else (), [pO_r] if (dvc == 0 and i == 0) else ())
        Sn, Sn_r = Sst.next()
        k.op("dve", LZ("scalar_tensor_tensor", Sn[:], Scur[:], eT_t[:, c:c + 1], pS[:], ALU.mult, ALU.add),
             [Scur_r, pS_r, eT_r], [Sn_r])
        Sb_c, Sb_cr = Sfb.next()
        k.op("act", LZ("copy", Sb_c[:], Sn[:]), [Sn_r], [Sb_cr])
        Scur, Scur_r = Sn, Sn_r
        k.op("act", LZ("copy", ob_[:, :, j * 128:(j + 1) * 128], pO[:]), [pO_r], [ob_r], partial=True)
        k.op("act", LZ("activation", sq4[:, :, j * 128:(j + 1) * 128], pO[:], AF.Square), [pO_r], [sq4_r], partial=True)
        if j == 1 and postq:
            postq.pop()()
        if j == 3:
            def post(tb=c // 4, ob_=ob_, ob_r=ob_r, sq4=sq4, sq4_r=sq4_r):
                pN, pN_r = psN.next()
                mm_group(k, pN[:], [(g["ones256"], sq4[:, 0, :]), (g["ones256"], sq4[:, 1, :])], [sq4_r, cbr], pN_r)
                rq, rq_r = rq4.next()
                k.op("act", LZ("activation", rq[:], pN[:], AF.Ln, bias=g["eps"]), [pN_r, g["eps_r"]], [rq_r])
                rd, rd_r = rd4.next()
                k.op("act", LZ("activation", rd[:], rq[:], AF.Exp, scale=-0.5), [rq_r], [rd_r])
                t1, t1_r = t14.next()
                k.op("dve", LZ("tensor_tensor", t1[:], ob_[:], rd[:].unsqueeze(1).to_broadcast([128, 2, 512]), ALU.mult),
                     [ob_r, rd_r], [t1_r])
                st, st_r = ostg.next()
                k.op("pool", LZ("tensor_tensor", st[:], t1[:], sgh[:, :, tb * 512:(tb + 1) * 512], ALU.mult),
                     [t1_r, sgh_r], [st_r])
                k.dma("sp", sc["oa"][2 * h:2 * h + 2, :, tb * 512:(tb + 1) * 512].rearrange("c p s -> p c s"), st[:],
                      [st_r], [], st_r)
            postq.append(post)
    while postq:
        postq.pop()()


def phase_N(k, es, l, g, nab, sc):
    nc = k.nc
    ident, cbr = g["ident"], g["cst_b_r"]
    NQ = 4
    qn = es.enter_context(nc.sbuf_tensor("qn_%d" % l, [128, 4, S], BF16))
    kn = es.enter_context(nc.sbuf_tensor("kn_%d" % l, [128, 4, S], BF16))
    Vx = es.enter_context(nc.sbuf_tensor("Vx_%d" % l, [128, NT, 520], BF16))
    bT = es.enter_context(nc.sbuf_tensor("bT_%d" % l, [128, 5, 5120], BF16))
    qn_r = [k.region("qn") for _ in range(NQ)]
    kn_r = [k.region("kn") for _ in range(NQ)]
    Vx_r = [k.region("Vx") for _ in range(NQ)]
    bT_r = [k.region("bT") for _ in range(5)]
    bst = Ring(k, es, "bst", [128, 2560], F32, 2)
    nqv = sc["nq"].rearrange("c p s -> p c s")
    nkv = sc["nk"].rearrange("c p s -> p c s")
    nvv = sc["nv"].rearrange("(t p) c -> p t c", p=128)

    def load_q(i):
        tk = slice(i * 1024, (i + 1) * 1024)
        k.dma("sp", qn[:, :, tk], nqv[:, :, tk], [], [qn_r[i]], qn_r[i])
        k.dma("sp", kn[:, :, tk], nkv[:, :, tk], [], [kn_r[i]], kn_r[i])
        k.dma("sp", Vx[:, i * 8:(i + 1) * 8, :], nvv[:, i * 8:(i + 1) * 8, :], [], [Vx_r[i]], Vx_r[i])

    def load_b(ty):
        for hf in range(2):
            b_, b_r = bst.next()
            k.dma("sp", b_[:], nab[l, ty, :, hf * 2560:(hf + 1) * 2560], [], [b_r], b_r)
            k.op("act", LZ("activation", bT[:, ty, hf * 2560:(hf + 1) * 2560], b_[:], AF.Exp), [b_r], [bT_r[ty]], partial=True)

    load_q(0)
    load_b(0)
    load_b(1)
    load_b(2)
    load_q(1)
    load_q(2)
    load_q(3)
    load_b(3)
    load_b(4)
    bTv = bT[:].rearrange("p t (h c q) -> p t h c q", h=8, c=5)
    psST = Ring(k, es, "psST", [128, 8, 128], F32, 3, psum=True)
    po = Ring(k, es, "po", [128, 4, 65], F32, 2, psum=True)
    PT = Ring(k, es, "PT", [128, 5, 128], BF16, 6)
    sng = Ring(k, es, "sng", [128, 512], BF16, 3)
    rec = Ring(k, es, "rec", [128, 8], F32, 3)
    obf = Ring(k, es, "obf", [128, 8, 64], F32, 3)
    obg = Ring(k, es, "obg", [128, 512], BF16, 2)
    stg = Ring(k, es, "nstg", [128, 4, 512], BF16, 2)
    types = {0: 0, 1: 1, 30: 3, 31: 4}
    tile = {}

    def scores(m, h):
        ty = types.get(m, 2)
        kb = min(max(m - 2, 0), 27)
        qtok = slice(m * 128, (m + 1) * 128)
        p_, hf = h // 2, h % 2
        prt = slice(64 * hf, 64 * hf + 64)
        pst, pst_r = psST.next()
        rds = [qn_r[m // 8]] + [kn_r[q] for q in sorted({kb // 8, (kb + 4) // 8})]
        for ch in range(5):
            kt = slice((kb + ch) * 128, (kb + ch + 1) * 128)
            fn = LZ("matmul", pst[:, ch, :], kn[prt, p_, kt], qn[prt, p_, qtok], start=True, stop=True)
            if ch == 4:
                k.op("pe", fn, rds, [pst_r])
            else:
                k.op_noinc("pe", fn, rds if ch == 0 else (), [pst_r] if ch == 0 else ())
        pt, pt_r = PT.next()
        k.op("act", LZ("activation", pt[:], pst[:, 0:5, :], AF.Exp), [pst_r], [pt_r])
        k.op("dve", LZ("tensor_tensor", pt[:], pt[:], bTv[:, ty, h, :, :], ALU.mult), [pt_r, bT_r[ty]], [pt_r])
        return pt, pt_r

    def pv(m, h, pt, pt_r):
        kb = min(max(m - 2, 0), 27)
        if h % 4 == 0:
            tile[m]["pos"].append(po.next())
        pO, pO_r = tile[m]["pos"][h // 4]
        rds = [pt_r] + [Vx_r[q] for q in sorted({kb // 8, (kb + 4) // 8})]
        for ch in range(5):
            fn = LZ("matmul", pO[:, h % 4, :], pt[:, ch, :], Vx[:, kb + ch, h * 65:(h + 1) * 65],
                    start=(ch == 0), stop=(ch == 4))
            if ch == 4:
                k.op("pe", fn, rds, [pO_r], partial=True)
            else:
                k.op_noinc("pe", fn, rds if ch == 0 else (), [pO_r] if ch == 0 else ())
        if h % 4 == 3:
            post_half(m, h // 4)
        if h == 7:
            post(m)

    def post_half(m, i):
        if i == 0:
            tile[m]["rc"] = rec.next()
            tile[m]["of"] = obf.next()
        rc, rc_r = tile[m]["rc"]
        of, of_r = tile[m]["of"]
        pO, pO_r = tile[m]["pos"][i]
        k.op("dve", LZ("reciprocal", rc[:, i * 4:(i + 1) * 4], pO[:, :, 64]), [pO_r], [rc_r], partial=True)
        k.op("dve", LZ("tensor_tensor", of[:, i * 4:(i + 1) * 4, :], pO[:, :, 0:64],
                       rc[:, i * 4:(i + 1) * 4].unsqueeze(2).to_broadcast([128, 4, 64]), ALU.mult),
             [pO_r, rc_r], [of_r], partial=True)

    def post(m):
        sg_, sg_r = tile[m]["sng"]
        of, of_r = tile[m]["of"]
        og, og_r = obg.next()
        k.op("dve", LZ("tensor_tensor", og[:], of[:].rearrange("p h d -> p (h d)"), sg_[:], ALU.mult),
             [of_r, sg_r], [og_r])
        pst, pt2_r = psST.next()
        pt2 = pst[:, 0:2, :].bitcast(BF16).rearrange("p a (b c) -> p (a b) c", c=128)
        for j in range(4):
            fn = LZ("transpose", pt2[:, j, :], og[:, j * 128:(j + 1) * 128], ident)
            if j == 3:
                k.op("pe", fn, [og_r, cbr], [pt2_r])
            else:
                k.op_noinc("pe", fn, [og_r, cbr] if j == 0 else (), [pt2_r] if j == 0 else ())
        if m % 4 == 0:
            tile["stg"] = stg.next()
        st, st_r = tile["stg"]
        k.op("act", LZ("copy", st[:, :, (m % 4) * 128:(m % 4 + 1) * 128], pt2), [pt2_r], [st_r], partial=True)
        if m % 4 == 3:
            tb = m // 4
            k.dma("sp", sc["ob"][:, :, tb * 512:(tb + 1) * 512].rearrange("c p s -> p c s"), st[:], [st_r], [], st_r)
        del tile[m]

    pend = []
    for m in range(NT):
        sg_, sg_r = sng.next()
        k.dma("sp", sg_[:], sc["sng"][m * 128:(m + 1) * 128, :], [], [sg_r], sg_r)
        tile[m] = dict(sng=(sg_, sg_r), pos=[])
        for h in range(8):
            pt, pt_r = scores(m, h)
            pend.append((m, h, pt, pt_r))
            if len(pend) > 2:
                pv(*pend.pop(0))
    while pend:
        pv(*pend.pop(0))


def alloc_F_weights(k, es, l):
    nc = k.nc
    w = dict(pa=sb(k, es, "pa", [128, 8, D], BF16), pb=sb(k, es, "pb", [128, 4, D], BF16),
             pc=sb(k, es, "pc", [128, 4, D], BF16), wo=sb(k, es, "wo", [128, 8, D], BF16))
    w["wst"] = Ring(k, es, "wst", [128, 2, D], F32, 2)
    return w


def load_F_weights(k, l, g, w, p_a, p_b, p_c, w_out):
    vec, vec_r = g["vec"], g["vec_r"]
    cnt = [0]

    def load(dst, dst_r, src, nkc, gout=False):
        v = src.rearrange("(kc p) c -> p kc c", p=128)
        for j in range(nkc // 2):
            s_, s_r = w["wst"].next()
            k.dma("sp", s_[:], v[:, 2 * j:2 * j + 2, :], [], [s_r], s_r)
            for i in range(2):
                kc = 2 * j + i
                cnt[0] += 1
                if gout:
                    col = V_GOUT + kc % 2
                    if cnt[0] % 2:
                        k.op("act", LZ("activation", dst[:, kc, :], s_[:, i, :], AF.Copy, scale=vec[:, col:col + 1]),
                             [s_r, vec_r], [dst_r], partial=True)
                    else:
                        k.op("dve", LZ("tensor_scalar", dst[:, kc, :], s_[:, i, :], vec[:, col:col + 1], None, ALU.mult),
                             [s_r, vec_r], [dst_r], partial=True)
                elif cnt[0] % 2:
                    k.op("act", LZ("copy", dst[:, kc, :], s_[:, i, :]), [s_r], [dst_r], partial=True)
                else:
                    k.op("dve", LZ("tensor_copy", dst[:, kc, :], s_[:, i, :]), [s_r], [dst_r], partial=True)

    load(w["pa"][0], w["pa"][1], p_a[l], 8, gout=True)
    load(w["pb"][0], w["pb"][1], p_b[l], 4)
    load(w["pc"][0], w["pc"][1], p_c[l], 4)
    load(w["wo"][0], w["wo"][1], w_out[l], 8)


def phase_C(k, es, l, g, kcT, kcT_r, vc, vc_r, sc, preload=None):
    cbr = g["cst_b_r"]
    mqr = Ring(k, es, "mqr", [128, 4, 512], BF16, 3)
    smr = Ring(k, es, "smr", [128, 4, 512], BF16, 3)
    pS = Ring(k, es, "pSc", [128, 512], F32, 4, psum=True)
    pN = Ring(k, es, "pNc", [128, 512], F32, 2, psum=True)
    pD = Ring(k, es, "pDc", [128, 512], F32, 2, psum=True)
    PT = Ring(k, es, "PTc", [128, 512], BF16, 6)
    rdr = Ring(k, es, "rdc", [128, 512], F32, 2)
    lqr = Ring(k, es, "lqc", [128, 512], F32, 2)
    t1r = Ring(k, es, "t1c", [128, 512], F32, 2)
    stg = Ring(k, es, "cstg", [128, 4, 512], BF16, 2)

    def loads(tb):
        blk = slice(tb * 512, (tb + 1) * 512)
        mq, mq_r = mqr.next()
        sm, sm_r = smr.next()
        k.dma("sp", mq[:], sc["mq"][:, :, blk].rearrange("c p s -> p c s"), [], [mq_r], mq_r)
        k.dma("sp", sm[:], sc["smg"][:, :, blk].rearrange("c p s -> p c s"), [], [sm_r], sm_r)
        return mq, mq_r, sm, sm_r

    def scores(mq, mq_r, h):
        pts = []
        for mc in range(2):
            ps_, ps_r = pS.next()
            k.op("pe", LZ("matmul", ps_[:], kcT[:, h, mc * 128:(mc + 1) * 128], mq[:, h, :], start=True, stop=True),
                 [kcT_r, mq_r], [ps_r])
            pt, pt_r = PT.next()
            k.op("act", LZ("activation", pt[:], ps_[:], AF.Exp), [ps_r], [pt_r])
            pts.append((pt, pt_r))
        return pts

    def pvn(pts, sm, sm_r, st, st_r, h):
        pn, pn_r = pN.next()
        mm_group(k, pn[:], [(vc[:, mc, h * 128:(h + 1) * 128], pts[mc][0][:]) for mc in range(2)],
                 [vc_r, pts[0][1], pts[1][1]], pn_r)
        pd, pd_r = pD.next()
        mm_group(k, pd[:], [(g["ones_b"], pts[mc][0][:]) for mc in range(2)], [cbr, pts[0][1], pts[1][1]], pd_r)
        lq, lq_r = lqr.next()
        k.op("act", LZ("activation", lq[:], pd[:], AF.Ln), [pd_r], [lq_r])
        rd, rd_r = rdr.next()
        k.op("act", LZ("activation", rd[:], lq[:], AF.Exp, scale=-1.0), [lq_r], [rd_r])
        t1, t1_r = t1r.next()
        k.op("dve", LZ("tensor_tensor", t1[:], pn[:], rd[:], ALU.mult), [pn_r, rd_r], [t1_r])
        k.op("dve", LZ("tensor_tensor", st[:, h, :], t1[:], sm[:, h, :], ALU.mult), [t1_r, sm_r], [st_r], partial=True)

    cur = loads(0)
    if preload is not None:
        preload()
    prev = None
    for tb in range(NB):
        nxt = loads(tb + 1) if tb + 1 < NB else None
        mq, mq_r, sm, sm_r = cur
        st, st_r = stg.next()
        for h in range(4):
            pts = scores(mq, mq_r, h)
            if prev is not None:
                pvn(*prev[:6])
                if prev[5] == 3:
                    ptb = prev[6]
                    k.dma("sp", sc["oc"][:, :, ptb * 512:(ptb + 1) * 512].rearrange("c p s -> p c s"), prev[3][:],
                          [prev[4]], [], prev[4])
            prev = (pts, sm, sm_r, st, st_r, h, tb)
        cur = nxt
    pvn(*prev[:6])
    k.dma("sp", sc["oc"][:, :, (NB - 1) * 512:NB * 512].rearrange("c p s -> p c s"), prev[3][:], [prev[4]], [], prev[4])


def pvn_unpack(*a):
    return a


def phase_F(k, es, l, g, w, x_src, x_dst, sc):
    ident, cbr = g["ident"], g["cst_b_r"]
    pa, pa_r = w["pa"]
    pb, pb_r = w["pb"]
    pc, pc_r = w["pc"]
    wo, wo_r = w["wo"]
    oar = Ring(k, es, "oar", [128, 8, 512], BF16, 2)
    obr = Ring(k, es, "obr", [128, 4, 512], BF16, 2)
    ocr = Ring(k, es, "ocr", [128, 4, 512], BF16, 2)
    mgr = Ring(k, es, "mgr", [128, 24, 512], BF16, 2)
    yTr = Ring(k, es, "yT", [128, 8, 512], BF16, 2)
    pY = Ring(k, es, "pY", [128, 512], F32, 4, psum=True)
    pZ = Ring(k, es, "pZ", [128, 512], F32, 2, psum=True)
    pX = Ring(k, es, "pX", [128, 512], F32, 2, psum=True)
    tr = Ring(k, es, "ft", [128, 512], BF16, 8)
    xr = Ring(k, es, "fx", [128, D], F32, 3)
    xnr = Ring(k, es, "fxn", [128, D], F32, 2)

    def loads(tb):
        blk = slice(tb * 512, (tb + 1) * 512)
        oa, oa_r = oar.next()
        ob, ob_r = obr.next()
        oc, oc_r = ocr.next()
        mg, mg_r = mgr.next()
        k.dma("sp", oa[:], sc["oa"][:, :, blk].rearrange("c p s -> p c s"), [], [oa_r], oa_r)
        k.dma("sp", ob[:], sc["ob"][:, :, blk].rearrange("c p s -> p c s"), [], [ob_r], ob_r)
        k.dma("sp", oc[:], sc["oc"][:, :, blk].rearrange("c p s -> p c s"), [], [oc_r], oc_r)
        k.dma("sp", mg[:], sc["mrg"][:, :, blk].rearrange("c p s -> p c s"), [], [mg_r], mg_r)
        return oa, oa_r, ob, ob_r, oc, oc_r, mg, mg_r

    cur = loads(0)
    pend_x = []
    for tb in range(NB):
        nxt = loads(tb + 1) if tb + 1 < NB else None
        oa, oa_r, ob, ob_r, oc, oc_r, mg, mg_r = cur
        xts = []
        for tt in range(4):
            t = tb * 4 + tt
            xts.append((None, None))
        yT, yT_r = yTr.next()
        pend_z = []
        for dc in range(8):
            dcs = slice(dc * 128, (dc + 1) * 128)
            pA, pA_r = pY.next()
            mm_group(k, pA[:], [(pa[:, kc, dcs], oa[:, kc, :]) for kc in range(8)], [pa_r, oa_r], pA_r)
            pB, pB_r = pY.next()
            mm_group(k, pB[:], [(pb[:, kc, dcs], ob[:, kc, :]) for kc in range(4)], [pb_r, ob_r], pB_r)
            pC, pC_r = pY.next()
            mm_group(k, pC[:], [(pc[:, kc, dcs], oc[:, kc, :]) for kc in range(4)], [pc_r, oc_r], pC_r)
            ts = []
            for (pp, pp_r, gi) in ((pA, pA_r, dc), (pB, pB_r, 8 + dc), (pC, pC_r, 16 + dc)):
                t_, t_r = tr.next()
                k.op("dve", LZ("tensor_tensor", t_[:], pp[:], mg[:, gi, :], ALU.mult), [pp_r, mg_r], [t_r])
                ts.append((t_, t_r))
            def zpart(ts=ts, dc=dc, yT=yT, yT_r=yT_r):
                pz, pz_r = pZ.next()
                mm_group(k, pz[:], [(ident, t_[:]) for (t_, t_r) in ts], [cbr] + [t_r for (t_, t_r) in ts], pz_r)
                k.op("act", LZ("copy", yT[:, dc, :], pz[:]), [pz_r], [yT_r], partial=True)
            if pend_z:
                pend_z.pop()()
            pend_z.append(zpart)
        pend_z.pop()()
        def xpart(tb=tb, xts=xts, yT=yT, yT_r=yT_r):
            def xload(tt):
                t = tb * 4 + tt
                xt, xt_r = xr.next()
                k.dma("sp", xt[:], x_src[t * 128:(t + 1) * 128, :], [], [xt_r], xt_r)
                return xt, xt_r
            xl = {tt: xload(tt) for tt in range(3)}
            for tt in range(4):
                t = tb * 4 + tt
                if tt == 1:
                    xl[3] = xload(3)
                xt, xt_r = xl[tt]
                xn, xn_r = xnr.next()
                for hf in range(2):
                    px, px_r = pX.next()
                    mm_group(k, px[:], [(yT[:, kc, tt * 128:(tt + 1) * 128], wo[:, kc, hf * 512:(hf + 1) * 512]) for kc in range(8)],
                             [yT_r, wo_r], px_r)
                    k.op("dve", LZ("tensor_tensor", xn[:, hf * 512:(hf + 1) * 512], px[:], xt[:, hf * 512:(hf + 1) * 512], ALU.add),
                         [px_r, xt_r], [xn_r], partial=True)
                k.dma("sp", x_dst[t * 128:(t + 1) * 128, :], xn[:], [xn_r], [], xn_r)
        if pend_x:
            pend_x.pop()()
        pend_x.append(xpart)
        cur = nxt
    pend_x.pop()()


def phase_P(k, es, nc, l, x_src, w_in, vec, vec_r, vx, vx_r, ident, cst_b, cst_b_r, sc):
    g = dict(ident=ident, cst_b_r=cst_b_r, vec_r=vec_r, vx_r=vx_r)
    hT = es.enter_context(nc.sbuf_tensor("hT_%d" % l, [128, 8, S], BF16))
    hT_rs = [k.region("hT") for _ in range(NB)]
    eps_t, eps_r = sb(k, es, "eps_p", [128, 1], F32)
    k.op("pool", LZ("memset", eps_t[:], EPS), [], [eps_r])
    g.update(eps=eps_t[:, 0:1], eps_r=eps_r)
    rings = norm_rings(k, es, nhb=10, nxt=6, ntp=1)
    hbs = {}
    xls = {}

    def p1l(t):
        if t < NT:
            xls[t] = norm_tile_load(k, rings, x_src[t * 128:(t + 1) * 128, :])

    def p1a(t):
        if t < NT:
            hbs[t] = norm_tile_a(k, g, rings, None, loaded=xls.pop(t))

    def p1b(t):
        if t < NT:
            h, h_r = hbs.pop(t)
            norm_tile_b(k, g, rings, h, h_r, hT, hT_rs[t // 4], t)

    wf = Ring(k, es, "wf", [128, 8, 512], F32, 2)
    wb = Ring(k, es, "wb", [128, 8, 512], BF16, 2)
    pacc = Ring(k, es, "pacc", [128, 512], F32, 5, psum=True)
    pss = Ring(k, es, "pss", [128, 512], F32, 2, psum=True)
    stg = Ring(k, es, "stg", [128, 512], BF16, 6)
    nrings = (Ring(k, es, "sq", [128, 512], BF16, 3), pss,
              Ring(k, es, "rsq", [128, 512], F32, 2), Ring(k, es, "rstd", [128, 512], F32, 2))
    nvst = Ring(k, es, "nvst", [128, 8, 65], BF16, 3)
    for i in range(3):
        k.op("pool", LZ("memset", nvst.t[i][:], 1.0), [], [nvst.r[i]])
    w_l = w_in[l].rearrange("(kc p) c -> p kc c", p=128)
    ev = [0]

    def load_w(c0, ncol):
        f, f_r = wf.next()
        k.dma("sp", f[:, :, 0:ncol], w_l[:, :, c0:c0 + ncol], [], [f_r], f_r)
        b, b_r = wb.next()
        g_b = vec[:, V_NORMG:V_NORMG + 8].unsqueeze(2).to_broadcast([128, 8, ncol])
        k.op("pool", LZ("tensor_tensor", b[:, :, 0:ncol], f[:, :, 0:ncol], g_b, ALU.mult), [f_r, vec_r], [b_r])
        return b, b_r

    def evac(pa, pa_r, st, st_r, func):
        ev[0] += 1
        if func is None and ev[0] % 2 == 0:
            k.op("dve", LZ("tensor_copy", st, pa), [pa_r], [st_r])
        else:
            k.op("act", LZ("activation", st, pa, AF.Copy if func is None else func), [pa_r], [st_r])

    def fm_block(b, b_r, dst, func=None, norm=None, first=False):
        pend = []
        for tb in range(NB):
            if first and tb == 0:
                for t in range(0, 5):
                    p1l(t)
                for t in range(0, 8):
                    p1l(t + 5)
                    p1a(t)
                for t in range(0, 4):
                    p1b(t)
            for sub in range(4):
                if first:
                    p1l(tb * 4 + 13 + sub)
                    p1b(tb * 4 + 4 + sub)
                    p1a(tb * 4 + 8 + sub)
                pa, pa_r = pacc.next()
                mm_group(k, pa[:], [(b[:, kc, sub * 128:(sub + 1) * 128], hT[:, kc, tb * 512:(tb + 1) * 512])
                                    for kc in range(8)], [b_r, hT_rs[tb]], pa_r)
                if norm is None:
                    st, st_r = stg.next()
                    evac(pa[:], pa_r, st[:], st_r, func)
                    k.dma("sp", dst[sub, :, tb * 512:(tb + 1) * 512], st[:], [st_r], [], st_r)
                else:
                    sq, sq_r = fm_norm_a(k, g, pa[:], pa_r, nrings)
                    if pend:
                        pend.pop()()

                    def tail(sq=sq, sq_r=sq_r, pa=pa, pa_r=pa_r, sub=sub, tb=tb):
                        st, st_r = stg.next()
                        fm_norm_b(k, g, sq, sq_r, pa[:], pa_r, st[:], st_r, norm[0], norm[1], nrings)
                        k.dma("sp", dst[sub, :, tb * 512:(tb + 1) * 512], st[:], [st_r], [], st_r)
                    pend.append(tail)
        if pend:
            pend.pop()()

    def lr_block(b, b_r):
        for tb in range(NB):
            for j, dst in enumerate((sc["lrf"], sc["lrb"])):
                pa, pa_r = pacc.next()
                mm_group(k, pa[0:16, :], [(b[:, kc, 16 * j:16 * j + 16], hT[:, kc, tb * 512:(tb + 1) * 512])
                                          for kc in range(8)], [b_r, hT_rs[tb]], pa_r)
                st, st_r = stg.next()
                evac(pa[0:16, :], pa_r, st[0:16, :], st_r, None)
                k.dma("sp", dst[:, tb * 512:(tb + 1) * 512], st[0:16, :], [st_r], [], st_r)

    def tm_block(b, b_r, dst_fn, func=None, nv=False):
        for t in range(NT):
            pa, pa_r = pacc.next()
            mm_group(k, pa[:], [(hT[:, kc, t * 128:(t + 1) * 128], b[:, kc, :]) for kc in range(8)],
                     [b_r, hT_rs[t // 4]], pa_r)
            if nv:
                st, st_r = nvst.next()
                pv_ = pa[:].rearrange("p (h d) -> p h d", h=8)
                if t % 2:
                    k.op("act", LZ("activation", st[:, :, 0:64], pv_, AF.Copy), [pa_r], [st_r], partial=True)
                else:
                    k.op("dve", LZ("tensor_copy", st[:, :, 0:64], pv_), [pa_r], [st_r], partial=True)
                k.dma("sp", dst_fn(t), st[:].rearrange("p h d -> p (h d)"), [st_r], [], st_r)
            else:
                st, st_r = stg.next()
                evac(pa[:], pa_r, st[:], st_r, func)
                k.dma("sp", dst_fn(t), st[:], [st_r], [], st_r)

    blk64 = cst_b[:, 256:384]
    ones128 = cst_b[:, 640:768]
    vrow = lambda name, c0: (lambda t: sc[name][t * 128:(t + 1) * 128, c0:c0 + 512])
    blocks = [
        (C_GQ, 512, lambda b, r: fm_block(b, r, sc["qT"], first=True)),
        (C_GK, 512, lambda b, r: fm_block(b, r, sc["kT"])),
        (C_GV, 512, lambda b, r: tm_block(b, r, vrow("v", 0))),
        (C_GV + 512, 512, lambda b, r: tm_block(b, r, vrow("v", 512))),
        (C_GG, 512, lambda b, r: fm_block(b, r, sc["sg"][0:4], AF.Silu)),
        (C_GG + 512, 512, lambda b, r: fm_block(b, r, sc["sg"][4:8], AF.Silu)),
        (C_NG, 512, lambda b, r: tm_block(b, r, vrow("sng", 0), AF.Silu)),
        (C_MG, 512, lambda b, r: fm_block(b, r, sc["smg"], AF.Silu)),
        (C_LRF, 32, lambda b, r: lr_block(b, r)),
        (C_NQ, 512, lambda b, r: fm_block(b, r, sc["nq"], norm=(blk64, vx[:, 0:1]))),
        (C_NK, 512, lambda b, r: fm_block(b, r, sc["nk"], norm=(blk64, vec[:, V_NAK:V_NAK + 1]))),
        (C_MQ, 512, lambda b, r: fm_block(b, r, sc["mq"], norm=(ones128, vx[:, 1:2]))),
        (C_NV, 512, lambda b, r: tm_block(b, r, lambda t: sc["nv"][t * 128:(t + 1) * 128, :], nv=True)),
    ]
    for j in range(6):
        blocks.append((C_MRG + 512 * j, 512,
                       (lambda j: (lambda b, r: fm_block(b, r, sc["mrg"][4 * j:4 * j + 4], AF.Sigmoid)))(j)))
    cur = load_w(blocks[0][0], blocks[0][1])
    for i, (c0, ncol, fn) in enumerate(blocks):
        nxt = load_w(blocks[i + 1][0], blocks[i + 1][1]) if i + 1 < len(blocks) else None
        fn(cur[0], cur[1])
        cur = nxt


def make_consts():
    c = np.zeros((128, 1024), np.float32)
    c[:, 0:128] = np.eye(128, dtype=np.float32)
    c[:, 128:256] = 1.0
    blk = np.zeros((128, 128), np.float32)
    blk[:64, :64] = 1.0 / 64
    blk[64:, 64:] = 1.0 / 64
    c[:, 256:384] = blk
    j = np.arange(128)[:, None]
    i = np.arange(128)[None, :]
    c[:, 384:512] = (i >= j).astype(np.float32)
    c[:, 512:640] = (i <= j).astype(np.float32)
    c[:, 640:768] = 1.0 / 128
    c[:, 768:896] = 1.0 / 256
    m = np.ones((128, 512), np.float32)
    m[:, 0::128] = 0.0
    return c, m


def na_gather_index():
    idx = np.full((5, 128, 5, 128), 15 * 31, np.int64)
    tiles = [0, 1, 2, 30, 31]
    for ti, m in enumerate(tiles):
        kb = min(max(m - 2, 0), 27)
        for q in range(128):
            r = 2 * m + q // 64
            c = q % 64
            rs = min(max(r - 4, 0), 56)
            cs = min(max(c - 8, 0), 48)
            for ch in range(5):
                for kk in range(128):
                    tok = (kb + ch) * 128 + kk
                    kr, kc = tok // 64, tok % 64
                    if rs <= kr < rs + 8 and cs <= kc < cs + 16:
                        idx[ti, kk, ch, q] = (kr - r + 7) * 31 + (kc - c + 15)
    return idx


_NA_IDX = None


def host_layout(inp):
    global _NA_IDX
    f = lambda a: np.ascontiguousarray(np.asarray(a, dtype=np.float32))
    vecs = np.zeros((L, 128, NVEC), np.float32)
    for l in range(L):
        vecs[l, :, V_NORMG:V_NORMG + 8] = f(inp["norm_g"])[l].reshape(8, 128).T
        vecs[l, :, V_MEMG:V_MEMG + 8] = f(inp["mem_norm_g"])[l].reshape(8, 128).T
        vecs[l, :, V_GOUT:V_GOUT + 2] = f(inp["gla_out_g"])[l].reshape(2, 128).T
        vecs[l, :, V_NAQ] = np.tile(f(inp["na_q_g"])[l], 2)
        vecs[l, :, V_NAK] = np.tile(f(inp["na_k_g"])[l], 2)
        vecs[l, :, V_MQ] = f(inp["mem_q_g"])[l]
        vecs[l, :, V_MK] = f(inp["mem_k_g"])[l]
        vecs[l, :, V_BF:V_BF + 4] = f(inp["gla_b_f"])[l].reshape(4, 128).T
        vecs[l, :, V_BB:V_BB + 4] = f(inp["gla_b_b"])[l].reshape(4, 128).T
    if _NA_IDX is None:
        _NA_IDX = na_gather_index()
    rpb = f(inp["na_rpb"]).reshape(L, 8, 15 * 31)
    rpb_pad = np.concatenate([rpb, np.full((L, 8, 1), NEG, np.float32)], axis=2)
    g = rpb_pad[:, :, _NA_IDX]
    nab = np.ascontiguousarray(g.transpose(0, 2, 3, 1, 4, 5)).reshape(L, 5, 128, 8 * 5 * 128)
    c, m = make_consts()
    shared = dict(w_in=f(inp["w_in"]), w2f=f(inp["gla_w2_f"]), w2b=f(inp["gla_w2_b"]), p_a=f(inp["p_a"]),
                  p_b=f(inp["p_b"]), p_c=f(inp["p_c"]), w_kv=f(inp["w_mem_kv"]), w_out=f(inp["w_out"]),
                  vecs=vecs, nab=nab, cst=c, scanm=m)
    x = f(inp["x"])
    mem = f(inp["mem"])
    return [dict(shared, x=x[b], mem=mem[b]) for b in range(8)]


def kernel(**inputs):
    in_maps = host_layout(inputs)
    nc = build()
    res = run_bass_kernel_spmd(nc, in_maps, core_ids=list(range(8)))
    return np.stack([np.asarray(r["y"], dtype=np.float32) for r in res.results], axis=0)
```
